# Optimizing a Trainium2 kernel written in Bass

```python
import math
import jax, jax.numpy as jnp
from jax import lax
import numpy as np

D_MODEL = 1024
BATCH = 2
SEQ = 16384
DEPTH = 1
DEC_BATCH = 16
DEC_SEQ = 4096
PAST_LEN = 128

N_META = 16
GRID_W = 64
NA_HEADS = 8
NA_HEAD_DIM = 64
NA_WIDTH = NA_HEADS * NA_HEAD_DIM
NA_WIN_ROWS = 8
NA_WIN_COLS = 16
WA_Q_HEADS = 8
WA_KV_HEADS = 2
WA_HEAD_DIM = 64
WA_WIDTH = WA_Q_HEADS * WA_HEAD_DIM
WA_KV_WIDTH = WA_KV_HEADS * WA_HEAD_DIM
WINDOW = 128
BLOCK = 128
T5_BUCKETS = 32
T5_MAX_DIST = 128
RMS_EPS = 1e-6
NEG_INF = -1e30
IN_SIZES = (NA_WIDTH, NA_WIDTH, NA_WIDTH, NA_WIDTH,
            WA_WIDTH, WA_KV_WIDTH, WA_KV_WIDTH, WA_WIDTH,
            D_MODEL, D_MODEL)
IN_WIDTH = sum(IN_SIZES)

kernel_name = 'hybrid_na_window_gqa_encoder'


def rms_norm(x, g):
    x32 = x.astype(jnp.float32)
    y = x32 * lax.rsqrt(jnp.mean(x32 * x32, axis=-1, keepdims=True) + RMS_EPS)
    return (y * g.astype(jnp.float32)).astype(x.dtype)


def t5_bucket(rel):
    half = T5_BUCKETS // 2
    exact = half // 2
    ret = jnp.where(rel > 0, half, 0)
    n = jnp.abs(rel)
    nf = jnp.maximum(n, 1).astype(jnp.float32)
    large = exact + (jnp.log(nf / exact) / math.log(T5_MAX_DIST / exact)
                     * (half - exact)).astype(jnp.int32)
    large = jnp.minimum(large, half - 1)
    return ret + jnp.where(n < exact, n, large)


def neighbourhood_attention(q, k, v, rpb, n):
    B, _, H, hd = q.shape
    rows = n // GRID_W
    kr = min(NA_WIN_ROWS, rows)
    kc = NA_WIN_COLS
    scale = hd ** -0.5
    f32 = jnp.float32
    qm, km, vm = q[:, :N_META], k[:, :N_META], v[:, :N_META]
    qg = q[:, N_META:].reshape(B, rows, GRID_W, H, hd)
    kg = k[:, N_META:].reshape(B, rows, GRID_W, H, hd)
    vg = v[:, N_META:].reshape(B, rows, GRID_W, H, hd)
    cols = jnp.arange(GRID_W)
    col_start = jnp.clip(cols - kc // 2, 0, GRID_W - kc)
    col_idx = col_start[:, None] + jnp.arange(kc)[None, :]
    col_rel = col_idx - cols[:, None] + (NA_WIN_COLS - 1)
    rpb_cols = rpb.astype(f32)[:, :, col_rel]

    def one_row(i):
        rs = jnp.clip(i - kr // 2, 0, rows - kr)
        k_rows = lax.dynamic_slice_in_dim(kg, rs, kr, axis=1)
        v_rows = lax.dynamic_slice_in_dim(vg, rs, kr, axis=1)
        k_win = k_rows[:, :, col_idx]
        v_win = v_rows[:, :, col_idx]
        q_row = lax.dynamic_index_in_dim(qg, i, axis=1, keepdims=False)
        row_rel = rs + jnp.arange(kr) - i + (NA_WIN_ROWS - 1)
        bias = rpb_cols[:, row_rel].transpose(0, 2, 1, 3)
        s_win = jnp.einsum('bjhd,bajchd->bhjac', q_row, k_win).astype(f32) * scale + bias
        s_meta = jnp.einsum('bjhd,bmhd->bhjm', q_row, km).astype(f32) * scale
        logits = jnp.concatenate([s_win.reshape(B, H, GRID_W, kr * kc), s_meta], axis=-1)
        p = jax.nn.softmax(logits, axis=-1).astype(v.dtype)
        p_win = p[..., :kr * kc].reshape(B, H, GRID_W, kr, kc)
        p_meta = p[..., kr * kc:]
        return (jnp.einsum('bhjac,bajchd->bjhd', p_win, v_win)
                + jnp.einsum('bhjm,bmhd->bjhd', p_meta, vm))

    og = lax.map(one_row, jnp.arange(rows))
    og = jnp.moveaxis(og, 0, 1).reshape(B, n, H, hd)
    s_m = jnp.einsum('bqhd,bmhd->bhqm', qm, km).astype(f32) * scale
    p_m = jax.nn.softmax(s_m, axis=-1).astype(v.dtype)
    o_meta = jnp.einsum('bhqm,bmhd->bqhd', p_m, vm)
    return jnp.concatenate([o_meta, og], axis=1)


def window_attention(q, k, v, t5_bias, sink, n):
    B, _, HQ, hd = q.shape
    HKV = k.shape[2]
    G = HQ // HKV
    nb = n // BLOCK
    C = 3 * BLOCK
    scale = hd ** -0.5
    f32 = jnp.float32
    bias_tab = t5_bias.astype(f32)
    sink_g = sink.astype(f32).reshape(HKV, G)
    km, vm = k[:, :N_META], v[:, :N_META]

    qr = q[:, N_META:].reshape(B, nb, BLOCK, HKV, G, hd)

    def band(t):
        tp = jnp.pad(t[:, N_META:], ((0, 0), (BLOCK, BLOCK), (0, 0), (0, 0)))
        tp = tp.reshape(B, nb + 2, BLOCK, HKV, hd)
        return jnp.concatenate([tp[:, :-2], tp[:, 1:-1], tp[:, 2:]], axis=2)

    k_band, v_band = band(k), band(v)
    qq = jnp.arange(BLOCK)
    kk = jnp.arange(C)
    blk = jnp.arange(nb)
    rel = kk[None, :] - BLOCK - qq[:, None]
    key_idx = blk[:, None] * BLOCK + kk[None, :] - BLOCK
    visible = (jnp.abs(rel) <= WINDOW)[None] & ((key_idx >= 0) & (key_idx < n))[:, None, :]
    band_bias = bias_tab[t5_bucket(rel)].reshape(BLOCK, C, HKV, G).transpose(2, 3, 0, 1)
    q_pos = N_META + blk[:, None] * BLOCK + qq[None, :]
    meta_rel = jnp.arange(N_META)[None, None, :] - q_pos[:, :, None]
    meta_bias = bias_tab[t5_bucket(meta_rel)].reshape(nb, BLOCK, N_META, HKV, G).transpose(0, 3, 4, 1, 2)

    s_band = jnp.einsum('bnqkgd,bnckd->bnkgqc', qr, k_band).astype(f32) * scale + band_bias
    s_band = jnp.where(visible[:, None, None], s_band, NEG_INF)
    s_meta = jnp.einsum('bnqkgd,bmkd->bnkgqm', qr, km).astype(f32) * scale + meta_bias
    sink_col = jnp.broadcast_to(sink_g[None, None, :, :, None, None], (B, nb, HKV, G, BLOCK, 1))
    p = jax.nn.softmax(jnp.concatenate([s_band, s_meta, sink_col], axis=-1), axis=-1).astype(v.dtype)
    o_real = (jnp.einsum('bnkgqc,bnckd->bnqkgd', p[..., :C], v_band)
              + jnp.einsum('bnkgqm,bmkd->bnqkgd', p[..., C:C + N_META], vm))
    o_real = o_real.reshape(B, n, HQ, hd)

    nk = N_META + WINDOW
    qm = q[:, :N_META].reshape(B, N_META, HKV, G, hd)
    k_lead, v_lead = k[:, :nk], v[:, :nk]
    rel_m = jnp.arange(nk)[None, :] - jnp.arange(N_META)[:, None]
    bias_m = bias_tab[t5_bucket(rel_m)].reshape(N_META, nk, HKV, G).transpose(2, 3, 0, 1)
    s_m = jnp.einsum('bqkgd,bckd->bkgqc', qm, k_lead).astype(f32) * scale + bias_m
    s_m = jnp.where(jnp.abs(rel_m) <= WINDOW, s_m, NEG_INF)
    sink_m = jnp.broadcast_to(sink_g[None, :, :, None, None], (B, HKV, G, N_META, 1))
    p_m = jax.nn.softmax(jnp.concatenate([s_m, sink_m], axis=-1), axis=-1)[..., :nk].astype(v.dtype)
    o_meta = jnp.einsum('bkgqc,bckd->bqkgd', p_m, v_lead).reshape(B, N_META, HQ, hd)
    return jnp.concatenate([o_meta, o_real], axis=1)


def mixer_layer(h, norm_g, w_in, na_rpb, sink, w_proj_a, w_proj_b, w_out, t5_bias):
    B, L, _ = h.shape
    n = L - N_META
    u = rms_norm(h, norm_g)
    proj = u @ w_in
    cuts = [int(c) for c in np.cumsum(IN_SIZES)[:-1]]
    qa, ka, va, za, qb, kb, vb, zb, ga, gb = jnp.split(proj, cuts, axis=-1)
    o_a = neighbourhood_attention(qa.reshape(B, L, NA_HEADS, NA_HEAD_DIM),
                                  ka.reshape(B, L, NA_HEADS, NA_HEAD_DIM),
                                  va.reshape(B, L, NA_HEADS, NA_HEAD_DIM),
                                  na_rpb, n).reshape(B, L, NA_WIDTH)
    o_b = window_attention(qb.reshape(B, L, WA_Q_HEADS, WA_HEAD_DIM),
                           kb.reshape(B, L, WA_KV_HEADS, WA_HEAD_DIM),
                           vb.reshape(B, L, WA_KV_HEADS, WA_HEAD_DIM),
                           t5_bias, sink, n).reshape(B, L, WA_WIDTH)
    y_a = (o_a * jax.nn.silu(za)) @ w_proj_a
    y_b = (o_b * jax.nn.silu(zb)) @ w_proj_b
    merged = jax.nn.sigmoid(ga) * y_a + jax.nn.sigmoid(gb) * y_b
    return merged @ w_out


def encode(x, meta_tokens, norm_g, w_in, na_rpb, sink_logit, w_proj_a, w_proj_b, w_out, t5_bias, final_g):
    B = x.shape[0]
    meta = jnp.broadcast_to(meta_tokens.astype(x.dtype)[None], (B, N_META, x.shape[-1]))
    h = jnp.concatenate([meta, x], axis=1)
    for l in range(DEPTH):
        h = h + mixer_layer(h, norm_g[l], w_in[l], na_rpb[l], sink_logit[l],
                            w_proj_a[l], w_proj_b[l], w_out[l], t5_bias)
    return rms_norm(h[:, N_META:], final_g)


def setup_inputs(seed: int = 0) -> dict:
    key = jax.random.key(seed)
    ks = jax.random.split(key, 12)
    nrm = jax.random.normal
    return {
        'x_prompt': nrm(ks[0], (BATCH, SEQ, D_MODEL), jnp.float32),
        'x_sample': nrm(ks[1], (DEC_BATCH, DEC_SEQ, D_MODEL), jnp.float32),
        'meta_tokens': nrm(ks[2], (N_META, D_MODEL), jnp.float32),
        'norm_g': 1.0 + 0.05 * nrm(ks[3], (DEPTH, D_MODEL), jnp.float32),
        'w_in': nrm(ks[4], (DEPTH, D_MODEL, IN_WIDTH), jnp.float32) * D_MODEL ** -0.5,
        'na_rpb': 0.5 * nrm(ks[5], (DEPTH, NA_HEADS, 2 * NA_WIN_ROWS - 1, 2 * NA_WIN_COLS - 1), jnp.float32),
        'sink_logit': nrm(ks[6], (DEPTH, WA_Q_HEADS), jnp.float32),
        'w_proj_a': nrm(ks[7], (DEPTH, NA_WIDTH, D_MODEL), jnp.float32) * NA_WIDTH ** -0.5,
        'w_proj_b': nrm(ks[8], (DEPTH, WA_WIDTH, D_MODEL), jnp.float32) * WA_WIDTH ** -0.5,
        'w_out': nrm(ks[9], (DEPTH, D_MODEL, D_MODEL), jnp.float32) * D_MODEL ** -0.5,
        't5_bias': 0.5 * nrm(ks[10], (T5_BUCKETS, WA_Q_HEADS), jnp.float32),
        'final_g': 1.0 + 0.05 * nrm(ks[11], (D_MODEL,), jnp.float32),
    }


def reference(x_prompt, x_sample, meta_tokens, norm_g, w_in, na_rpb, sink_logit, w_proj_a, w_proj_b, w_out, t5_bias, final_g):
    y_prompt = encode(x_prompt, meta_tokens, norm_g, w_in, na_rpb, sink_logit,
                      w_proj_a, w_proj_b, w_out, t5_bias, final_g)
    y_sample = encode(x_sample, meta_tokens, norm_g, w_in, na_rpb, sink_logit,
                      w_proj_a, w_proj_b, w_out, t5_bias, final_g)
    return (y_prompt, y_sample)
```

```python
import math
from contextlib import ExitStack

import numpy as np
import ml_dtypes

import concourse.bass as bass
import concourse.mybir as mybir
from concourse.bass_utils import run_bass_kernel_spmd

F32 = mybir.dt.float32
BF16 = mybir.dt.bfloat16
AF = mybir.ActivationFunctionType
ALU = mybir.AluOpType

D = 1024
P = 128
N_META = 16
GRID_W = 64
NCORES = 8
EPS = 1e-6
NEG = -30000.0

QA, QB, KA, KB, VA, VB, ZA, ZB, GA, GB, WCOLS = 0, 512, 1024, 1536, 1792, 2304, 2432, 2944, 3456, 4480, 5504
WBLOCKS = [(0, 1024), (1024, 1792), (1792, 2432), (2432, 3456), (3456, 4480), (4480, 5504)]


def wblk(col):
    for i, (a, b) in enumerate(WBLOCKS):
        if a <= col < b:
            return i
    raise ValueError(col)


def t5_bucket_np(rel):
    rel = np.asarray(rel, dtype=np.int64)
    half, exact = 16, 8
    ret = np.where(rel > 0, half, 0)
    n = np.abs(rel)
    nf = np.maximum(n, 1).astype(np.float32)
    large = exact + (np.log(nf / np.float32(exact)) / np.float32(math.log(128 / exact))
                     * np.float32(half - exact)).astype(np.int32)
    for nn, b in ((8, 8), (16, 10), (32, 12), (64, 14)):
        large = np.where(n == nn, b, large)
    large = np.minimum(large, half - 1)
    return ret + np.where(n < exact, n, large)


def col_perm():
    offs = np.cumsum([0, 512, 512, 512, 512, 512, 128, 128, 512, 1024, 1024])
    qa, ka, va, za, qb, kb, vb, zb, ga, gb = [np.arange(offs[i], offs[i + 1]) for i in range(10)]
    kbd = np.concatenate([kb[0:64], kb[0:64], kb[64:128], kb[64:128]])
    perm = np.concatenate([qa, qb, ka, kbd, va, vb, za, zb, ga, gb])
    assert perm.shape[0] == WCOLS
    return perm


def unit_slots(start, nq, n):
    nt = nq + 4
    pos = start - 256 + np.arange(nt * P)
    act = np.where(pos < 0, pos + 512, np.where(pos >= n, pos - 512, pos))
    return pos, act


def na_tile(rpb_ext, pos, act, n, start, j, dt):
    R = n // GRID_W
    tq = start + j * P + np.arange(P)
    e = (j + 2 + dt) * P + np.arange(P)
    kpos, kact = pos[e], act[e]
    qr, qc = tq // GRID_W, tq % GRID_W
    kprow = np.floor_divide(kpos, GRID_W)
    kr, kc = kact // GRID_W, kact % GRID_W
    cs = np.clip(qc - 8, 0, GRID_W - 16)
    rs = np.clip(qr - 4, 0, R - 8)
    vis = ((kprow[:, None] >= qr[None, :] - 4) & (kprow[:, None] <= qr[None, :] + 3)
           & (kc[:, None] >= cs[None, :]) & (kc[:, None] < cs[None, :] + 16))
    inwin = (kr[:, None] >= rs[None, :]) & (kr[:, None] < rs[None, :] + 8)
    assert not np.any(vis & ~inwin), "halo wrap produced an out-of-window key"
    rr = np.clip(kr[:, None] - qr[None, :] + 7, 0, 14)
    cc = np.clip(kc[:, None] - qc[None, :] + 15, 0, 30)
    idx = rr * 31 + cc
    nmask = 15 * 31
    idx = np.where(vis, idx, nmask)
    return idx


def wa_tile(pos, n, start, j, b):
    tq = start + j * P + np.arange(P)
    e = (j + 2 + b) * P + np.arange(P)
    kpos = pos[e]
    rel = kpos[:, None] - tq[None, :]
    vis = (np.abs(rel) <= 128) & (kpos[:, None] >= 0) & (kpos[:, None] < n)
    bk = t5_bucket_np(np.clip(rel, -200, 200))
    return np.where(vis, bk, 32)


def meta_tile(start, j):
    tq = start + j * P + np.arange(P)
    m = np.arange(N_META)
    rel = m[:, None] - (N_META + tq[None, :])
    return t5_bucket_np(np.clip(rel, -200, 200))


def build_bias_inputs(na_rpb, t5_bias, units, nq):
    rpb_ext = np.concatenate([na_rpb.reshape(8, 15 * 31), np.full((8, 1), NEG, np.float32)], axis=1)
    t5_ext = np.concatenate([t5_bias, np.full((1, 8), NEG, np.float32)], axis=0)

    def na_vals(idx):
        return np.ascontiguousarray(rpb_ext[:, idx].transpose(1, 0, 2))

    def wa_vals(bk):
        return np.ascontiguousarray(t5_ext[bk].transpose(0, 2, 1))

    big_n = 1 << 20
    gstart = 1 << 16
    gpos, gact = unit_slots(gstart, 8, big_n)
    gna = np.stack([na_vals(na_tile(rpb_ext, gpos, gact, big_n, gstart, 4, dt)) for dt in (-2, -1, 0, 1, 2)], axis=1)
    gwa = np.stack([wa_vals(wa_tile(gpos, big_n, gstart, 4, b)) for b in (-1, 0, 1)], axis=1)
    sna, swa, smeta = [], [], []
    for (start, n) in units:
        pos, act = unit_slots(start, nq, n)
        spec = [(0, -2), (0, -1), (1, -2), (nq - 2, 2), (nq - 1, 1), (nq - 1, 2)]
        sna.append(np.stack([na_vals(na_tile(rpb_ext, pos, act, n, start, j, dt)) for (j, dt) in spec], axis=1))
        swa.append(np.stack([wa_vals(wa_tile(pos, n, start, 0, -1)), wa_vals(wa_tile(pos, n, start, nq - 1, 1))], axis=1))
        mb = meta_tile(start, 0)
        mt = t5_bias[mb]
        smeta.append(np.ascontiguousarray(mt.transpose(2, 0, 1)).reshape(P, P))
    return (gna.astype(np.float32), gwa.astype(np.float32), np.stack(sna).astype(np.float32),
            np.stack(swa).astype(np.float32), np.stack(smeta).astype(np.float32))


class Op:
    __slots__ = ("eng", "fn", "deps", "sig", "sigval", "dma_key", "dma_val")

    def __init__(self, eng, fn):
        self.eng, self.fn = eng, fn
        self.deps = ()
        self.sig = False
        self.sigval = 0
        self.dma_key = None
        self.dma_val = 0


class Prog:
    ENG = ("pe", "act", "dve", "pool", "sp")

    LIMIT = None
    MARKS = []

    def mark(self, name):
        Prog.MARKS.append((name, self.nops))

    def __init__(self):
        self.nops = 0
        self.q = {e: [] for e in self.ENG}
        self.lastw = {}
        self.readers = {}
        self.dma_cnt = {}
        self.out_dmas = []

    def op(self, eng, fn, r=(), w=(), dma=None):
        o = Op(eng, fn)
        self.nops += 1
        if Prog.LIMIT is not None and self.nops > Prog.LIMIT:
            return o
        deps = set()
        for x in r:
            lw = self.lastw.get(x)
            if lw is not None:
                deps.add(lw)
        for x in w:
            lw = self.lastw.get(x)
            if lw is not None:
                deps.add(lw)
            rd = self.readers.get(x)
            if rd:
                deps.update(rd.values())
        o.deps = tuple(deps)
        for x in r:
            rk = eng if dma is None else ("dma", self.nops)
            self.readers.setdefault(x, {})[rk] = o
        for x in w:
            self.lastw[x] = o
            self.readers[x] = {}
        if dma is not None:
            c = self.dma_cnt.get(dma, 0) + 1
            self.dma_cnt[dma] = c
            o.dma_key, o.dma_val = dma, 16 * c
        self.q[eng].append(o)
        return o

    def emit(self, nc, es):
        for e in self.ENG:
            for o in self.q[e]:
                for d in o.deps:
                    if d.dma_key is None and not (d.eng == "pe" and o.eng == "pe" and o.dma_key is None):
                        d.sig = True
        for e in self.ENG:
            c = 0
            for o in self.q[e]:
                if o.sig:
                    c += 1
                    o.sigval = c
        esem = {e: es.enter_context(nc.semaphore("s_" + e)) for e in ("pe", "act", "dve", "pool")}
        dsem = {k: es.enter_context(nc.semaphore("d_" + k)) for k in self.dma_cnt}
        blk = es.enter_context(nc.Block())
        final = [(o.dma_key, o.dma_val) for o in self.out_dmas if o.dma_key is not None]

        def run(e, engobj):
            waited = {}
            for o in self.q[e]:
                for d in o.deps:
                    if d.dma_key is not None:
                        key, val, sem = ("d", d.dma_key), d.dma_val, dsem[d.dma_key]
                    else:
                        if d.eng == "pe" and e == "pe" and o.dma_key is None:
                            continue
                        key, val, sem = ("e", d.eng), d.sigval, esem[d.eng]
                    if waited.get(key, 0) >= val:
                        continue
                    waited[key] = val
                    engobj.wait_ge(sem, val)
                ins = o.fn(engobj)
                if o.dma_key is not None:
                    ins.then_inc(dsem[o.dma_key], 16)
                elif o.sig:
                    ins.then_inc(esem[e], 1)
            if e == "pool":
                fin = {}
                for k, v in final:
                    fin[k] = max(fin.get(k, 0), v)
                for k, v in fin.items():
                    engobj.wait_ge(dsem[k], v)

        blk.tensor(lambda t: run("pe", t))
        blk.scalar(lambda t: run("act", t))
        blk.vector(lambda t: run("dve", t))
        blk.gpsimd(lambda t: run("pool", t))
        blk.sync(lambda t: run("sp", t))


def build_program(units, nq):
    U = units
    NT = nq + 4
    NR = 5
    nc = bass.Bass("TRN2", target_bir_lowering=False)
    dram = lambda n, s, dt=F32, kind="ExternalInput": nc.dram_tensor(n, s, dt, kind=kind)
    xe = dram("xe", [U, NT * P, D])
    wp = dram("wp", [P, 8, WCOLS])
    wa_d = dram("wa", [P, 4, D])
    wb_d = dram("wb", [P, 4, D])
    wo_d = dram("wo", [P, 8, D])
    gcol_d = dram("gcol", [P, 8])
    fgb_d = dram("fgb", [P, D])
    sink_d = dram("sinkb", [P, 8])
    b15_d = dram("b15", [P, 1])
    ident_d = dram("ident", [P, P], BF16)
    xmeta_d = dram("xmeta", [P, D])
    maskk_d = dram("maskk", [P, 4, P])
    maskv_d = dram("maskv", [P, 8])
    gna_d = dram("gna", [P, 5, 8, P])
    gwa_d = dram("gwa", [P, 3, 8, P])
    sna_d = dram("sna", [U, P, 6, 8, P])
    swa_d = dram("swa", [U, P, 2, 8, P])
    smeta_d = dram("smeta", [U, P, P])
    y_d = dram("y", [U, nq * P, D], F32, kind="ExternalOutput")
    scr = nc.dram_tensor("espscr", [U, 2, P, 33 * P], BF16)

    pg = Prog()
    es = ExitStack()
    sb = lambda n, s, dt: es.enter_context(nc.sbuf_tensor(n, s, dt))
    W = sb("W", [P, 8, WCOLS], BF16)
    Wa = sb("Wa", [P, 4, D], BF16)
    Wb = sb("Wb", [P, 4, D], BF16)
    Wo = sb("Wo", [P, 8, D], BF16)
    EBNA = sb("EBNA", [P, 5, 8 * P], BF16)
    EBWA = sb("EBWA", [P, 3, 8 * P], BF16)
    ESP = sb("ESP", [P, 33 * P], BF16)
    BDK = sb("BDK", [P, 8, P], BF16)
    VMA = sb("VMA", [P, 2, 260], BF16)
    VMB = sb("VMB", [P, 2, 260], BF16)
    FGB = sb("FGB", [P, D], F32)
    IDT = sb("IDT", [P, P], BF16)
    CST = sb("CST", [P, 32], F32)
    ST = sb("ST", [P, 32], F32)
    XA = [sb(f"XA{i}", [P, D], F32) for i in range(2)]
    XB = sb("XB", [P, D], F32)
    XN = sb("XN", [P, D], BF16)
    UT = [sb(f"UT{i}", [P, 8, P], BF16) for i in range(3)]
    KAr = [sb(f"KAr{i}", [P, 4, P], BF16) for i in range(NR)]
    KBr = [sb(f"KBr{i}", [P, 2, P], BF16) for i in range(NR)]
    VAr = [sb(f"VAr{i}", [P, 8, 72], BF16) for i in range(NR)]
    VBr = [sb(f"VBr{i}", [P, 2, 72], BF16) for i in range(NR)]
    QEO = sb("QEO", [P, 8, 2, P], BF16)
    SZ = sb("SZ", [P, D], BF16)
    TG = sb("TG", [P, 2 * D], BF16)
    PT = [sb(f"PT{i}", [P, 4 * P], BF16) for i in range(3)]
    PM = sb("PM", [P, 2 * P], BF16)
    G = sb("G", [P, D], BF16)
    GT = sb("GT", [P, 8, P], BF16)
    M1 = sb("M1", [P, 512], F32)
    MSK = M1[:, :].rearrange("p (c k) -> p c k", k=P)
    M2 = sb("M2", [P, 512], F32)
    banks = [es.enter_context(nc.psum_tensor(f"bank{i}", [P, 512], F32)) for i in range(8)]
    bview = [b.bitcast(BF16) for b in banks]

    gctr = [0]
    sctr = [0]

    def nextG():
        i = gctr[0] % 3
        gctr[0] += 1
        return i

    def nextS():
        i = 3 + sctr[0] % 3
        sctr[0] += 1
        return i

    BO = [6, 7]
    flat = lambda ap3: ap3.rearrange("p a b -> p (a b)")

    def dma(eng, out, in_, key, r=(), w=()):
        return pg.op(eng, lambda e: e.dma_start(out=out, in_=in_), r=r, w=w, dma=key)

    def act(out, in_, func, r, w, **kw):
        return pg.op("act", lambda e: e.activation(out=out, in_=in_, func=func, **kw), r=r, w=w)

    dma("sp", IDT[:, :], ident_d[:, :], "IDT", w=["IDT"])
    dma("sp", CST[:, 0:8], gcol_d[:, :], "C0", w=["GCOL"])
    dma("sp", CST[:, 8:16], sink_d[:, :], "C1", w=["SINK"])
    dma("sp", CST[:, 16:17], b15_d[:, :], "C2", w=["B15"])
    dma("sp", CST[:, 17:25], maskv_d[:, :], "C3", w=["MASKV"])
    dma("sp", FGB[:, :], fgb_d[:, :], "FGB", w=["FGB"])
    dma("sp", MSK, maskk_d[:, :, :], "MSK", w=["M1"])
    act(CST[:, 8:16], CST[:, 8:16], AF.Exp, r=["SINK"], w=["SINK"])
    pg.op("dve", lambda e: e.memset(ST[:, 24:25], 1.0), w=["ONE"])

    pg.mark("consts")
    TGf = TG.bitcast(F32)
    SZf = SZ.bitcast(F32)
    f32flat = lambda t: t.bitcast(F32)[:, :, :].rearrange("p a b -> p (a b)")
    big = [dict(ap=XA[0][:, :], res=["XA0"], key="XA0"), dict(ap=XA[1][:, :], res=["XA1"], key="XA1"),
           dict(ap=XB[:, :], res=["XB"], key="XB"),
           dict(ap=TGf[:, :], res=["TG0", "TG1", "TG2", "TG3"], key="STG_TG")]
    small = big + [dict(ap=SZf[:, :], res=["SZ0", "SZ1"], key="STG_SZ"),
                   dict(ap=f32flat(GT), res=["GT"], key="STG_GT"),
                   dict(ap=f32flat(UT[1]), res=["UT1"], key="STG_UT1"),
                   dict(ap=f32flat(UT[2]), res=["UT2"], key="STG_UT2"),
                   dict(ap=M2[:, :], res=["M2"], key="STG_M2")]
    bigc = [0]
    smallc = [0]

    def next_big():
        i = bigc[0] % len(big)
        bigc[0] += 1
        return big[i]

    def next_small():
        i = smallc[0] % len(small)
        smallc[0] += 1
        return small[i]

    castctr = [0]
    dq = [0]

    def cast_block(dst, src_d, res, scale_ap=None, scale_res=None):
        n = dst.shape[-1]
        eng = "act" if castctr[0] % 2 == 0 else "dve"
        castctr[0] += 1
        for off in range(0, n, 512):
            m = min(512, n - off)
            u_ = next_small()
            st = u_["ap"][:, 0:m]
            dpiece, spiece = dst[:, off:off + m], src_d[:, off:off + m]
            dq[0] += 1
            if dq[0] % 2:
                dma("sp", st, spiece, u_["key"], w=u_["res"])
            else:
                dma("pool", st, spiece, u_["key"] + "_P", w=u_["res"])
            rr = list(u_["res"]) + ([scale_res] if scale_res else [])
            if eng == "act":
                if scale_ap is None:
                    act(dpiece, st, AF.Copy, r=rr, w=[res])
                else:
                    act(dpiece, st, AF.Copy, r=rr, w=[res], scale=scale_ap)
            else:
                if scale_ap is None:
                    pg.op("dve", lambda e, dpiece=dpiece, st=st: e.tensor_copy(out=dpiece, in_=st), r=rr, w=[res])
                else:
                    pg.op("dve", lambda e, dpiece=dpiece, st=st: e.tensor_scalar(
                        out=dpiece, in0=st, scalar1=scale_ap, scalar2=None, op0=ALU.mult), r=rr, w=[res])

    for c in range(8):
        for bi, (a, b) in enumerate(WBLOCKS):
            cast_block(W[:, c, a:b], wp[:, c, a:b], f"W{c}_{bi}", scale_ap=CST[:, c:c + 1], scale_res="GCOL")
    for c in range(4):
        cast_block(Wa[:, c, :], wa_d[:, c, :], f"Wa{c}")
        cast_block(Wb[:, c, :], wb_d[:, c, :], f"Wb{c}")
    for c in range(8):
        cast_block(Wo[:, c, :], wo_d[:, c, :], f"Wo{c}")

    pg.mark("wcast")

    def exp_block(dst, src_d, res):
        n = dst.shape[-1]
        u_ = next_big()
        dma("sp", u_["ap"][:, 0:n], src_d, u_["key"], w=u_["res"])
        act(dst, u_["ap"][:, 0:n], AF.Exp, r=u_["res"], w=[res])

    for i in range(5):
        exp_block(EBNA[:, i, :], gna_d[:, i, :, :].rearrange("p h q -> p (h q)"), f"EBNA{i}")
    for i in range(3):
        exp_block(EBWA[:, i, :], gwa_d[:, i, :, :].rearrange("p h q -> p (h q)"), f"EBWA{i}")

    pg.mark("ebgen")
    tb = [(XN, "XN"), (G, "G")]
    tbc = [0]
    for u in range(U):
        pieces = [(0, 0, sna_d[u, :, 0, :, :]), (0, 1, sna_d[u, :, 1, :, :]), (0, 2, sna_d[u, :, 2, :, :]),
                  (0, 3, swa_d[u, :, 0, :, :]),
                  (1, 0, sna_d[u, :, 3, :, :]), (1, 1, sna_d[u, :, 4, :, :]), (1, 2, sna_d[u, :, 5, :, :]),
                  (1, 3, swa_d[u, :, 1, :, :])]
        for (side, pi, src) in pieces:
            tbuf, tres = tb[tbc[0] % 2]
            tbc[0] += 1
            exp_block(tbuf[:, :], src.rearrange("p h q -> p (h q)"), tres)
            dma("sp", scr[u, side, :, pi * 1024:(pi + 1) * 1024], tbuf[:, :], "SCRW" + tres, r=[tres],
                w=[f"SCR{u}_{side}_{pi}"])
        tbuf, tres = tb[tbc[0] % 2]
        tbc[0] += 1
        exp_block(tbuf[:, 0:P], smeta_d[u, :, :], tres)
        dma("sp", scr[u, 0, :, 4096:4096 + P], tbuf[:, 0:P], "SCRW" + tres, r=[tres], w=[f"SCR{u}_0_4"])

    pg.mark("scratch")

    def a_pre(xbuf, xres, ss_col):
        ss, rms, rstd = ST[:, ss_col:ss_col + 1], ST[:, ss_col + 1:ss_col + 2], ST[:, ss_col + 2:ss_col + 3]
        nm = f"ST{ss_col}"
        act(XN[:, :], xbuf[:, :], AF.Square, r=[xres], w=["XN", nm + "a"], accum_out=ss)
        act(rms, ss, AF.Sqrt, r=[nm + "a", "EPS"], w=[nm + "b"], scale=1.0 / D, bias=EPS_AP[0])
        pg.op("dve", lambda e: e.reciprocal(out=rstd, in_=rms), r=[nm + "b"], w=[nm + "c"])
        act(XN[:, :], xbuf[:, :], AF.Copy, r=[xres, nm + "c"], w=["XN"], scale=rstd)

    EPS_AP = [None]

    def a_tr(uslot):
        g = nextG()
        for c in range(8):
            pg.op("pe", lambda e, c=c: e.transpose(out=bview[g][:, c * P:(c + 1) * P], in_=XN[:, c * P:(c + 1) * P],
                                                    identity=IDT[:, :]), r=["XN", "IDT"], w=[f"B{g}"])
        pg.op("dve", lambda e: e.tensor_copy(out=flat(UT[uslot][:, :, :]), in_=bview[g][:, 0:D]),
              r=[f"B{g}"], w=[f"UT{uslot}"])

    def fm_proj(g, off, uslot, col):
        for c in range(8):
            pg.op("pe", lambda e, c=c: e.matmul(banks[g][:, off * P:(off + 1) * P], lhsT=W[:, c, col:col + P],
                                                 rhs=UT[uslot][:, c, :], start=(c == 0), stop=(c == 7)),
                  r=[f"W{c}_{wblk(col)}", f"UT{uslot}"], w=[f"B{g}"])

    def tm_proj(g, boff, n, uslot, col):
        for c in range(8):
            pg.op("pe", lambda e, c=c: e.matmul(banks[g][:, boff:boff + n], lhsT=UT[uslot][:, c, :],
                                                 rhs=W[:, c, col:col + n], start=(c == 0), stop=(c == 7)),
                  r=[f"W{c}_{wblk(col)}", f"UT{uslot}"], w=[f"B{g}"])

    def a_kv(uslot, rs):
        g1 = nextG()
        for f in range(4):
            fm_proj(g1, f, uslot, KA + f * P)
        act(flat(KAr[rs][:, :, :]), banks[g1][:, :], AF.Copy, r=[f"B{g1}"], w=[f"KA{rs}"])
        g2 = nextG()
        for f in range(2):
            fm_proj(g2, f, uslot, KB + f * P)
        tm_proj(g2, 256, 128, uslot, VB)
        act(flat(KBr[rs][:, :, :]), banks[g2][:, 0:256], AF.Copy, r=[f"B{g2}"], w=[f"KB{rs}"])
        act(VBr[rs][:, :, 0:64], banks[g2][:, 256:384].rearrange("p (h e) -> p h e", e=64), AF.Copy,
            r=[f"B{g2}"], w=[f"VB{rs}"])
        g3 = nextG()
        tm_proj(g3, 0, 512, uslot, VA)
        pg.op("dve", lambda e: e.tensor_copy(out=VAr[rs][:, :, 0:64],
                                             in_=banks[g3][:, :].rearrange("p (h e) -> p h e", e=64)),
              r=[f"B{g3}"], w=[f"VA{rs}"])

    def qzg_phases(uslot):
        ph = []

        def q1():
            g = nextG()
            for f in range(4):
                fm_proj(g, f, uslot, QA + f * P)
            act(QEO[0:64, 0:4, 0, :], banks[g][0:64, :].rearrange("p (c q) -> p c q", q=P), AF.Copy,
                r=[f"B{g}"], w=["QA"])
            act(QEO[64:128, 0:4, 1, :], banks[g][64:128, :].rearrange("p (c q) -> p c q", q=P), AF.Copy,
                r=[f"B{g}"], w=["QA"])

        def q2():
            g = nextG()
            for f in range(4):
                fm_proj(g, f, uslot, QB + f * P)
            pg.op("dve", lambda e, g=g: e.tensor_copy(
                out=QEO[0:64, 4:8, 0, :], in_=banks[g][0:64, :].rearrange("p (c q) -> p c q", q=P)),
                r=[f"B{g}"], w=["QB"])
            pg.op("dve", lambda e, g=g: e.tensor_copy(
                out=QEO[64:128, 4:8, 1, :], in_=banks[g][64:128, :].rearrange("p (c q) -> p c q", q=P)),
                r=[f"B{g}"], w=["QB"])

        def zk(k, col):
            g = nextG()
            tm_proj(g, 0, 512, uslot, col)
            act(SZ[:, k * 512:(k + 1) * 512], banks[g][:, :], AF.Tanh, r=[f"B{g}"], w=[f"SZ{k}"], scale=0.5)
            pg.op("dve", lambda e, g=g, k=k: e.scalar_tensor_tensor(
                out=SZ[:, k * 512:(k + 1) * 512], in0=SZ[:, k * 512:(k + 1) * 512], scalar=1.0, in1=banks[g][:, :],
                op0=ALU.add, op1=ALU.mult), r=[f"B{g}", f"SZ{k}"], w=[f"SZ{k}"])

        def gk(k):
            g = nextG()
            tm_proj(g, 0, 512, uslot, GA + k * 512)
            act(TG[:, k * 512:(k + 1) * 512], banks[g][:, :], AF.Tanh, r=[f"B{g}"], w=[f"TG{k}"], scale=0.5)

        ph.append(q1)
        ph.append(q2)
        ph.append(lambda: zk(0, ZA))
        ph.append(lambda: zk(1, ZB))
        for k in range(4):
            ph.append(lambda k=k: gk(k))
        return ph

    ptc = [0]
    LOOK = 2

    def attention(rs_of, j, nqv):
        def meta():
            s0 = nextS()
            for br in range(2):
                for c in range(4):
                    for qi in range(2):
                        pg.op("pe", lambda e, br=br, c=c, qi=qi: e.matmul(
                            banks[s0][:, br * P:(br + 1) * P], lhsT=BDK[:, br * 4 + c, :],
                            rhs=QEO[:, br * 4 + c, qi, :],
                            start=(c == 0 and qi == 0), stop=(c == 3 and qi == 1)),
                            r=["BDK", "QA" if br == 0 else "QB"], w=[f"B{s0}"])
            act(PM[:, 0:P], banks[s0][:, 0:P], AF.Exp, r=[f"B{s0}"], w=["PMA"], scale=0.125)
            if j == 0:
                act(PM[:, P:2 * P], banks[s0][:, P:2 * P], AF.Exp, r=[f"B{s0}"], w=["PMB"], scale=0.125)
                pg.op("dve", lambda e: e.tensor_tensor(out=PM[:, P:2 * P], in0=PM[:, P:2 * P],
                                                       in1=ESP[:, 32 * P:33 * P], op=ALU.mult),
                      r=["PMB", "ESP"], w=["PMB"])
            else:
                act(PM[:, P:2 * P], banks[s0][:, P:2 * P], AF.Exp, r=[f"B{s0}", "B15"], w=["PMB"], scale=0.125,
                    bias=CST[:, 16:17])

        jobs = []
        for br in range(2):
            dts = (-2, -1, 0, 1, 2) if br == 0 else (-1, 0, 1)
            for hg in range(2):
                for di, dt in enumerate(dts):
                    jobs.append(dict(br=br, hg=hg, dt=dt, first=(di == 0), last=(di == len(dts) - 1)))

        def emit_qk(jb):
            br, hg, dt = jb["br"], jb["hg"], jb["dt"]
            rs = rs_of(dt)
            sbk = nextS()
            jb["rs"], jb["sbk"] = rs, sbk
            if br == 0:
                for pp in range(2):
                    c = 2 * hg + pp
                    pg.op("pe", lambda e, sbk=sbk, pp=pp, c=c, rs=rs: e.matmul(
                        banks[sbk][:, pp * 256:(pp + 1) * 256], lhsT=KAr[rs][:, c, :],
                        rhs=QEO[:, c, :, :].rearrange("p a q -> p (a q)"), start=True, stop=True),
                        r=[f"KA{rs}", "QA"], w=[f"B{sbk}"])
            else:
                pg.op("pe", lambda e, sbk=sbk, hg=hg, rs=rs: e.matmul(
                    banks[sbk][:, :], lhsT=KBr[rs][:, hg, :],
                    rhs=QEO[:, 4 + 2 * hg:6 + 2 * hg, :, :].rearrange("p c a q -> p (c a q)"), start=True, stop=True),
                    r=[f"KB{rs}", "QB"], w=[f"B{sbk}"])

        def emit_sm(jb):
            br, hg, dt, sbk = jb["br"], jb["hg"], jb["dt"], jb["sbk"]
            pb = ptc[0] % 3
            ptc[0] += 1
            jb["pb"] = pb
            act(PT[pb][:, :], banks[sbk][:, :], AF.Exp, r=[f"B{sbk}"], w=[f"PT{pb}"], scale=0.125)
            if br == 0:
                base = None
                if j == 0 and dt == -2:
                    base = 0
                elif j == 0 and dt == -1:
                    base = 8
                elif j == 1 and dt == -2:
                    base = 16
                elif j == nqv - 2 and dt == 2:
                    base = 0
                elif j == nqv - 1 and dt == 1:
                    base = 8
                elif j == nqv - 1 and dt == 2:
                    base = 16
                if base is None:
                    eb, ebres = EBNA[:, dt + 2, hg * 512:(hg + 1) * 512], f"EBNA{dt + 2}"
                else:
                    eb, ebres = ESP[:, (base + 4 * hg) * P:(base + 4 * hg + 4) * P], "ESP"
            else:
                if (j == 0 and dt == -1) or (j == nqv - 1 and dt == 1):
                    eb, ebres = ESP[:, (24 + 4 * hg) * P:(24 + 4 * hg + 4) * P], "ESP"
                else:
                    eb, ebres = EBWA[:, dt + 1, hg * 512:(hg + 1) * 512], f"EBWA{dt + 1}"
            pg.op("dve", lambda e, pb=pb, eb=eb: e.tensor_tensor(out=PT[pb][:, :], in0=PT[pb][:, :], in1=eb,
                                                                  op=ALU.mult),
                  r=[f"PT{pb}", ebres], w=[f"PT{pb}"])

        def emit_pv(jb):
            br, hg, rs, pb = jb["br"], jb["hg"], jb["rs"], jb["pb"]
            ob = BO[hg]
            if jb["first"]:
                VM = VMA if br == 0 else VMB
                pg.op("pe", lambda e, ob=ob, VM=VM, hg=hg, br=br: e.matmul(
                    banks[ob][:, 0:260], lhsT=PM[:, br * P:(br + 1) * P], rhs=VM[:, hg, :], start=True, stop=False,
                    skip_group_check=True),
                    r=["PMA" if br == 0 else "PMB", "VM"], w=[f"B{ob}"])
            last = jb["last"]
            for hl in range(4):
                h = 4 * hg + hl
                rhs = VAr[rs][:, h, 0:65] if br == 0 else VBr[rs][:, h // 4, 0:65]
                pg.op("pe", lambda e, ob=ob, hl=hl, pb=pb, rhs=rhs, last=last: e.matmul(
                    banks[ob][:, hl * 65:(hl + 1) * 65], lhsT=PT[pb][:, hl * P:(hl + 1) * P], rhs=rhs,
                    start=False, stop=last, skip_group_check=True),
                    r=[f"PT{pb}", (f"VA{rs}" if br == 0 else f"VB{rs}")], w=[f"B{ob}"])
            if not last:
                return
            ov = banks[ob][:, 0:260].rearrange("p (h e) -> p h e", e=65)
            dcol = 8 + br * 8 + hg * 4
            den = ST[:, dcol:dcol + 4]
            dres = f"DEN{br}{hg}"
            if br == 0:
                pg.op("dve", lambda e, ov=ov, den=den: e.reciprocal(out=den, in_=ov[:, :, 64]),
                      r=[f"B{ob}"], w=[dres])
            else:
                pg.op("dve", lambda e, ov=ov, den=den, hg=hg: e.tensor_tensor(
                    out=den, in0=ov[:, :, 64], in1=CST[:, 8 + 4 * hg:12 + 4 * hg], op=ALU.add),
                    r=[f"B{ob}", "SINK"], w=[dres])
                pg.op("dve", lambda e, den=den: e.reciprocal(out=den, in_=den), r=[dres], w=[dres])
            o0 = br * 512 + hg * 256
            szv = SZ[:, o0:o0 + 256].rearrange("p (h e) -> p h e", e=64)
            pg.op("pool", lambda e, szv=szv, den=den: e.tensor_tensor(
                out=szv, in0=szv, in1=den.unsqueeze(2).broadcast_to([P, 4, 64]), op=ALU.mult),
                r=[dres, f"SZ{br}"], w=[f"SZ{br}"])
            gv = G[:, o0:o0 + 256].rearrange("p (h e) -> p h e", e=64)
            pg.op("dve", lambda e, gv=gv, ov=ov, szv=szv: e.tensor_tensor(out=gv, in0=ov[:, :, 0:64], in1=szv,
                                                                          op=ALU.mult),
                  r=[f"B{ob}", f"SZ{br}"], w=["G"])

        n = len(jobs)

        def pre():
            meta()
            for i in range(LOOK):
                emit_qk(jobs[i])
                emit_sm(jobs[i])

        def main():
            for i in range(n - LOOK):
                emit_qk(jobs[i + LOOK])
                emit_sm(jobs[i + LOOK])
                emit_pv(jobs[i])

        def drain():
            for i in range(n - LOOK, n):
                emit_pv(jobs[i])

        return pre, main, drain

    def tail_phases(u, j):
        ph = []

        def p1():
            g = nextG()
            for c in range(8):
                pg.op("pe", lambda e, c=c, g=g: e.transpose(out=bview[g][:, c * P:(c + 1) * P],
                                                             in_=G[:, c * P:(c + 1) * P], identity=IDT[:, :]),
                      r=["G", "IDT"], w=[f"B{g}"])
            act(flat(GT[:, :, :]), bview[g][:, 0:D], AF.Copy, r=[f"B{g}"], w=["GT"])

        def ph_y(half):
            ga = nextG()
            for c in range(4):
                pg.op("pe", lambda e, c=c, ga=ga, half=half: e.matmul(
                    banks[ga][:, :], lhsT=GT[:, c, :], rhs=Wa[:, c, half * 512:(half + 1) * 512],
                    start=(c == 0), stop=(c == 3)), r=["GT", f"Wa{c}"], w=[f"B{ga}"])
            pg.op("dve", lambda e, ga=ga, half=half: e.scalar_tensor_tensor(
                out=M1[:, :], in0=TG[:, half * 512:(half + 1) * 512], scalar=1.0, in1=banks[ga][:, :],
                op0=ALU.add, op1=ALU.mult), r=[f"B{ga}", f"TG{half}"], w=["M1"])
            gb = nextG()
            for c in range(4):
                pg.op("pe", lambda e, c=c, gb=gb, half=half: e.matmul(
                    banks[gb][:, :], lhsT=GT[:, 4 + c, :], rhs=Wb[:, c, half * 512:(half + 1) * 512],
                    start=(c == 0), stop=(c == 3)), r=["GT", f"Wb{c}"], w=[f"B{gb}"])
            pg.op("dve", lambda e, gb=gb, half=half: e.scalar_tensor_tensor(
                out=M2[:, :], in0=TG[:, D + half * 512:D + (half + 1) * 512], scalar=1.0, in1=banks[gb][:, :],
                op0=ALU.add, op1=ALU.mult), r=[f"B{gb}", f"TG{2 + half}"], w=["M2"])
            pg.op("dve", lambda e, half=half: e.tensor_tensor(out=G[:, half * 512:(half + 1) * 512], in0=M1[:, :],
                                                              in1=M2[:, :], op=ALU.add),
                  r=["M1", "M2"], w=["G"])

        def p4():
            g = nextG()
            for c in range(8):
                pg.op("pe", lambda e, c=c, g=g: e.transpose(out=bview[g][:, c * P:(c + 1) * P],
                                                             in_=G[:, c * P:(c + 1) * P], identity=IDT[:, :]),
                      r=["G", "IDT"], w=[f"B{g}"])
            act(flat(GT[:, :, :]), bview[g][:, 0:D], AF.Copy, r=[f"B{g}"], w=["GT"])

        def ph_o(half):
            go = nextG()
            for c in range(8):
                pg.op("pe", lambda e, c=c, go=go, half=half: e.matmul(
                    banks[go][:, :], lhsT=GT[:, c, :], rhs=Wo[:, c, half * 512:(half + 1) * 512],
                    start=(c == 0), stop=(c == 7)), r=["GT", f"Wo{c}"], w=[f"B{go}"])
            pg.op("dve", lambda e, go=go, half=half: e.scalar_tensor_tensor(
                out=XB[:, half * 512:(half + 1) * 512], in0=banks[go][:, :], scalar=0.25,
                in1=XB[:, half * 512:(half + 1) * 512], op0=ALU.mult, op1=ALU.add), r=[f"B{go}", "XB"], w=["XB"])

        ph.append(p1)
        ph.append(lambda: ph_y(0))
        ph.append(lambda: ph_y(1))
        ph.append(p4)
        ph.append(lambda: ph_o(0))
        ph.append(lambda: ph_o(1))
        return ph

    def tail_finish(u, j):
        ss, rms, rstd = ST[:, 4:5], ST[:, 5:6], ST[:, 6:7]
        act(flat(GT[:, :, :]), XB[:, :], AF.Square, r=["XB"], w=["GT", "ST4a"], accum_out=ss)
        act(rms, ss, AF.Sqrt, r=["ST4a", "EPS"], w=["ST4b"], scale=1.0 / D, bias=EPS_AP[0])
        pg.op("dve", lambda e: e.reciprocal(out=rstd, in_=rms), r=["ST4b"], w=["ST4c"])
        pg.op("dve", lambda e: e.scalar_tensor_tensor(out=XB[:, :], in0=XB[:, :], scalar=rstd, in1=FGB[:, :],
                                                      op0=ALU.mult, op1=ALU.mult), r=["XB", "ST4c", "FGB"], w=["XB"])
        o = dma("pool", y_d[u, j * P:(j + 1) * P, :], XB[:, :], "OUT", r=["XB"], w=["YOUT"])
        pg.out_dmas.append(o)

    pg.op("dve", lambda e: e.memset(ST[:, 25:26], EPS), w=["EPS"])
    EPS_AP[0] = ST[:, 25:26]
    _orig_act = act

    pg.op("pool", lambda e: e.memset(QEO[:, :, :, :].rearrange("p a b c -> p (a b c)"), 0.0), w=["QA", "QB"])
    for i in range(NR):
        pg.op("pool", lambda e, i=i: e.memset(VAr[i][:, :, 64:65], 1.0), w=[f"VA{i}"])
        pg.op("pool", lambda e, i=i: e.memset(VBr[i][:, :, 64:65], 1.0), w=[f"VB{i}"])

    pg.mark("onescols")
    dma("sp", XA[0][:, :], xmeta_d[:, :], "XA0", w=["XA0"])
    a_pre(XA[0], "XA0", 0)
    pg.mark("meta_pre")
    a_tr(0)
    pg.mark("meta_tr")
    gm = nextG()
    for f in range(4):
        fm_proj(gm, f, 0, KA + f * P)
    for c in range(4):
        pg.op("dve", lambda e, c=c: e.tensor_tensor(out=BDK[:, c, :], in0=banks[gm][:, c * P:(c + 1) * P],
                                                    in1=MSK[:, c, :], op=ALU.mult), r=[f"B{gm}", "M1"], w=["BDK"])
    gm2 = nextG()
    for f in range(2):
        fm_proj(gm2, f, 0, KB + f * P)
    tm_proj(gm2, 256, 128, 0, VB)
    for c in range(4):
        pg.op("dve", lambda e, c=c: e.tensor_tensor(out=BDK[:, 4 + c, :], in0=banks[gm2][:, (c // 2) * P:(c // 2 + 1) * P],
                                                    in1=MSK[:, c, :], op=ALU.mult), r=[f"B{gm2}", "M1"], w=["BDK"])
    gm3 = nextG()
    tm_proj(gm3, 0, 512, 0, VA)
    for h in range(8):
        hg, hl = h // 4, h % 4
        pg.op("dve", lambda e, h=h, hg=hg, hl=hl: e.tensor_scalar(
            out=VMA[:, hg, hl * 65:hl * 65 + 64], in0=banks[gm3][:, h * 64:(h + 1) * 64], scalar1=CST[:, 17 + h:18 + h],
            scalar2=None, op0=ALU.mult), r=[f"B{gm3}", "MASKV"], w=["VM"])
        pg.op("dve", lambda e, h=h, hg=hg, hl=hl: e.tensor_scalar(
            out=VMB[:, hg, hl * 65:hl * 65 + 64], in0=banks[gm2][:, 256 + hg * 64:256 + (hg + 1) * 64],
            scalar1=CST[:, 17 + h:18 + h], scalar2=None, op0=ALU.mult), r=[f"B{gm2}", "MASKV"], w=["VM"])
        pg.op("dve", lambda e, h=h, hg=hg, hl=hl: e.tensor_copy(out=VMA[:, hg, hl * 65 + 64:hl * 65 + 65],
                                                                in_=CST[:, 17 + h:18 + h]), r=["MASKV"], w=["VM"])
        pg.op("dve", lambda e, h=h, hg=hg, hl=hl: e.tensor_copy(out=VMB[:, hg, hl * 65 + 64:hl * 65 + 65],
                                                                in_=CST[:, 17 + h:18 + h]), r=["MASKV"], w=["VM"])

    pg.mark("meta_done")
    gt = [0]
    for u in range(U):
        dma("sp", ESP[:, :], scr[u, 0, :, :], "ESP", r=[f"SCR{u}_0_{i}" for i in range(5)], w=["ESP"])
        base = gt[0]
        xa_of = lambda s: (base + s) % 2
        ring_of = lambda t: (base + t) % NR
        ut_of = lambda t: (base + t) % 3
        dma("sp", XA[xa_of(0)][:, :], xe[u, 0:P, :], f"XA{xa_of(0)}", w=[f"XA{xa_of(0)}"])
        pending_finish = None
        pending_drain = None
        for s in range(NT + 3):
            tq = s - 2
            tt = s - 3
            doB = 2 <= tq <= NT - 3
            doT = 2 <= tt <= NT - 3
            if pending_finish is not None:
                tail_finish(u, pending_finish)
                pending_finish = None
            if s + 1 < NT:
                k = xa_of(s + 1)
                dma("sp", XA[k][:, :], xe[u, (s + 1) * P:(s + 2) * P, :], f"XA{k}", w=[f"XA{k}"])
            if doT:
                dma("sp", XB[:, :], xe[u, tt * P:(tt + 1) * P, :], "XB", w=["XB"])
            if s < NT:
                a_pre(XA[xa_of(s)], f"XA{xa_of(s)}", 0)
            tp = tail_phases(u, tt - 2) if doT else []
            qp = qzg_phases(ut_of(tq)) if doB else []
            def run(lst, i):
                if i < len(lst):
                    lst[i]()
            run(qp, 0)
            run(qp, 1)
            if pending_drain is not None:
                pending_drain()
                pending_drain = None
            run(qp, 2)
            run(qp, 3)
            run(tp, 0)
            if s < NT:
                a_tr(ut_of(s))
            run(tp, 1)
            run(qp, 4)
            run(tp, 2)
            run(qp, 5)
            run(tp, 3)
            run(qp, 6)
            run(tp, 4)
            run(qp, 7)
            run(tp, 5)
            if doB:
                j = tq - 2
                pre, main, drain = attention(lambda dt, tq=tq: ring_of(tq + dt), j, nq)
                pre()
            if s < NT:
                a_kv(ut_of(s), ring_of(s))
            if doB:
                main()
                pending_drain = drain
                if j == 1:
                    dma("sp", ESP[:, 0:32 * P], scr[u, 1, :, 0:32 * P], "ESP", r=[f"SCR{u}_1_{i}" for i in range(4)],
                        w=["ESP"])
            if doT:
                pending_finish = tt - 2
        assert pending_drain is None
        if pending_finish is not None:
            tail_finish(u, pending_finish)
        gt[0] += NT
    pg.emit(nc, es)
    es.close()
    return nc


def make_core_inputs(x_chunks, shared):
    return None


def prep_shared(meta_tokens, norm_g, w_in, w_proj_a, w_proj_b, w_out, sink_logit, t5_bias, final_g):
    perm = col_perm()
    w = w_in[0][:, perm]
    wp = np.ascontiguousarray(w.reshape(8, P, WCOLS).transpose(1, 0, 2))
    wa = np.ascontiguousarray(w_proj_a[0].reshape(4, P, D).transpose(1, 0, 2))
    wb = np.ascontiguousarray(w_proj_b[0].reshape(4, P, D).transpose(1, 0, 2))
    wo = np.ascontiguousarray(w_out[0].reshape(8, P, D).transpose(1, 0, 2))
    gcol = np.ascontiguousarray(norm_g[0].reshape(8, P).T)
    fgb = np.ascontiguousarray(np.broadcast_to(final_g[None, :], (P, D)))
    sinkb = np.ascontiguousarray(np.broadcast_to(sink_logit[0][None, :], (P, 8)))
    b15 = np.ascontiguousarray(np.repeat(t5_bias[15], N_META)[:, None])
    ident = np.eye(P, dtype=np.float32).astype(ml_dtypes.bfloat16)
    xmeta = np.ascontiguousarray(np.tile(meta_tokens, (8, 1)))
    maskk = np.zeros((P, 4, P), np.float32)
    for c in range(4):
        for hh in range(2):
            maskk[hh * 64:(hh + 1) * 64, c, (2 * c + hh) * 16:(2 * c + hh + 1) * 16] = 1.0
    maskv = np.zeros((P, 8), np.float32)
    for h in range(8):
        maskv[h * 16:(h + 1) * 16, h] = 1.0
    return dict(wp=wp.astype(np.float32), wa=wa, wb=wb, wo=wo, gcol=gcol, fgb=fgb, sinkb=sinkb, b15=b15,
                ident=ident, xmeta=xmeta, maskk=maskk, maskv=maskv)


def ext_tokens(xseq, start, nq):
    n = xseq.shape[0]
    _, actt = unit_slots(start, nq, n)
    return np.ascontiguousarray(xseq[actt])


def kernel(x_prompt, x_sample, meta_tokens, norm_g, w_in, na_rpb, sink_logit, w_proj_a, w_proj_b, w_out,
           t5_bias, final_g):
    f = lambda a: np.asarray(a, dtype=np.float32)
    x_prompt, x_sample = f(x_prompt), f(x_sample)
    NQ = 32
    UNITS = 3
    shared = prep_shared(f(meta_tokens), f(norm_g), f(w_in), f(w_proj_a), f(w_proj_b), f(w_out), f(sink_logit),
                         f(t5_bias), f(final_g))
    nc = build_program(UNITS, NQ)
    in_maps = []
    for c in range(NCORES):
        ulist = [(x_prompt[c // 4], (c % 4) * 4096), (x_sample[2 * c], 0), (x_sample[2 * c + 1], 0)]
        xe = np.stack([ext_tokens(xs, st, NQ) for xs, st in ulist])
        gna, gwa, sna, swa, smeta = build_bias_inputs(f(na_rpb)[0], f(t5_bias), [(st, xs.shape[0]) for xs, st in ulist], NQ)
        m = dict(shared)
        m.update(xe=xe, gna=gna, gwa=gwa, sna=sna, swa=swa, smeta=smeta)
        in_maps.append(m)
    res = run_bass_kernel_spmd(nc, in_maps, core_ids=list(range(NCORES)))
    yp = np.empty_like(x_prompt)
    ys = np.empty_like(x_sample)
    for c in range(NCORES):
        y = np.asarray(res.results[c]["y"])
        yp[c // 4, (c % 4) * 4096:(c % 4 + 1) * 4096] = y[0]
        ys[2 * c] = y[1]
        ys[2 * c + 1] = y[2]
    return (yp, ys)
```

```python
import math
from contextlib import ExitStack

import numpy as np
import ml_dtypes

import concourse.bass as bass
import concourse.mybir as mybir
from concourse.bass_utils import run_bass_kernel_spmd

F32 = mybir.dt.float32
BF16 = mybir.dt.bfloat16
AF = mybir.ActivationFunctionType
ALU = mybir.AluOpType

D = 1024
P = 128
N_META = 16
GRID_W = 64
NCORES = 8
EPS = 1e-6
NEG = -30000.0

QA, QB, KA, KB, VA, VB, ZA, ZB, GA, GB, WCOLS = 0, 512, 1024, 1536, 1792, 2304, 2432, 2944, 3456, 4480, 5504
WBLOCKS = [(0, 1024), (1024, 1792), (1792, 2432), (2432, 3456), (3456, 4480), (4480, 5504)]


def wblk(col):
    for i, (a, b) in enumerate(WBLOCKS):
        if a <= col < b:
            return i
    raise ValueError(col)


def t5_bucket_np(rel):
    rel = np.asarray(rel, dtype=np.int64)
    half, exact = 16, 8
    ret = np.where(rel > 0, half, 0)
    n = np.abs(rel)
    nf = np.maximum(n, 1).astype(np.float32)
    large = exact + (np.log(nf / np.float32(exact)) / np.float32(math.log(128 / exact))
                     * np.float32(half - exact)).astype(np.int32)
    for nn, b in ((8, 8), (16, 10), (32, 12), (64, 14)):
        large = np.where(n == nn, b, large)
    large = np.minimum(large, half - 1)
    return ret + np.where(n < exact, n, large)


def col_perm():
    offs = np.cumsum([0, 512, 512, 512, 512, 512, 128, 128, 512, 1024, 1024])
    qa, ka, va, za, qb, kb, vb, zb, ga, gb = [np.arange(offs[i], offs[i + 1]) for i in range(10)]
    kbd = np.concatenate([kb[0:64], kb[0:64], kb[64:128], kb[64:128]])
    perm = np.concatenate([qa, qb, ka, kbd, va, vb, za, zb, ga, gb])
    assert perm.shape[0] == WCOLS
    return perm


def unit_slots(start, nq, n):
    nt = nq + 4
    pos = start - 256 + np.arange(nt * P)
    act = np.where(pos < 0, pos + 512, np.where(pos >= n, pos - 512, pos))
    return pos, act


def na_tile(rpb_ext, pos, act, n, start, j, dt):
    R = n // GRID_W
    tq = start + j * P + np.arange(P)
    e = (j + 2 + dt) * P + np.arange(P)
    kpos, kact = pos[e], act[e]
    qr, qc = tq // GRID_W, tq % GRID_W
    kprow = np.floor_divide(kpos, GRID_W)
    kr, kc = kact // GRID_W, kact % GRID_W
    cs = np.clip(qc - 8, 0, GRID_W - 16)
    rs = np.clip(qr - 4, 0, R - 8)
    vis = ((kprow[:, None] >= qr[None, :] - 4) & (kprow[:, None] <= qr[None, :] + 3)
           & (kc[:, None] >= cs[None, :]) & (kc[:, None] < cs[None, :] + 16))
    inwin = (kr[:, None] >= rs[None, :]) & (kr[:, None] < rs[None, :] + 8)
    assert not np.any(vis & ~inwin), "halo wrap produced an out-of-window key"
    rr = np.clip(kr[:, None] - qr[None, :] + 7, 0, 14)
    cc = np.clip(kc[:, None] - qc[None, :] + 15, 0, 30)
    idx = rr * 31 + cc
    nmask = 15 * 31
    idx = np.where(vis, idx, nmask)
    return idx


def wa_tile(pos, n, start, j, b):
    tq = start + j * P + np.arange(P)
    e = (j + 2 + b) * P + np.arange(P)
    kpos = pos[e]
    rel = kpos[:, None] - tq[None, :]
    vis = (np.abs(rel) <= 128) & (kpos[:, None] >= 0) & (kpos[:, None] < n)
    bk = t5_bucket_np(np.clip(rel, -200, 200))
    return np.where(vis, bk, 32)


def meta_tile(start, j):
    tq = start + j * P + np.arange(P)
    m = np.arange(N_META)
    rel = m[:, None] - (N_META + tq[None, :])
    return t5_bucket_np(np.clip(rel, -200, 200))


def build_bias_inputs(na_rpb, t5_bias, units, nq):
    rpb_ext = np.concatenate([na_rpb.reshape(8, 15 * 31), np.full((8, 1), NEG, np.float32)], axis=1)
    t5_ext = np.concatenate([t5_bias, np.full((1, 8), NEG, np.float32)], axis=0)

    def na_vals(idx):
        return np.ascontiguousarray(rpb_ext[:, idx].transpose(1, 0, 2))

    def wa_vals(bk):
        return np.ascontiguousarray(t5_ext[bk].transpose(0, 2, 1))

    big_n = 1 << 20
    gstart = 1 << 16
    gpos, gact = unit_slots(gstart, 8, big_n)
    gna = np.stack([na_vals(na_tile(rpb_ext, gpos, gact, big_n, gstart, 4, dt)) for dt in (-2, -1, 0, 1, 2)], axis=1)
    gwa = np.stack([wa_vals(wa_tile(gpos, big_n, gstart, 4, b)) for b in (-1, 0, 1)], axis=1)
    sna, swa, smeta = [], [], []
    for (start, n) in units:
        pos, act = unit_slots(start, nq, n)
        spec = [(0, -2), (0, -1), (1, -2), (nq - 2, 2), (nq - 1, 1), (nq - 1, 2)]
        sna.append(np.stack([na_vals(na_tile(rpb_ext, pos, act, n, start, j, dt)) for (j, dt) in spec], axis=1))
        swa.append(np.stack([wa_vals(wa_tile(pos, n, start, 0, -1)), wa_vals(wa_tile(pos, n, start, nq - 1, 1))], axis=1))
        mb = meta_tile(start, 0)
        mt = t5_bias[mb]
        smeta.append(np.ascontiguousarray(mt.transpose(2, 0, 1)).reshape(P, P))
    return (gna.astype(np.float32), gwa.astype(np.float32), np.stack(sna).astype(np.float32),
            np.stack(swa).astype(np.float32), np.stack(smeta).astype(np.float32))


class Op:
    __slots__ = ("eng", "fn", "deps", "sig", "sigval", "dma_key", "dma_val")

    def __init__(self, eng, fn):
        self.eng, self.fn = eng, fn
        self.deps = ()
        self.sig = False
        self.sigval = 0
        self.dma_key = None
        self.dma_val = 0


class Prog:
    ENG = ("pe", "act", "dve", "pool", "sp")

    LIMIT = None
    MARKS = []

    def mark(self, name):
        Prog.MARKS.append((name, self.nops))

    def __init__(self):
        self.nops = 0
        self.q = {e: [] for e in self.ENG}
        self.lastw = {}
        self.readers = {}
        self.dma_cnt = {}
        self.out_dmas = []

    def op(self, eng, fn, r=(), w=(), dma=None):
        o = Op(eng, fn)
        self.nops += 1
        if Prog.LIMIT is not None and self.nops > Prog.LIMIT:
            return o
        deps = set()
        for x in r:
            lw = self.lastw.get(x)
            if lw is not None:
                deps.add(lw)
        for x in w:
            lw = self.lastw.get(x)
            if lw is not None:
                deps.add(lw)
            rd = self.readers.get(x)
            if rd:
                deps.update(rd.values())
        o.deps = tuple(deps)
        for x in r:
            rk = eng if dma is None else ("dma", self.nops)
            self.readers.setdefault(x, {})[rk] = o
        for x in w:
            self.lastw[x] = o
            self.readers[x] = {}
        if dma is not None:
            c = self.dma_cnt.get(dma, 0) + 1
            self.dma_cnt[dma] = c
            o.dma_key, o.dma_val = dma, 16 * c
        self.q[eng].append(o)
        return o

    def emit(self, nc, es):
        for e in self.ENG:
            for o in self.q[e]:
                for d in o.deps:
                    if d.dma_key is None and not (d.eng == "pe" and o.eng == "pe" and o.dma_key is None):
                        d.sig = True
        for e in self.ENG:
            c = 0
            for o in self.q[e]:
                if o.sig:
                    c += 1
                    o.sigval = c
        esem = {e: es.enter_context(nc.semaphore("s_" + e)) for e in ("pe", "act", "dve", "pool")}
        dsem = {k: es.enter_context(nc.semaphore("d_" + k)) for k in self.dma_cnt}
        blk = es.enter_context(nc.Block())
        final = [(o.dma_key, o.dma_val) for o in self.out_dmas if o.dma_key is not None]

        def run(e, engobj):
            waited = {}
            for o in self.q[e]:
                for d in o.deps:
                    if d.dma_key is not None:
                        key, val, sem = ("d", d.dma_key), d.dma_val, dsem[d.dma_key]
                    else:
                        if d.eng == "pe" and e == "pe" and o.dma_key is None:
                            continue
                        key, val, sem = ("e", d.eng), d.sigval, esem[d.eng]
                    if waited.get(key, 0) >= val:
                        continue
                    waited[key] = val
                    engobj.wait_ge(sem, val)
                ins = o.fn(engobj)
                if o.dma_key is not None:
                    ins.then_inc(dsem[o.dma_key], 16)
                elif o.sig:
                    ins.then_inc(esem[e], 1)
            if e == "pool":
                fin = {}
                for k, v in final:
                    fin[k] = max(fin.get(k, 0), v)
                for k, v in fin.items():
                    engobj.wait_ge(dsem[k], v)

        blk.tensor(lambda t: run("pe", t))
        blk.scalar(lambda t: run("act", t))
        blk.vector(lambda t: run("dve", t))
        blk.gpsimd(lambda t: run("pool", t))
        blk.sync(lambda t: run("sp", t))


def build_program(units, nq):
    U = units
    NT = nq + 4
    NR = 5
    nc = bass.Bass("TRN2", target_bir_lowering=False)
    dram = lambda n, s, dt=F32, kind="ExternalInput": nc.dram_tensor(n, s, dt, kind=kind)
    xe = dram("xe", [U, NT * P, D])
    wp = dram("wp", [P, 8, WCOLS])
    wa_d = dram("wa", [P, 4, D])
    wb_d = dram("wb", [P, 4, D])
    wo_d = dram("wo", [P, 8, D])
    gcol_d = dram("gcol", [P, 8])
    fgb_d = dram("fgb", [P, D])
    sink_d = dram("sinkb", [P, 8])
    b15_d = dram("b15", [P, 1])
    ident_d = dram("ident", [P, P], BF16)
    xmeta_d = dram("xmeta", [P, D])
    maskk_d = dram("maskk", [P, 4, P])
    maskv_d = dram("maskv", [P, 8])
    gna_d = dram("gna", [P, 5, 8, P])
    gwa_d = dram("gwa", [P, 3, 8, P])
    sna_d = dram("sna", [U, P, 6, 8, P])
    swa_d = dram("swa", [U, P, 2, 8, P])
    smeta_d = dram("smeta", [U, P, P])
    y_d = dram("y", [U, nq * P, D], F32, kind="ExternalOutput")
    scr = nc.dram_tensor("espscr", [U, 2, P, 33 * P], BF16)

    pg = Prog()
    es = ExitStack()
    sb = lambda n, s, dt: es.enter_context(nc.sbuf_tensor(n, s, dt))
    W = sb("W", [P, 8, WCOLS], BF16)
    Wa = sb("Wa", [P, 4, D], BF16)
    Wb = sb("Wb", [P, 4, D], BF16)
    Wo = sb("Wo", [P, 8, D], BF16)
    EBNA = sb("EBNA", [P, 5, 8 * P], BF16)
    EBWA = sb("EBWA", [P, 3, 8 * P], BF16)
    ESP = sb("ESP", [P, 33 * P], BF16)
    BDK = sb("BDK", [P, 8, P], BF16)
    VMA = sb("VMA", [P, 2, 260], BF16)
    VMB = sb("VMB", [P, 2, 260], BF16)
    FGB = sb("FGB", [P, D], F32)
    IDT = sb("IDT", [P, P], BF16)
    CST = sb("CST", [P, 32], F32)
    ST = sb("ST", [P, 32], F32)
    XA = [sb(f"XA{i}", [P, D], F32) for i in range(2)]
    XB = sb("XB", [P, D], F32)
    XN = sb("XN", [P, D], BF16)
    UT = [sb(f"UT{i}", [P, 8, P], BF16) for i in range(3)]
    KAr = [sb(f"KAr{i}", [P, 4, P], BF16) for i in range(NR)]
    KBr = [sb(f"KBr{i}", [P, 2, P], BF16) for i in range(NR)]
    VAr = [sb(f"VAr{i}", [P, 8, 72], BF16) for i in range(NR)]
    VBr = [sb(f"VBr{i}", [P, 2, 72], BF16) for i in range(NR)]
    QEO = sb("QEO", [P, 8, 2, P], BF16)
    SZ = sb("SZ", [P, D], BF16)
    TG = sb("TG", [P, 2 * D], BF16)
    PT = [sb(f"PT{i}", [P, 4 * P], BF16) for i in range(4)]
    PM = sb("PM", [P, 2 * P], BF16)
    G = sb("G", [P, D], BF16)
    GT = sb("GT", [P, 8, P], BF16)
    M1 = sb("M1", [P, 512], F32)
    MSK = M1[:, :].rearrange("p (c k) -> p c k", k=P)
    M2 = sb("M2", [P, 512], F32)
    banks = [es.enter_context(nc.psum_tensor(f"bank{i}", [P, 512], F32)) for i in range(8)]
    bview = [b.bitcast(BF16) for b in banks]

    gctr = [0]
    sctr = [0]

    def nextG():
        i = gctr[0] % 3
        gctr[0] += 1
        return i

    def nextS():
        i = 3 + sctr[0] % 3
        sctr[0] += 1
        return i

    BO = [6, 7]
    flat = lambda ap3: ap3.rearrange("p a b -> p (a b)")

    def dma(eng, out, in_, key, r=(), w=()):
        return pg.op(eng, lambda e: e.dma_start(out=out, in_=in_), r=r, w=w, dma=key)

    def act(out, in_, func, r, w, **kw):
        return pg.op("act", lambda e: e.activation(out=out, in_=in_, func=func, **kw), r=r, w=w)

    dma("sp", IDT[:, :], ident_d[:, :], "IDT", w=["IDT"])
    dma("sp", CST[:, 0:8], gcol_d[:, :], "C0", w=["GCOL"])
    dma("sp", CST[:, 8:16], sink_d[:, :], "C1", w=["SINK"])
    dma("sp", CST[:, 16:17], b15_d[:, :], "C2", w=["B15"])
    dma("sp", CST[:, 17:25], maskv_d[:, :], "C3", w=["MASKV"])
    dma("sp", FGB[:, :], fgb_d[:, :], "FGB", w=["FGB"])
    dma("sp", MSK, maskk_d[:, :, :], "MSK", w=["M1"])
    act(CST[:, 8:16], CST[:, 8:16], AF.Exp, r=["SINK"], w=["SINK"])
    pg.op("dve", lambda e: e.memset(ST[:, 24:25], 1.0), w=["ONE"])

    pg.mark("consts")
    TGf = TG.bitcast(F32)
    SZf = SZ.bitcast(F32)
    f32flat = lambda t: t.bitcast(F32)[:, :, :].rearrange("p a b -> p (a b)")
    big = [dict(ap=XA[0][:, :], res=["XA0"], key="XA0"), dict(ap=XA[1][:, :], res=["XA1"], key="XA1"),
           dict(ap=XB[:, :], res=["XB"], key="XB"),
           dict(ap=TGf[:, :], res=["TG0", "TG1", "TG2", "TG3"], key="STG_TG")]
    small = big + [dict(ap=SZf[:, :], res=["SZ0", "SZ1"], key="STG_SZ"),
                   dict(ap=f32flat(GT), res=["GT"], key="STG_GT"),
                   dict(ap=f32flat(UT[1]), res=["UT1"], key="STG_UT1"),
                   dict(ap=f32flat(UT[2]), res=["UT2"], key="STG_UT2"),
                   dict(ap=M2[:, :], res=["M2"], key="STG_M2")]
    bigc = [0]
    smallc = [0]

    def next_big():
        i = bigc[0] % len(big)
        bigc[0] += 1
        return big[i]

    def next_small():
        i = smallc[0] % len(small)
        smallc[0] += 1
        return small[i]

    castctr = [0]
    dq = [0]

    def cast_block(dst, src_d, res, scale_ap=None, scale_res=None):
        n = dst.shape[-1]
        eng = "act" if castctr[0] % 2 == 0 else "dve"
        castctr[0] += 1
        for off in range(0, n, 512):
            m = min(512, n - off)
            u_ = next_small()
            st = u_["ap"][:, 0:m]
            dpiece, spiece = dst[:, off:off + m], src_d[:, off:off + m]
            dq[0] += 1
            dma("sp" if dq[0] % 2 else "act", st, spiece, u_["key"], w=u_["res"])
            rr = list(u_["res"]) + ([scale_res] if scale_res else [])
            if eng == "act":
                if scale_ap is None:
                    act(dpiece, st, AF.Copy, r=rr, w=[res])
                else:
                    act(dpiece, st, AF.Copy, r=rr, w=[res], scale=scale_ap)
            else:
                if scale_ap is None:
                    pg.op("dve", lambda e, dpiece=dpiece, st=st: e.tensor_copy(out=dpiece, in_=st), r=rr, w=[res])
                else:
                    pg.op("dve", lambda e, dpiece=dpiece, st=st: e.tensor_scalar(
                        out=dpiece, in0=st, scalar1=scale_ap, scalar2=None, op0=ALU.mult), r=rr, w=[res])

    for c in range(8):
        for bi, (a, b) in enumerate(WBLOCKS):
            cast_block(W[:, c, a:b], wp[:, c, a:b], f"W{c}_{bi}", scale_ap=CST[:, c:c + 1], scale_res="GCOL")
    for c in range(4):
        cast_block(Wa[:, c, :], wa_d[:, c, :], f"Wa{c}")
        cast_block(Wb[:, c, :], wb_d[:, c, :], f"Wb{c}")
    for c in range(8):
        cast_block(Wo[:, c, :], wo_d[:, c, :], f"Wo{c}")

    pg.mark("wcast")

    def exp_block(dst, src_d, res):
        n = dst.shape[-1]
        u_ = next_big()
        dma("sp", u_["ap"][:, 0:n], src_d, u_["key"], w=u_["res"])
        act(dst, u_["ap"][:, 0:n], AF.Exp, r=u_["res"], w=[res])

    for i in range(5):
        exp_block(EBNA[:, i, :], gna_d[:, i, :, :].rearrange("p h q -> p (h q)"), f"EBNA{i}")
    for i in range(3):
        exp_block(EBWA[:, i, :], gwa_d[:, i, :, :].rearrange("p h q -> p (h q)"), f"EBWA{i}")

    pg.mark("ebgen")
    tb = [(XN, "XN"), (G, "G")]
    tbc = [0]
    for u in range(U):
        pieces = [(0, 0, sna_d[u, :, 0, :, :]), (0, 1, sna_d[u, :, 1, :, :]), (0, 2, sna_d[u, :, 2, :, :]),
                  (0, 3, swa_d[u, :, 0, :, :]),
                  (1, 0, sna_d[u, :, 3, :, :]), (1, 1, sna_d[u, :, 4, :, :]), (1, 2, sna_d[u, :, 5, :, :]),
                  (1, 3, swa_d[u, :, 1, :, :])]
        for (side, pi, src) in pieces:
            tbuf, tres = tb[tbc[0] % 2]
            tbc[0] += 1
            exp_block(tbuf[:, :], src.rearrange("p h q -> p (h q)"), tres)
            dma("sp", scr[u, side, :, pi * 1024:(pi + 1) * 1024], tbuf[:, :], "SCRW" + tres, r=[tres],
                w=[f"SCR{u}_{side}_{pi}"])
        tbuf, tres = tb[tbc[0] % 2]
        tbc[0] += 1
        exp_block(tbuf[:, 0:P], smeta_d[u, :, :], tres)
        dma("sp", scr[u, 0, :, 4096:4096 + P], tbuf[:, 0:P], "SCRW" + tres, r=[tres], w=[f"SCR{u}_0_4"])

    pg.mark("scratch")

    def a_pre(xbuf, xres, ss_col):
        ss, rms, rstd = ST[:, ss_col:ss_col + 1], ST[:, ss_col + 1:ss_col + 2], ST[:, ss_col + 2:ss_col + 3]
        nm = f"ST{ss_col}"
        act(XN[:, :], xbuf[:, :], AF.Square, r=[xres], w=["XN", nm + "a"], accum_out=ss)
        act(rms, ss, AF.Sqrt, r=[nm + "a", "EPS"], w=[nm + "b"], scale=1.0 / D, bias=EPS_AP[0])
        pg.op("dve", lambda e: e.reciprocal(out=rstd, in_=rms), r=[nm + "b"], w=[nm + "c"])
        act(XN[:, :], xbuf[:, :], AF.Copy, r=[xres, nm + "c"], w=["XN"], scale=rstd)

    EPS_AP = [None]

    def a_tr(uslot):
        g = nextG()
        for c in range(8):
            pg.op("pe", lambda e, c=c: e.transpose(out=bview[g][:, c * P:(c + 1) * P], in_=XN[:, c * P:(c + 1) * P],
                                                    identity=IDT[:, :]), r=["XN", "IDT"], w=[f"B{g}"])
        pg.op("dve", lambda e: e.tensor_copy(out=flat(UT[uslot][:, :, :]), in_=bview[g][:, 0:D]),
              r=[f"B{g}"], w=[f"UT{uslot}"])

    def fm_proj(g, off, uslot, col):
        for c in range(8):
            pg.op("pe", lambda e, c=c: e.matmul(banks[g][:, off * P:(off + 1) * P], lhsT=W[:, c, col:col + P],
                                                 rhs=UT[uslot][:, c, :], start=(c == 0), stop=(c == 7)),
                  r=[f"W{c}_{wblk(col)}", f"UT{uslot}"], w=[f"B{g}"])

    def tm_proj(g, boff, n, uslot, col):
        for c in range(8):
            pg.op("pe", lambda e, c=c: e.matmul(banks[g][:, boff:boff + n], lhsT=UT[uslot][:, c, :],
                                                 rhs=W[:, c, col:col + n], start=(c == 0), stop=(c == 7)),
                  r=[f"W{c}_{wblk(col)}", f"UT{uslot}"], w=[f"B{g}"])

    def a_kv(uslot, rs):
        g1 = nextG()
        for f in range(4):
            fm_proj(g1, f, uslot, KA + f * P)
        act(flat(KAr[rs][:, :, :]), banks[g1][:, :], AF.Copy, r=[f"B{g1}"], w=[f"KA{rs}"])
        g2 = nextG()
        for f in range(2):
            fm_proj(g2, f, uslot, KB + f * P)
        tm_proj(g2, 256, 128, uslot, VB)
        act(flat(KBr[rs][:, :, :]), banks[g2][:, 0:256], AF.Copy, r=[f"B{g2}"], w=[f"KB{rs}"])
        act(VBr[rs][:, :, 0:64], banks[g2][:, 256:384].rearrange("p (h e) -> p h e", e=64), AF.Copy,
            r=[f"B{g2}"], w=[f"VB{rs}"])
        g3 = nextG()
        tm_proj(g3, 0, 512, uslot, VA)
        pg.op("dve", lambda e: e.tensor_copy(out=VAr[rs][:, :, 0:64],
                                             in_=banks[g3][:, :].rearrange("p (h e) -> p h e", e=64)),
              r=[f"B{g3}"], w=[f"VA{rs}"])

    def qzg_phases(uslot):
        ph = []

        def q1():
            g = nextG()
            for f in range(4):
                fm_proj(g, f, uslot, QA + f * P)
            act(QEO[0:64, 0:4, 0, :], banks[g][0:64, :].rearrange("p (c q) -> p c q", q=P), AF.Copy,
                r=[f"B{g}"], w=["QA"])
            act(QEO[64:128, 0:4, 1, :], banks[g][64:128, :].rearrange("p (c q) -> p c q", q=P), AF.Copy,
                r=[f"B{g}"], w=["QA"])

        def q2():
            g = nextG()
            for f in range(4):
                fm_proj(g, f, uslot, QB + f * P)
            pg.op("dve", lambda e, g=g: e.tensor_copy(
                out=QEO[0:64, 4:8, 0, :], in_=banks[g][0:64, :].rearrange("p (c q) -> p c q", q=P)),
                r=[f"B{g}"], w=["QB"])
            pg.op("dve", lambda e, g=g: e.tensor_copy(
                out=QEO[64:128, 4:8, 1, :], in_=banks[g][64:128, :].rearrange("p (c q) -> p c q", q=P)),
                r=[f"B{g}"], w=["QB"])

        def zk(k, col):
            g = nextG()
            tm_proj(g, 0, 512, uslot, col)
            act(SZ[:, k * 512:(k + 1) * 512], banks[g][:, :], AF.Tanh, r=[f"B{g}"], w=[f"SZ{k}"], scale=0.5)
            pg.op("dve", lambda e, g=g, k=k: e.scalar_tensor_tensor(
                out=SZ[:, k * 512:(k + 1) * 512], in0=SZ[:, k * 512:(k + 1) * 512], scalar=1.0, in1=banks[g][:, :],
                op0=ALU.add, op1=ALU.mult), r=[f"B{g}", f"SZ{k}"], w=[f"SZ{k}"])

        def gk(k):
            g = nextG()
            tm_proj(g, 0, 512, uslot, GA + k * 512)
            act(TG[:, k * 512:(k + 1) * 512], banks[g][:, :], AF.Tanh, r=[f"B{g}"], w=[f"TG{k}"], scale=0.5)

        ph.append(q1)
        ph.append(q2)
        ph.append(lambda: zk(0, ZA))
        ph.append(lambda: zk(1, ZB))
        for k in range(4):
            ph.append(lambda k=k: gk(k))
        return ph

    ptc = [0]
    LOOK = 3

    def attention(rs_of, j, nqv):
        def meta():
            s0 = nextS()
            for br in range(2):
                for c in range(4):
                    for qi in range(2):
                        pg.op("pe", lambda e, br=br, c=c, qi=qi: e.matmul(
                            banks[s0][:, br * P:(br + 1) * P], lhsT=BDK[:, br * 4 + c, :],
                            rhs=QEO[:, br * 4 + c, qi, :],
                            start=(c == 0 and qi == 0), stop=(c == 3 and qi == 1)),
                            r=["BDK", "QA" if br == 0 else "QB"], w=[f"B{s0}"])
            act(PM[:, 0:P], banks[s0][:, 0:P], AF.Exp, r=[f"B{s0}"], w=["PMA"], scale=0.125)
            if j == 0:
                act(PM[:, P:2 * P], banks[s0][:, P:2 * P], AF.Exp, r=[f"B{s0}"], w=["PMB"], scale=0.125)
                pg.op("dve", lambda e: e.tensor_tensor(out=PM[:, P:2 * P], in0=PM[:, P:2 * P],
                                                       in1=ESP[:, 32 * P:33 * P], op=ALU.mult),
                      r=["PMB", "ESP"], w=["PMB"])
            else:
                act(PM[:, P:2 * P], banks[s0][:, P:2 * P], AF.Exp, r=[f"B{s0}", "B15"], w=["PMB"], scale=0.125,
                    bias=CST[:, 16:17])

        jobs = []
        for br in range(2):
            dts = (-2, -1, 0, 1, 2) if br == 0 else (-1, 0, 1)
            for hg in range(2):
                for di, dt in enumerate(dts):
                    jobs.append(dict(br=br, hg=hg, dt=dt, first=(di == 0), last=(di == len(dts) - 1)))

        def emit_qk(jb):
            br, hg, dt = jb["br"], jb["hg"], jb["dt"]
            rs = rs_of(dt)
            sbk = nextS()
            jb["rs"], jb["sbk"] = rs, sbk
            if br == 0:
                for pp in range(2):
                    c = 2 * hg + pp
                    pg.op("pe", lambda e, sbk=sbk, pp=pp, c=c, rs=rs: e.matmul(
                        banks[sbk][:, pp * 256:(pp + 1) * 256], lhsT=KAr[rs][:, c, :],
                        rhs=QEO[:, c, :, :].rearrange("p a q -> p (a q)"), start=True, stop=True),
                        r=[f"KA{rs}", "QA"], w=[f"B{sbk}"])
            else:
                pg.op("pe", lambda e, sbk=sbk, hg=hg, rs=rs: e.matmul(
                    banks[sbk][:, :], lhsT=KBr[rs][:, hg, :],
                    rhs=QEO[:, 4 + 2 * hg:6 + 2 * hg, :, :].rearrange("p c a q -> p (c a q)"), start=True, stop=True),
                    r=[f"KB{rs}", "QB"], w=[f"B{sbk}"])

        def emit_sm(jb):
            br, hg, dt, sbk = jb["br"], jb["hg"], jb["dt"], jb["sbk"]
            pb = ptc[0] % 4
            ptc[0] += 1
            jb["pb"] = pb
            act(PT[pb][:, :], banks[sbk][:, :], AF.Exp, r=[f"B{sbk}"], w=[f"PT{pb}"], scale=0.125)
            if br == 0:
                base = None
                if j == 0 and dt == -2:
                    base = 0
                elif j == 0 and dt == -1:
                    base = 8
                elif j == 1 and dt == -2:
                    base = 16
                elif j == nqv - 2 and dt == 2:
                    base = 0
                elif j == nqv - 1 and dt == 1:
                    base = 8
                elif j == nqv - 1 and dt == 2:
                    base = 16
                if base is None:
                    eb, ebres = EBNA[:, dt + 2, hg * 512:(hg + 1) * 512], f"EBNA{dt + 2}"
                else:
                    eb, ebres = ESP[:, (base + 4 * hg) * P:(base + 4 * hg + 4) * P], "ESP"
            else:
                if (j == 0 and dt == -1) or (j == nqv - 1 and dt == 1):
                    eb, ebres = ESP[:, (24 + 4 * hg) * P:(24 + 4 * hg + 4) * P], "ESP"
                else:
                    eb, ebres = EBWA[:, dt + 1, hg * 512:(hg + 1) * 512], f"EBWA{dt + 1}"
            pg.op("dve", lambda e, pb=pb, eb=eb: e.tensor_tensor(out=PT[pb][:, :], in0=PT[pb][:, :], in1=eb,
                                                                  op=ALU.mult),
                  r=[f"PT{pb}", ebres], w=[f"PT{pb}"])

        def emit_pv(jb):
            br, hg, rs, pb = jb["br"], jb["hg"], jb["rs"], jb["pb"]
            ob = BO[hg]
            if jb["first"]:
                VM = VMA if br == 0 else VMB
                pg.op("pe", lambda e, ob=ob, VM=VM, hg=hg, br=br: e.matmul(
                    banks[ob][:, 0:260], lhsT=PM[:, br * P:(br + 1) * P], rhs=VM[:, hg, :], start=True, stop=False,
                    skip_group_check=True),
                    r=["PMA" if br == 0 else "PMB", "VM"], w=[f"B{ob}"])
            last = jb["last"]
            for hl in range(4):
                h = 4 * hg + hl
                rhs = VAr[rs][:, h, 0:65] if br == 0 else VBr[rs][:, h // 4, 0:65]
                pg.op("pe", lambda e, ob=ob, hl=hl, pb=pb, rhs=rhs, last=last: e.matmul(
                    banks[ob][:, hl * 65:(hl + 1) * 65], lhsT=PT[pb][:, hl * P:(hl + 1) * P], rhs=rhs,
                    start=False, stop=last, skip_group_check=True),
                    r=[f"PT{pb}", (f"VA{rs}" if br == 0 else f"VB{rs}")], w=[f"B{ob}"])
            if not last:
                return
            ov = banks[ob][:, 0:260].rearrange("p (h e) -> p h e", e=65)
            dcol = 8 + br * 8 + hg * 4
            den = ST[:, dcol:dcol + 4]
            dres = f"DEN{br}{hg}"
            if br == 0:
                pg.op("dve", lambda e, ov=ov, den=den: e.reciprocal(out=den, in_=ov[:, :, 64]),
                      r=[f"B{ob}"], w=[dres])
            else:
                pg.op("dve", lambda e, ov=ov, den=den, hg=hg: e.tensor_tensor(
                    out=den, in0=ov[:, :, 64], in1=CST[:, 8 + 4 * hg:12 + 4 * hg], op=ALU.add),
                    r=[f"B{ob}", "SINK"], w=[dres])
                pg.op("dve", lambda e, den=den: e.reciprocal(out=den, in_=den), r=[dres], w=[dres])
            o0 = br * 512 + hg * 256
            szv = SZ[:, o0:o0 + 256].rearrange("p (h e) -> p h e", e=64)
            pg.op("pool", lambda e, szv=szv, den=den: e.tensor_tensor(
                out=szv, in0=szv, in1=den.unsqueeze(2).broadcast_to([P, 4, 64]), op=ALU.mult),
                r=[dres, f"SZ{br}"], w=[f"SZ{br}"])
            gv = G[:, o0:o0 + 256].rearrange("p (h e) -> p h e", e=64)
            pg.op("dve", lambda e, gv=gv, ov=ov, szv=szv: e.tensor_tensor(out=gv, in0=ov[:, :, 0:64], in1=szv,
                                                                          op=ALU.mult),
                  r=[f"B{ob}", f"SZ{br}"], w=["G"])

        n = len(jobs)

        def pre():
            meta()
            for i in range(LOOK):
                emit_qk(jobs[i])
                emit_sm(jobs[i])

        def main():
            for i in range(n - LOOK):
                emit_qk(jobs[i + LOOK])
                emit_sm(jobs[i + LOOK])
                emit_pv(jobs[i])

        def drain():
            for i in range(n - LOOK, n):
                emit_pv(jobs[i])

        return pre, main, drain

    def tail_phases(u, j):
        ph = []

        def p1():
            g = nextG()
            for c in range(8):
                pg.op("pe", lambda e, c=c, g=g: e.transpose(out=bview[g][:, c * P:(c + 1) * P],
                                                             in_=G[:, c * P:(c + 1) * P], identity=IDT[:, :]),
                      r=["G", "IDT"], w=[f"B{g}"])
            act(flat(GT[:, :, :]), bview[g][:, 0:D], AF.Copy, r=[f"B{g}"], w=["GT"])

        def ph_y(half):
            ga = nextG()
            for c in range(4):
                pg.op("pe", lambda e, c=c, ga=ga, half=half: e.matmul(
                    banks[ga][:, :], lhsT=GT[:, c, :], rhs=Wa[:, c, half * 512:(half + 1) * 512],
                    start=(c == 0), stop=(c == 3)), r=["GT", f"Wa{c}"], w=[f"B{ga}"])
            pg.op("dve", lambda e, ga=ga, half=half: e.scalar_tensor_tensor(
                out=M1[:, :], in0=TG[:, half * 512:(half + 1) * 512], scalar=1.0, in1=banks[ga][:, :],
                op0=ALU.add, op1=ALU.mult), r=[f"B{ga}", f"TG{half}"], w=["M1"])
            gb = nextG()
            for c in range(4):
                pg.op("pe", lambda e, c=c, gb=gb, half=half: e.matmul(
                    banks[gb][:, :], lhsT=GT[:, 4 + c, :], rhs=Wb[:, c, half * 512:(half + 1) * 512],
                    start=(c == 0), stop=(c == 3)), r=["GT", f"Wb{c}"], w=[f"B{gb}"])
            pg.op("dve", lambda e, gb=gb, half=half: e.scalar_tensor_tensor(
                out=M2[:, :], in0=TG[:, D + half * 512:D + (half + 1) * 512], scalar=1.0, in1=banks[gb][:, :],
                op0=ALU.add, op1=ALU.mult), r=[f"B{gb}", f"TG{2 + half}"], w=["M2"])
            pg.op("dve", lambda e, half=half: e.tensor_tensor(out=G[:, half * 512:(half + 1) * 512], in0=M1[:, :],
                                                              in1=M2[:, :], op=ALU.add),
                  r=["M1", "M2"], w=["G"])

        def p4():
            g = nextG()
            for c in range(8):
                pg.op("pe", lambda e, c=c, g=g: e.transpose(out=bview[g][:, c * P:(c + 1) * P],
                                                             in_=G[:, c * P:(c + 1) * P], identity=IDT[:, :]),
                      r=["G", "IDT"], w=[f"B{g}"])
            act(flat(GT[:, :, :]), bview[g][:, 0:D], AF.Copy, r=[f"B{g}"], w=["GT"])

        def ph_o(half):
            go = nextG()
            for c in range(8):
                pg.op("pe", lambda e, c=c, go=go, half=half: e.matmul(
                    banks[go][:, :], lhsT=GT[:, c, :], rhs=Wo[:, c, half * 512:(half + 1) * 512],
                    start=(c == 0), stop=(c == 7)), r=["GT", f"Wo{c}"], w=[f"B{go}"])
            pg.op("dve", lambda e, go=go, half=half: e.scalar_tensor_tensor(
                out=XB[:, half * 512:(half + 1) * 512], in0=banks[go][:, :], scalar=0.25,
                in1=XB[:, half * 512:(half + 1) * 512], op0=ALU.mult, op1=ALU.add), r=[f"B{go}", "XB"], w=["XB"])

        ph.append(p1)
        ph.append(lambda: ph_y(0))
        ph.append(lambda: ph_y(1))
        ph.append(p4)
        ph.append(lambda: ph_o(0))
        ph.append(lambda: ph_o(1))
        return ph

    def tail_finish(u, j):
        ss, rms, rstd = ST[:, 4:5], ST[:, 5:6], ST[:, 6:7]
        act(flat(GT[:, :, :]), XB[:, :], AF.Square, r=["XB"], w=["GT", "ST4a"], accum_out=ss)
        act(rms, ss, AF.Sqrt, r=["ST4a", "EPS"], w=["ST4b"], scale=1.0 / D, bias=EPS_AP[0])
        pg.op("dve", lambda e: e.reciprocal(out=rstd, in_=rms), r=["ST4b"], w=["ST4c"])
        pg.op("dve", lambda e: e.scalar_tensor_tensor(out=XB[:, :], in0=XB[:, :], scalar=rstd, in1=FGB[:, :],
                                                      op0=ALU.mult, op1=ALU.mult), r=["XB", "ST4c", "FGB"], w=["XB"])
        o = dma("pool", y_d[u, j * P:(j + 1) * P, :], XB[:, :], "OUT", r=["XB"], w=["YOUT"])
        pg.out_dmas.append(o)

    pg.op("dve", lambda e: e.memset(ST[:, 25:26], EPS), w=["EPS"])
    EPS_AP[0] = ST[:, 25:26]
    _orig_act = act

    pg.op("pool", lambda e: e.memset(QEO[:, :, :, :].rearrange("p a b c -> p (a b c)"), 0.0), w=["QA", "QB"])
    for i in range(NR):
        pg.op("pool", lambda e, i=i: e.memset(VAr[i][:, :, 64:65], 1.0), w=[f"VA{i}"])
        pg.op("pool", lambda e, i=i: e.memset(VBr[i][:, :, 64:65], 1.0), w=[f"VB{i}"])

    pg.mark("onescols")
    dma("sp", XA[0][:, :], xmeta_d[:, :], "XA0", w=["XA0"])
    a_pre(XA[0], "XA0", 0)
    pg.mark("meta_pre")
    a_tr(0)
    pg.mark("meta_tr")
    gm = nextG()
    for f in range(4):
        fm_proj(gm, f, 0, KA + f * P)
    for c in range(4):
        pg.op("dve", lambda e, c=c: e.tensor_tensor(out=BDK[:, c, :], in0=banks[gm][:, c * P:(c + 1) * P],
                                                    in1=MSK[:, c, :], op=ALU.mult), r=[f"B{gm}", "M1"], w=["BDK"])
    gm2 = nextG()
    for f in range(2):
        fm_proj(gm2, f, 0, KB + f * P)
    tm_proj(gm2, 256, 128, 0, VB)
    for c in range(4):
        pg.op("dve", lambda e, c=c: e.tensor_tensor(out=BDK[:, 4 + c, :], in0=banks[gm2][:, (c // 2) * P:(c // 2 + 1) * P],
                                                    in1=MSK[:, c, :], op=ALU.mult), r=[f"B{gm2}", "M1"], w=["BDK"])
    gm3 = nextG()
    tm_proj(gm3, 0, 512, 0, VA)
    for h in range(8):
        hg, hl = h // 4, h % 4
        pg.op("dve", lambda e, h=h, hg=hg, hl=hl: e.tensor_scalar(
            out=VMA[:, hg, hl * 65:hl * 65 + 64], in0=banks[gm3][:, h * 64:(h + 1) * 64], scalar1=CST[:, 17 + h:18 + h],
            scalar2=None, op0=ALU.mult), r=[f"B{gm3}", "MASKV"], w=["VM"])
        pg.op("dve", lambda e, h=h, hg=hg, hl=hl: e.tensor_scalar(
            out=VMB[:, hg, hl * 65:hl * 65 + 64], in0=banks[gm2][:, 256 + hg * 64:256 + (hg + 1) * 64],
            scalar1=CST[:, 17 + h:18 + h], scalar2=None, op0=ALU.mult), r=[f"B{gm2}", "MASKV"], w=["VM"])
        pg.op("dve", lambda e, h=h, hg=hg, hl=hl: e.tensor_copy(out=VMA[:, hg, hl * 65 + 64:hl * 65 + 65],
                                                                in_=CST[:, 17 + h:18 + h]), r=["MASKV"], w=["VM"])
        pg.op("dve", lambda e, h=h, hg=hg, hl=hl: e.tensor_copy(out=VMB[:, hg, hl * 65 + 64:hl * 65 + 65],
                                                                in_=CST[:, 17 + h:18 + h]), r=["MASKV"], w=["VM"])

    pg.mark("meta_done")
    gt = [0]
    for u in range(U):
        dma("sp", ESP[:, :], scr[u, 0, :, :], "ESP", r=[f"SCR{u}_0_{i}" for i in range(5)], w=["ESP"])
        base = gt[0]
        xa_of = lambda s: (base + s) % 2
        ring_of = lambda t: (base + t) % NR
        ut_of = lambda t: (base + t) % 3
        dma("sp", XA[xa_of(0)][:, :], xe[u, 0:P, :], f"XA{xa_of(0)}", w=[f"XA{xa_of(0)}"])
        pending_finish = None
        pending_drain = None
        for s in range(NT + 3):
            tq = s - 2
            tt = s - 3
            doB = 2 <= tq <= NT - 3
            doT = 2 <= tt <= NT - 3
            if pending_finish is not None:
                tail_finish(u, pending_finish)
                pending_finish = None
            if s + 1 < NT:
                k = xa_of(s + 1)
                dma("sp", XA[k][:, :], xe[u, (s + 1) * P:(s + 2) * P, :], f"XA{k}", w=[f"XA{k}"])
            if doT:
                dma("sp", XB[:, :], xe[u, tt * P:(tt + 1) * P, :], "XB", w=["XB"])
            if s < NT:
                a_pre(XA[xa_of(s)], f"XA{xa_of(s)}", 0)
            tp = tail_phases(u, tt - 2) if doT else []
            qp = qzg_phases(ut_of(tq)) if doB else []
            def run(lst, i):
                if i < len(lst):
                    lst[i]()
            run(qp, 0)
            run(qp, 1)
            if pending_drain is not None:
                pending_drain()
                pending_drain = None
            run(qp, 2)
            run(qp, 3)
            run(tp, 0)
            if s < NT:
                a_tr(ut_of(s))
            run(tp, 1)
            run(qp, 4)
            run(tp, 2)
            run(qp, 5)
            run(tp, 3)
            run(qp, 6)
            run(tp, 4)
            run(qp, 7)
            run(tp, 5)
            if doB:
                j = tq - 2
                pre, main, drain = attention(lambda dt, tq=tq: ring_of(tq + dt), j, nq)
                pre()
            if s < NT:
                a_kv(ut_of(s), ring_of(s))
            if doB:
                main()
                pending_drain = drain
                if j == 1:
                    dma("sp", ESP[:, 0:32 * P], scr[u, 1, :, 0:32 * P], "ESP", r=[f"SCR{u}_1_{i}" for i in range(4)],
                        w=["ESP"])
            if doT:
                pending_finish = tt - 2
        assert pending_drain is None
        if pending_finish is not None:
            tail_finish(u, pending_finish)
        gt[0] += NT
    pg.emit(nc, es)
    es.close()
    return nc


def make_core_inputs(x_chunks, shared):
    return None


def prep_shared(meta_tokens, norm_g, w_in, w_proj_a, w_proj_b, w_out, sink_logit, t5_bias, final_g):
    perm = col_perm()
    w = w_in[0][:, perm]
    wp = np.ascontiguousarray(w.reshape(8, P, WCOLS).transpose(1, 0, 2))
    wa = np.ascontiguousarray(w_proj_a[0].reshape(4, P, D).transpose(1, 0, 2))
    wb = np.ascontiguousarray(w_proj_b[0].reshape(4, P, D).transpose(1, 0, 2))
    wo = np.ascontiguousarray(w_out[0].reshape(8, P, D).transpose(1, 0, 2))
    gcol = np.ascontiguousarray(norm_g[0].reshape(8, P).T)
    fgb = np.ascontiguousarray(np.broadcast_to(final_g[None, :], (P, D)))
    sinkb = np.ascontiguousarray(np.broadcast_to(sink_logit[0][None, :], (P, 8)))
    b15 = np.ascontiguousarray(np.repeat(t5_bias[15], N_META)[:, None])
    ident = np.eye(P, dtype=np.float32).astype(ml_dtypes.bfloat16)
    xmeta = np.ascontiguousarray(np.tile(meta_tokens, (8, 1)))
    maskk = np.zeros((P, 4, P), np.float32)
    for c in range(4):
        for hh in range(2):
            maskk[hh * 64:(hh + 1) * 64, c, (2 * c + hh) * 16:(2 * c + hh + 1) * 16] = 1.0
    maskv = np.zeros((P, 8), np.float32)
    for h in range(8):
        maskv[h * 16:(h + 1) * 16, h] = 1.0
    return dict(wp=wp.astype(np.float32), wa=wa, wb=wb, wo=wo, gcol=gcol, fgb=fgb, sinkb=sinkb, b15=b15,
                ident=ident, xmeta=xmeta, maskk=maskk, maskv=maskv)


def ext_tokens(xseq, start, nq):
    n = xseq.shape[0]
    _, actt = unit_slots(start, nq, n)
    return np.ascontiguousarray(xseq[actt])


def kernel(x_prompt, x_sample, meta_tokens, norm_g, w_in, na_rpb, sink_logit, w_proj_a, w_proj_b, w_out,
           t5_bias, final_g):
    f = lambda a: np.asarray(a, dtype=np.float32)
    x_prompt, x_sample = f(x_prompt), f(x_sample)
    NQ = 32
    UNITS = 3
    shared = prep_shared(f(meta_tokens), f(norm_g), f(w_in), f(w_proj_a), f(w_proj_b), f(w_out), f(sink_logit),
                         f(t5_bias), f(final_g))
    nc = build_program(UNITS, NQ)
    in_maps = []
    for c in range(NCORES):
        ulist = [(x_prompt[c // 4], (c % 4) * 4096), (x_sample[2 * c], 0), (x_sample[2 * c + 1], 0)]
        xe = np.stack([ext_tokens(xs, st, NQ) for xs, st in ulist])
        gna, gwa, sna, swa, smeta = build_bias_inputs(f(na_rpb)[0], f(t5_bias), [(st, xs.shape[0]) for xs, st in ulist], NQ)
        m = dict(shared)
        m.update(xe=xe, gna=gna, gwa=gwa, sna=sna, swa=swa, smeta=smeta)
        in_maps.append(m)
    res = run_bass_kernel_spmd(nc, in_maps, core_ids=list(range(NCORES)))
    yp = np.empty_like(x_prompt)
    ys = np.empty_like(x_sample)
    for c in range(NCORES):
        y = np.asarray(res.results[c]["y"])
        yp[c // 4, (c % 4) * 4096:(c % 4 + 1) * 4096] = y[0]
        ys[2 * c] = y[1]
        ys[2 * c + 1] = y[2]
    return (yp, ys)
```

```python
import math
from contextlib import ExitStack

import numpy as np
import ml_dtypes

import concourse.bass as bass
import concourse.mybir as mybir
from concourse.bass_utils import run_bass_kernel_spmd

F32 = mybir.dt.float32
BF16 = mybir.dt.bfloat16
AF = mybir.ActivationFunctionType
ALU = mybir.AluOpType

D = 1024
P = 128
N_META = 16
GRID_W = 64
NCORES = 8
EPS = 1e-6
NEG = -30000.0

QA, QB, KA, KB, VA, VB, ZA, ZB, GA, GB, WCOLS = 0, 512, 1024, 1536, 1792, 2304, 2432, 2944, 3456, 4480, 5504
WBLOCKS = [(0, 1024), (1024, 1792), (1792, 2432), (2432, 3456), (3456, 4480), (4480, 5504)]


def wblk(col):
    for i, (a, b) in enumerate(WBLOCKS):
        if a <= col < b:
            return i
    raise ValueError(col)


def t5_bucket_np(rel):
    rel = np.asarray(rel, dtype=np.int64)
    half, exact = 16, 8
    ret = np.where(rel > 0, half, 0)
    n = np.abs(rel)
    nf = np.maximum(n, 1).astype(np.float32)
    large = exact + (np.log(nf / np.float32(exact)) / np.float32(math.log(128 / exact))
                     * np.float32(half - exact)).astype(np.int32)
    for nn, b in ((8, 8), (16, 10), (32, 12), (64, 14)):
        large = np.where(n == nn, b, large)
    large = np.minimum(large, half - 1)
    return ret + np.where(n < exact, n, large)


def col_perm():
    offs = np.cumsum([0, 512, 512, 512, 512, 512, 128, 128, 512, 1024, 1024])
    qa, ka, va, za, qb, kb, vb, zb, ga, gb = [np.arange(offs[i], offs[i + 1]) for i in range(10)]
    kbd = np.concatenate([kb[0:64], kb[0:64], kb[64:128], kb[64:128]])
    perm = np.concatenate([qa, qb, ka, kbd, va, vb, za, zb, ga, gb])
    assert perm.shape[0] == WCOLS
    return perm


def unit_slots(start, nq, n):
    nt = nq + 4
    pos = start - 256 + np.arange(nt * P)
    act = np.where(pos < 0, pos + 512, np.where(pos >= n, pos - 512, pos))
    return pos, act


def na_tile(rpb_ext, pos, act, n, start, j, dt):
    R = n // GRID_W
    tq = start + j * P + np.arange(P)
    e = (j + 2 + dt) * P + np.arange(P)
    kpos, kact = pos[e], act[e]
    qr, qc = tq // GRID_W, tq % GRID_W
    kprow = np.floor_divide(kpos, GRID_W)
    kr, kc = kact // GRID_W, kact % GRID_W
    cs = np.clip(qc - 8, 0, GRID_W - 16)
    rs = np.clip(qr - 4, 0, R - 8)
    vis = ((kprow[:, None] >= qr[None, :] - 4) & (kprow[:, None] <= qr[None, :] + 3)
           & (kc[:, None] >= cs[None, :]) & (kc[:, None] < cs[None, :] + 16))
    inwin = (kr[:, None] >= rs[None, :]) & (kr[:, None] < rs[None, :] + 8)
    assert not np.any(vis & ~inwin), "halo wrap produced an out-of-window key"
    rr = np.clip(kr[:, None] - qr[None, :] + 7, 0, 14)
    cc = np.clip(kc[:, None] - qc[None, :] + 15, 0, 30)
    idx = rr * 31 + cc
    nmask = 15 * 31
    idx = np.where(vis, idx, nmask)
    return idx


def wa_tile(pos, n, start, j, b):
    tq = start + j * P + np.arange(P)
    e = (j + 2 + b) * P + np.arange(P)
    kpos = pos[e]
    rel = kpos[:, None] - tq[None, :]
    vis = (np.abs(rel) <= 128) & (kpos[:, None] >= 0) & (kpos[:, None] < n)
    bk = t5_bucket_np(np.clip(rel, -200, 200))
    return np.where(vis, bk, 32)


def meta_tile(start, j):
    tq = start + j * P + np.arange(P)
    m = np.arange(N_META)
    rel = m[:, None] - (N_META + tq[None, :])
    return t5_bucket_np(np.clip(rel, -200, 200))


def build_bias_inputs(na_rpb, t5_bias, units, nq):
    rpb_ext = np.concatenate([na_rpb.reshape(8, 15 * 31), np.full((8, 1), NEG, np.float32)], axis=1)
    t5_ext = np.concatenate([t5_bias, np.full((1, 8), NEG, np.float32)], axis=0)

    def na_vals(idx):
        return np.ascontiguousarray(rpb_ext[:, idx].transpose(1, 0, 2))

    def wa_vals(bk):
        return np.ascontiguousarray(t5_ext[bk].transpose(0, 2, 1))

    big_n = 1 << 20
    gstart = 1 << 16
    gpos, gact = unit_slots(gstart, 8, big_n)
    gna = np.stack([na_vals(na_tile(rpb_ext, gpos, gact, big_n, gstart, 4, dt)) for dt in (-2, -1, 0, 1, 2)], axis=1)
    gwa = np.stack([wa_vals(wa_tile(gpos, big_n, gstart, 4, b)) for b in (-1, 0, 1)], axis=1)
    sna, swa, smeta = [], [], []
    for (start, n) in units:
        pos, act = unit_slots(start, nq, n)
        spec = [(0, -2), (0, -1), (1, -2), (nq - 2, 2), (nq - 1, 1), (nq - 1, 2)]
        sna.append(np.stack([na_vals(na_tile(rpb_ext, pos, act, n, start, j, dt)) for (j, dt) in spec], axis=1))
        swa.append(np.stack([wa_vals(wa_tile(pos, n, start, 0, -1)), wa_vals(wa_tile(pos, n, start, nq - 1, 1))], axis=1))
        mb = meta_tile(start, 0)
        mt = t5_bias[mb]
        smeta.append(np.ascontiguousarray(mt.transpose(2, 0, 1)).reshape(P, P))
    return (gna.astype(np.float32), gwa.astype(np.float32), np.stack(sna).astype(np.float32),
            np.stack(swa).astype(np.float32), np.stack(smeta).astype(np.float32))


class Op:
    __slots__ = ("eng", "fn", "deps", "sig", "sigval", "dma_key", "dma_val")

    def __init__(self, eng, fn):
        self.eng, self.fn = eng, fn
        self.deps = ()
        self.sig = False
        self.sigval = 0
        self.dma_key = None
        self.dma_val = 0


class Prog:
    ENG = ("pe", "act", "dve", "pool", "sp")

    LIMIT = None
    MARKS = []

    def mark(self, name):
        Prog.MARKS.append((name, self.nops))

    def __init__(self):
        self.nops = 0
        self.q = {e: [] for e in self.ENG}
        self.lastw = {}
        self.readers = {}
        self.dma_cnt = {}
        self.out_dmas = []

    def op(self, eng, fn, r=(), w=(), dma=None):
        o = Op(eng, fn)
        self.nops += 1
        if Prog.LIMIT is not None and self.nops > Prog.LIMIT:
            return o
        deps = set()
        for x in r:
            lw = self.lastw.get(x)
            if lw is not None:
                deps.add(lw)
        for x in w:
            lw = self.lastw.get(x)
            if lw is not None:
                deps.add(lw)
            rd = self.readers.get(x)
            if rd:
                deps.update(rd.values())
        o.deps = tuple(deps)
        for x in r:
            rk = eng if dma is None else ("dma", self.nops)
            self.readers.setdefault(x, {})[rk] = o
        for x in w:
            self.lastw[x] = o
            self.readers[x] = {}
        if dma is not None:
            c = self.dma_cnt.get(dma, 0) + 1
            self.dma_cnt[dma] = c
            o.dma_key, o.dma_val = dma, 16 * c
        self.q[eng].append(o)
        return o

    def emit(self, nc, es):
        for e in self.ENG:
            for o in self.q[e]:
                for d in o.deps:
                    if d.dma_key is None and not (d.eng == "pe" and o.eng == "pe" and o.dma_key is None):
                        d.sig = True
        for e in self.ENG:
            c = 0
            for o in self.q[e]:
                if o.sig:
                    c += 1
                    o.sigval = c
        esem = {e: es.enter_context(nc.semaphore("s_" + e)) for e in ("pe", "act", "dve", "pool")}
        dsem = {k: es.enter_context(nc.semaphore("d_" + k)) for k in self.dma_cnt}
        blk = es.enter_context(nc.Block())
        final = [(o.dma_key, o.dma_val) for o in self.out_dmas if o.dma_key is not None]

        def run(e, engobj):
            waited = {}
            for o in self.q[e]:
                for d in o.deps:
                    if d.dma_key is not None:
                        key, val, sem = ("d", d.dma_key), d.dma_val, dsem[d.dma_key]
                    else:
                        if d.eng == "pe" and e == "pe" and o.dma_key is None:
                            continue
                        key, val, sem = ("e", d.eng), d.sigval, esem[d.eng]
                    if waited.get(key, 0) >= val:
                        continue
                    waited[key] = val
                    engobj.wait_ge(sem, val)
                ins = o.fn(engobj)
                if o.dma_key is not None:
                    ins.then_inc(dsem[o.dma_key], 16)
                elif o.sig:
                    ins.then_inc(esem[e], 1)
            if e == "pool":
                fin = {}
                for k, v in final:
                    fin[k] = max(fin.get(k, 0), v)
                for k, v in fin.items():
                    engobj.wait_ge(dsem[k], v)

        blk.tensor(lambda t: run("pe", t))
        blk.scalar(lambda t: run("act", t))
        blk.vector(lambda t: run("dve", t))
        blk.gpsimd(lambda t: run("pool", t))
        blk.sync(lambda t: run("sp", t))


def build_program(units, nq):
    U = units
    NT = nq + 4
    NR = 5
    nc = bass.Bass("TRN2", target_bir_lowering=False)
    dram = lambda n, s, dt=F32, kind="ExternalInput": nc.dram_tensor(n, s, dt, kind=kind)
    xe = dram("xe", [U, NT * P, D])
    wp = dram("wp", [P, 8, WCOLS])
    wa_d = dram("wa", [P, 4, D])
    wb_d = dram("wb", [P, 4, D])
    wo_d = dram("wo", [P, 8, D])
    gcol_d = dram("gcol", [P, 8])
    fgb_d = dram("fgb", [P, D])
    sink_d = dram("sinkb", [P, 8])
    b15_d = dram("b15", [P, 1])
    ident_d = dram("ident", [P, P], BF16)
    xmeta_d = dram("xmeta", [P, D])
    maskk_d = dram("maskk", [P, 4, P])
    maskv_d = dram("maskv", [P, 8])
    gna_d = dram("gna", [P, 5, 8, P])
    gwa_d = dram("gwa", [P, 3, 8, P])
    sna_d = dram("sna", [U, P, 6, 8, P])
    swa_d = dram("swa", [U, P, 2, 8, P])
    smeta_d = dram("smeta", [U, P, P])
    y_d = dram("y", [U, nq * P, D], F32, kind="ExternalOutput")
    scr = nc.dram_tensor("espscr", [U, 2, P, 33 * P], BF16)

    pg = Prog()
    es = ExitStack()
    sb = lambda n, s, dt: es.enter_context(nc.sbuf_tensor(n, s, dt))
    W = sb("W", [P, 8, WCOLS], BF16)
    Wa = sb("Wa", [P, 4, D], BF16)
    Wb = sb("Wb", [P, 4, D], BF16)
    Wo = sb("Wo", [P, 8, D], BF16)
    EBNA = sb("EBNA", [P, 5, 8 * P], BF16)
    EBWA = sb("EBWA", [P, 3, 8 * P], BF16)
    ESP = sb("ESP", [P, 33 * P], BF16)
    BDK = sb("BDK", [P, 8, P], BF16)
    VMA = sb("VMA", [P, 2, 260], BF16)
    VMB = sb("VMB", [P, 2, 260], BF16)
    FGB = sb("FGB", [P, D], F32)
    IDT = sb("IDT", [P, P], BF16)
    CST = sb("CST", [P, 32], F32)
    ST = sb("ST", [P, 32], F32)
    XA = [sb(f"XA{i}", [P, D], F32) for i in range(2)]
    XB = sb("XB", [P, D], F32)
    XN = sb("XN", [P, D], BF16)
    UT = [sb(f"UT{i}", [P, 8, P], BF16) for i in range(3)]
    KAr = [sb(f"KAr{i}", [P, 4, P], BF16) for i in range(NR)]
    KBr = [sb(f"KBr{i}", [P, 2, P], BF16) for i in range(NR)]
    VAr = [sb(f"VAr{i}", [P, 8, 72], BF16) for i in range(NR)]
    VBr = [sb(f"VBr{i}", [P, 2, 72], BF16) for i in range(NR)]
    QEO = sb("QEO", [P, 8, 2, P], BF16)
    SZ = sb("SZ", [P, D], BF16)
    TG = sb("TG", [P, 2 * D], BF16)
    PT = [sb(f"PT{i}", [P, 4 * P], BF16) for i in range(4)]
    PM = sb("PM", [P, 2 * P], BF16)
    G = sb("G", [P, D], BF16)
    GT = sb("GT", [P, 8, P], BF16)
    M1 = sb("M1", [P, 512], F32)
    MSK = M1[:, :].rearrange("p (c k) -> p c k", k=P)
    M2 = sb("M2", [P, 512], F32)
    banks = [es.enter_context(nc.psum_tensor(f"bank{i}", [P, 512], F32)) for i in range(8)]
    bview = [b.bitcast(BF16) for b in banks]

    gctr = [0]
    sctr = [0]

    def nextG():
        i = gctr[0] % 3
        gctr[0] += 1
        return i

    def nextS():
        i = 3 + sctr[0] % 3
        sctr[0] += 1
        return i

    BO = [6, 7]
    flat = lambda ap3: ap3.rearrange("p a b -> p (a b)")

    def dma(eng, out, in_, key, r=(), w=()):
        return pg.op(eng, lambda e: e.dma_start(out=out, in_=in_), r=r, w=w, dma=key)

    def act(out, in_, func, r, w, **kw):
        return pg.op("act", lambda e: e.activation(out=out, in_=in_, func=func, **kw), r=r, w=w)

    dma("sp", IDT[:, :], ident_d[:, :], "IDT", w=["IDT"])
    dma("sp", CST[:, 0:8], gcol_d[:, :], "C0", w=["GCOL"])
    dma("sp", CST[:, 8:16], sink_d[:, :], "C1", w=["SINK"])
    dma("sp", CST[:, 16:17], b15_d[:, :], "C2", w=["B15"])
    dma("sp", CST[:, 17:25], maskv_d[:, :], "C3", w=["MASKV"])
    dma("sp", FGB[:, :], fgb_d[:, :], "FGB", w=["FGB"])
    dma("sp", MSK, maskk_d[:, :, :], "MSK", w=["M1"])
    act(CST[:, 8:16], CST[:, 8:16], AF.Exp, r=["SINK"], w=["SINK"])
    pg.op("dve", lambda e: e.memset(ST[:, 24:25], 1.0), w=["ONE"])

    pg.mark("consts")
    TGf = TG.bitcast(F32)
    SZf = SZ.bitcast(F32)
    f32flat = lambda t: t.bitcast(F32)[:, :, :].rearrange("p a b -> p (a b)")
    big = [dict(ap=XA[0][:, :], res=["XA0"], key="XA0"), dict(ap=XA[1][:, :], res=["XA1"], key="XA1"),
           dict(ap=XB[:, :], res=["XB"], key="XB"),
           dict(ap=TGf[:, :], res=["TG0", "TG1", "TG2", "TG3"], key="STG_TG")]
    small = big + [dict(ap=SZf[:, :], res=["SZ0", "SZ1"], key="STG_SZ"),
                   dict(ap=f32flat(GT), res=["GT"], key="STG_GT"),
                   dict(ap=f32flat(UT[1]), res=["UT1"], key="STG_UT1"),
                   dict(ap=f32flat(UT[2]), res=["UT2"], key="STG_UT2"),
                   dict(ap=M2[:, :], res=["M2"], key="STG_M2")]
    bigc = [0]
    smallc = [0]

    def next_big():
        i = bigc[0] % len(big)
        bigc[0] += 1
        return big[i]

    def next_small():
        i = smallc[0] % len(small)
        smallc[0] += 1
        return small[i]

    castctr = [0]
    dq = [0]

    def cast_block(dst, src_d, res, scale_ap=None, scale_res=None):
        n = dst.shape[-1]
        eng = "act" if castctr[0] % 2 == 0 else "dve"
        castctr[0] += 1
        off = 0
        while off < n:
            u_ = next_small()
            m = min(u_["ap"].shape[-1], n - off)
            st = u_["ap"][:, 0:m]
            dpiece, spiece = dst[:, off:off + m], src_d[:, off:off + m]
            dma("sp", st, spiece, u_["key"], w=u_["res"])
            rr = list(u_["res"]) + ([scale_res] if scale_res else [])
            if eng == "act":
                if scale_ap is None:
                    act(dpiece, st, AF.Copy, r=rr, w=[res])
                else:
                    act(dpiece, st, AF.Copy, r=rr, w=[res], scale=scale_ap)
            else:
                if scale_ap is None:
                    pg.op("dve", lambda e, dpiece=dpiece, st=st: e.tensor_copy(out=dpiece, in_=st), r=rr, w=[res])
                else:
                    pg.op("dve", lambda e, dpiece=dpiece, st=st: e.tensor_scalar(
                        out=dpiece, in0=st, scalar1=scale_ap, scalar2=None, op0=ALU.mult), r=rr, w=[res])
            off += m

    for c in range(8):
        for bi, (a, b) in enumerate(WBLOCKS):
            cast_block(W[:, c, a:b], wp[:, c, a:b], f"W{c}_{bi}", scale_ap=CST[:, c:c + 1], scale_res="GCOL")
    for c in range(4):
        cast_block(Wa[:, c, :], wa_d[:, c, :], f"Wa{c}")
        cast_block(Wb[:, c, :], wb_d[:, c, :], f"Wb{c}")
    for c in range(8):
        cast_block(Wo[:, c, :], wo_d[:, c, :], f"Wo{c}")

    pg.mark("wcast")

    def exp_block(dst, src_d, res):
        n = dst.shape[-1]
        u_ = next_big()
        dma("sp", u_["ap"][:, 0:n], src_d, u_["key"], w=u_["res"])
        act(dst, u_["ap"][:, 0:n], AF.Exp, r=u_["res"], w=[res])

    for i in range(5):
        exp_block(EBNA[:, i, :], gna_d[:, i, :, :].rearrange("p h q -> p (h q)"), f"EBNA{i}")
    for i in range(3):
        exp_block(EBWA[:, i, :], gwa_d[:, i, :, :].rearrange("p h q -> p (h q)"), f"EBWA{i}")

    pg.mark("ebgen")
    tb = [(XN, "XN"), (G, "G")]
    tbc = [0]
    for u in range(U):
        pieces = [(0, 0, sna_d[u, :, 0, :, :]), (0, 1, sna_d[u, :, 1, :, :]), (0, 2, sna_d[u, :, 2, :, :]),
                  (0, 3, swa_d[u, :, 0, :, :]),
                  (1, 0, sna_d[u, :, 3, :, :]), (1, 1, sna_d[u, :, 4, :, :]), (1, 2, sna_d[u, :, 5, :, :]),
                  (1, 3, swa_d[u, :, 1, :, :])]
        for (side, pi, src) in pieces:
            tbuf, tres = tb[tbc[0] % 2]
            tbc[0] += 1
            exp_block(tbuf[:, :], src.rearrange("p h q -> p (h q)"), tres)
            dma("sp", scr[u, side, :, pi * 1024:(pi + 1) * 1024], tbuf[:, :], "SCRW" + tres, r=[tres],
                w=[f"SCR{u}_{side}_{pi}"])
        tbuf, tres = tb[tbc[0] % 2]
        tbc[0] += 1
        exp_block(tbuf[:, 0:P], smeta_d[u, :, :], tres)
        dma("sp", scr[u, 0, :, 4096:4096 + P], tbuf[:, 0:P], "SCRW" + tres, r=[tres], w=[f"SCR{u}_0_4"])

    pg.mark("scratch")

    def a_pre(xbuf, xres, ss_col):
        ss, rms, rstd = ST[:, ss_col:ss_col + 1], ST[:, ss_col + 1:ss_col + 2], ST[:, ss_col + 2:ss_col + 3]
        nm = f"ST{ss_col}"
        act(XN[:, :], xbuf[:, :], AF.Square, r=[xres], w=["XN", nm + "a"], accum_out=ss)
        act(rms, ss, AF.Sqrt, r=[nm + "a", "EPS"], w=[nm + "b"], scale=1.0 / D, bias=EPS_AP[0])
        pg.op("dve", lambda e: e.reciprocal(out=rstd, in_=rms), r=[nm + "b"], w=[nm + "c"])
        act(XN[:, :], xbuf[:, :], AF.Copy, r=[xres, nm + "c"], w=["XN"], scale=rstd)

    EPS_AP = [None]

    def a_tr(uslot):
        g = nextG()
        for c in range(8):
            pg.op("pe", lambda e, c=c: e.transpose(out=bview[g][:, c * P:(c + 1) * P], in_=XN[:, c * P:(c + 1) * P],
                                                    identity=IDT[:, :]), r=["XN", "IDT"], w=[f"B{g}"])
        pg.op("dve", lambda e: e.tensor_copy(out=flat(UT[uslot][:, :, :]), in_=bview[g][:, 0:D]),
              r=[f"B{g}"], w=[f"UT{uslot}"])

    def fm_proj(g, off, uslot, col):
        for c in range(8):
            pg.op("pe", lambda e, c=c: e.matmul(banks[g][:, off * P:(off + 1) * P], lhsT=W[:, c, col:col + P],
                                                 rhs=UT[uslot][:, c, :], start=(c == 0), stop=(c == 7)),
                  r=[f"W{c}_{wblk(col)}", f"UT{uslot}"], w=[f"B{g}"])

    def tm_proj(g, boff, n, uslot, col):
        for c in range(8):
            pg.op("pe", lambda e, c=c: e.matmul(banks[g][:, boff:boff + n], lhsT=UT[uslot][:, c, :],
                                                 rhs=W[:, c, col:col + n], start=(c == 0), stop=(c == 7)),
                  r=[f"W{c}_{wblk(col)}", f"UT{uslot}"], w=[f"B{g}"])

    def a_kv(uslot, rs):
        g1 = nextG()
        for f in range(4):
            fm_proj(g1, f, uslot, KA + f * P)
        act(flat(KAr[rs][:, :, :]), banks[g1][:, :], AF.Copy, r=[f"B{g1}"], w=[f"KA{rs}"])
        g2 = nextG()
        for f in range(2):
            fm_proj(g2, f, uslot, KB + f * P)
        tm_proj(g2, 256, 128, uslot, VB)
        act(flat(KBr[rs][:, :, :]), banks[g2][:, 0:256], AF.Copy, r=[f"B{g2}"], w=[f"KB{rs}"])
        act(VBr[rs][:, :, 0:64], banks[g2][:, 256:384].rearrange("p (h e) -> p h e", e=64), AF.Copy,
            r=[f"B{g2}"], w=[f"VB{rs}"])
        g3 = nextG()
        tm_proj(g3, 0, 512, uslot, VA)
        pg.op("dve", lambda e: e.tensor_copy(out=VAr[rs][:, :, 0:64],
                                             in_=banks[g3][:, :].rearrange("p (h e) -> p h e", e=64)),
              r=[f"B{g3}"], w=[f"VA{rs}"])

    def qzg_phases(uslot):
        ph = []

        def q1():
            g = nextG()
            for f in range(4):
                fm_proj(g, f, uslot, QA + f * P)
            act(QEO[0:64, 0:4, 0, :], banks[g][0:64, :].rearrange("p (c q) -> p c q", q=P), AF.Copy,
                r=[f"B{g}"], w=["QA"])
            act(QEO[64:128, 0:4, 1, :], banks[g][64:128, :].rearrange("p (c q) -> p c q", q=P), AF.Copy,
                r=[f"B{g}"], w=["QA"])

        def q2():
            g = nextG()
            for f in range(4):
                fm_proj(g, f, uslot, QB + f * P)
            pg.op("dve", lambda e, g=g: e.tensor_copy(
                out=QEO[0:64, 4:8, 0, :], in_=banks[g][0:64, :].rearrange("p (c q) -> p c q", q=P)),
                r=[f"B{g}"], w=["QB"])
            pg.op("dve", lambda e, g=g: e.tensor_copy(
                out=QEO[64:128, 4:8, 1, :], in_=banks[g][64:128, :].rearrange("p (c q) -> p c q", q=P)),
                r=[f"B{g}"], w=["QB"])

        def zk(k, col):
            g = nextG()
            tm_proj(g, 0, 512, uslot, col)
            act(SZ[:, k * 512:(k + 1) * 512], banks[g][:, :], AF.Tanh, r=[f"B{g}"], w=[f"SZ{k}"], scale=0.5)
            pg.op("dve", lambda e, g=g, k=k: e.scalar_tensor_tensor(
                out=SZ[:, k * 512:(k + 1) * 512], in0=SZ[:, k * 512:(k + 1) * 512], scalar=1.0, in1=banks[g][:, :],
                op0=ALU.add, op1=ALU.mult), r=[f"B{g}", f"SZ{k}"], w=[f"SZ{k}"])

        def gk(k):
            g = nextG()
            tm_proj(g, 0, 512, uslot, GA + k * 512)
            act(TG[:, k * 512:(k + 1) * 512], banks[g][:, :], AF.Tanh, r=[f"B{g}"], w=[f"TG{k}"], scale=0.5)

        ph.append(q1)
        ph.append(q2)
        ph.append(lambda: zk(0, ZA))
        ph.append(lambda: zk(1, ZB))
        for k in range(4):
            ph.append(lambda k=k: gk(k))
        return ph

    ptc = [0]
    LOOK = 3

    def attention(rs_of, j, nqv):
        def meta():
            s0 = nextS()
            for br in range(2):
                for c in range(4):
                    for qi in range(2):
                        pg.op("pe", lambda e, br=br, c=c, qi=qi: e.matmul(
                            banks[s0][:, br * P:(br + 1) * P], lhsT=BDK[:, br * 4 + c, :],
                            rhs=QEO[:, br * 4 + c, qi, :],
                            start=(c == 0 and qi == 0), stop=(c == 3 and qi == 1)),
                            r=["BDK", "QA" if br == 0 else "QB"], w=[f"B{s0}"])
            act(PM[:, 0:P], banks[s0][:, 0:P], AF.Exp, r=[f"B{s0}"], w=["PMA"], scale=0.125)
            if j == 0:
                act(PM[:, P:2 * P], banks[s0][:, P:2 * P], AF.Exp, r=[f"B{s0}"], w=["PMB"], scale=0.125)
                pg.op("dve", lambda e: e.tensor_tensor(out=PM[:, P:2 * P], in0=PM[:, P:2 * P],
                                                       in1=ESP[:, 32 * P:33 * P], op=ALU.mult),
                      r=["PMB", "ESP"], w=["PMB"])
            else:
                act(PM[:, P:2 * P], banks[s0][:, P:2 * P], AF.Exp, r=[f"B{s0}", "B15"], w=["PMB"], scale=0.125,
                    bias=CST[:, 16:17])

        jobs = []
        for br in range(2):
            dts = (-2, -1, 0, 1, 2) if br == 0 else (-1, 0, 1)
            for hg in range(2):
                for di, dt in enumerate(dts):
                    jobs.append(dict(br=br, hg=hg, dt=dt, first=(di == 0), last=(di == len(dts) - 1)))

        def emit_qk(jb):
            br, hg, dt = jb["br"], jb["hg"], jb["dt"]
            rs = rs_of(dt)
            sbk = nextS()
            jb["rs"], jb["sbk"] = rs, sbk
            if br == 0:
                for pp in range(2):
                    c = 2 * hg + pp
                    pg.op("pe", lambda e, sbk=sbk, pp=pp, c=c, rs=rs: e.matmul(
                        banks[sbk][:, pp * 256:(pp + 1) * 256], lhsT=KAr[rs][:, c, :],
                        rhs=QEO[:, c, :, :].rearrange("p a q -> p (a q)"), start=True, stop=True),
                        r=[f"KA{rs}", "QA"], w=[f"B{sbk}"])
            else:
                pg.op("pe", lambda e, sbk=sbk, hg=hg, rs=rs: e.matmul(
                    banks[sbk][:, :], lhsT=KBr[rs][:, hg, :],
                    rhs=QEO[:, 4 + 2 * hg:6 + 2 * hg, :, :].rearrange("p c a q -> p (c a q)"), start=True, stop=True),
                    r=[f"KB{rs}", "QB"], w=[f"B{sbk}"])

        def emit_sm(jb):
            br, hg, dt, sbk = jb["br"], jb["hg"], jb["dt"], jb["sbk"]
            pb = ptc[0] % 4
            ptc[0] += 1
            jb["pb"] = pb
            act(PT[pb][:, :], banks[sbk][:, :], AF.Exp, r=[f"B{sbk}"], w=[f"PT{pb}"], scale=0.125)
            if br == 0:
                base = None
                if j == 0 and dt == -2:
                    base = 0
                elif j == 0 and dt == -1:
                    base = 8
                elif j == 1 and dt == -2:
                    base = 16
                elif j == nqv - 2 and dt == 2:
                    base = 0
                elif j == nqv - 1 and dt == 1:
                    base = 8
                elif j == nqv - 1 and dt == 2:
                    base = 16
                if base is None:
                    eb, ebres = EBNA[:, dt + 2, hg * 512:(hg + 1) * 512], f"EBNA{dt + 2}"
                else:
                    eb, ebres = ESP[:, (base + 4 * hg) * P:(base + 4 * hg + 4) * P], "ESP"
            else:
                if (j == 0 and dt == -1) or (j == nqv - 1 and dt == 1):
                    eb, ebres = ESP[:, (24 + 4 * hg) * P:(24 + 4 * hg + 4) * P], "ESP"
                else:
                    eb, ebres = EBWA[:, dt + 1, hg * 512:(hg + 1) * 512], f"EBWA{dt + 1}"
            pg.op("dve", lambda e, pb=pb, eb=eb: e.tensor_tensor(out=PT[pb][:, :], in0=PT[pb][:, :], in1=eb,
                                                                  op=ALU.mult),
                  r=[f"PT{pb}", ebres], w=[f"PT{pb}"])

        def emit_pv(jb):
            br, hg, rs, pb = jb["br"], jb["hg"], jb["rs"], jb["pb"]
            ob = BO[hg]
            if jb["first"]:
                VM = VMA if br == 0 else VMB
                pg.op("pe", lambda e, ob=ob, VM=VM, hg=hg, br=br: e.matmul(
                    banks[ob][:, 0:260], lhsT=PM[:, br * P:(br + 1) * P], rhs=VM[:, hg, :], start=True, stop=False,
                    skip_group_check=True),
                    r=["PMA" if br == 0 else "PMB", "VM"], w=[f"B{ob}"])
            last = jb["last"]
            for hl in range(4):
                h = 4 * hg + hl
                rhs = VAr[rs][:, h, 0:65] if br == 0 else VBr[rs][:, h // 4, 0:65]
                pg.op("pe", lambda e, ob=ob, hl=hl, pb=pb, rhs=rhs, last=last: e.matmul(
                    banks[ob][:, hl * 65:(hl + 1) * 65], lhsT=PT[pb][:, hl * P:(hl + 1) * P], rhs=rhs,
                    start=False, stop=last, skip_group_check=True),
                    r=[f"PT{pb}", (f"VA{rs}" if br == 0 else f"VB{rs}")], w=[f"B{ob}"])
            if not last:
                return
            ov = banks[ob][:, 0:260].rearrange("p (h e) -> p h e", e=65)
            dcol = 8 + br * 8 + hg * 4
            den = ST[:, dcol:dcol + 4]
            dres = f"DEN{br}{hg}"
            if br == 0:
                pg.op("dve", lambda e, ov=ov, den=den: e.reciprocal(out=den, in_=ov[:, :, 64]),
                      r=[f"B{ob}"], w=[dres])
            else:
                pg.op("dve", lambda e, ov=ov, den=den, hg=hg: e.tensor_tensor(
                    out=den, in0=ov[:, :, 64], in1=CST[:, 8 + 4 * hg:12 + 4 * hg], op=ALU.add),
                    r=[f"B{ob}", "SINK"], w=[dres])
                pg.op("dve", lambda e, den=den: e.reciprocal(out=den, in_=den), r=[dres], w=[dres])
            o0 = br * 512 + hg * 256
            szv = SZ[:, o0:o0 + 256].rearrange("p (h e) -> p h e", e=64)
            pg.op("pool", lambda e, szv=szv, den=den: e.tensor_tensor(
                out=szv, in0=szv, in1=den.unsqueeze(2).broadcast_to([P, 4, 64]), op=ALU.mult),
                r=[dres, f"SZ{br}"], w=[f"SZ{br}"])
            gv = G[:, o0:o0 + 256].rearrange("p (h e) -> p h e", e=64)
            pg.op("dve", lambda e, gv=gv, ov=ov, szv=szv: e.tensor_tensor(out=gv, in0=ov[:, :, 0:64], in1=szv,
                                                                          op=ALU.mult),
                  r=[f"B{ob}", f"SZ{br}"], w=["G"])

        n = len(jobs)

        def pre():
            meta()
            for i in range(LOOK):
                emit_qk(jobs[i])
                emit_sm(jobs[i])

        def main():
            for i in range(n - LOOK):
                emit_qk(jobs[i + LOOK])
                emit_sm(jobs[i + LOOK])
                emit_pv(jobs[i])

        def drain():
            for i in range(n - LOOK, n):
                emit_pv(jobs[i])

        return pre, main, drain

    def tail_phases(u, j):
        ph = []

        def p1():
            g = nextG()
            for c in range(8):
                pg.op("pe", lambda e, c=c, g=g: e.transpose(out=bview[g][:, c * P:(c + 1) * P],
                                                             in_=G[:, c * P:(c + 1) * P], identity=IDT[:, :]),
                      r=["G", "IDT"], w=[f"B{g}"])
            act(flat(GT[:, :, :]), bview[g][:, 0:D], AF.Copy, r=[f"B{g}"], w=["GT"])

        def ph_y(half):
            ga = nextG()
            for c in range(4):
                pg.op("pe", lambda e, c=c, ga=ga, half=half: e.matmul(
                    banks[ga][:, :], lhsT=GT[:, c, :], rhs=Wa[:, c, half * 512:(half + 1) * 512],
                    start=(c == 0), stop=(c == 3)), r=["GT", f"Wa{c}"], w=[f"B{ga}"])
            pg.op("dve", lambda e, ga=ga, half=half: e.scalar_tensor_tensor(
                out=M1[:, :], in0=TG[:, half * 512:(half + 1) * 512], scalar=1.0, in1=banks[ga][:, :],
                op0=ALU.add, op1=ALU.mult), r=[f"B{ga}", f"TG{half}"], w=["M1"])
            gb = nextG()
            for c in range(4):
                pg.op("pe", lambda e, c=c, gb=gb, half=half: e.matmul(
                    banks[gb][:, :], lhsT=GT[:, 4 + c, :], rhs=Wb[:, c, half * 512:(half + 1) * 512],
                    start=(c == 0), stop=(c == 3)), r=["GT", f"Wb{c}"], w=[f"B{gb}"])
            pg.op("dve", lambda e, gb=gb, half=half: e.scalar_tensor_tensor(
                out=M2[:, :], in0=TG[:, D + half * 512:D + (half + 1) * 512], scalar=1.0, in1=banks[gb][:, :],
                op0=ALU.add, op1=ALU.mult), r=[f"B{gb}", f"TG{2 + half}"], w=["M2"])
            pg.op("dve", lambda e, half=half: e.tensor_tensor(out=G[:, half * 512:(half + 1) * 512], in0=M1[:, :],
                                                              in1=M2[:, :], op=ALU.add),
                  r=["M1", "M2"], w=["G"])

        def p4():
            g = nextG()
            for c in range(8):
                pg.op("pe", lambda e, c=c, g=g: e.transpose(out=bview[g][:, c * P:(c + 1) * P],
                                                             in_=G[:, c * P:(c + 1) * P], identity=IDT[:, :]),
                      r=["G", "IDT"], w=[f"B{g}"])
            act(flat(GT[:, :, :]), bview[g][:, 0:D], AF.Copy, r=[f"B{g}"], w=["GT"])

        def ph_o(half):
            go = nextG()
            for c in range(8):
                pg.op("pe", lambda e, c=c, go=go, half=half: e.matmul(
                    banks[go][:, :], lhsT=GT[:, c, :], rhs=Wo[:, c, half * 512:(half + 1) * 512],
                    start=(c == 0), stop=(c == 7)), r=["GT", f"Wo{c}"], w=[f"B{go}"])
            pg.op("dve", lambda e, go=go, half=half: e.scalar_tensor_tensor(
                out=XB[:, half * 512:(half + 1) * 512], in0=banks[go][:, :], scalar=0.25,
                in1=XB[:, half * 512:(half + 1) * 512], op0=ALU.mult, op1=ALU.add), r=[f"B{go}", "XB"], w=["XB"])

        ph.append(p1)
        ph.append(lambda: ph_y(0))
        ph.append(lambda: ph_y(1))
        ph.append(p4)
        ph.append(lambda: ph_o(0))
        ph.append(lambda: ph_o(1))
        return ph

    def tail_finish(u, j):
        ss, rms, rstd = ST[:, 4:5], ST[:, 5:6], ST[:, 6:7]
        act(flat(GT[:, :, :]), XB[:, :], AF.Square, r=["XB"], w=["GT", "ST4a"], accum_out=ss)
        act(rms, ss, AF.Sqrt, r=["ST4a", "EPS"], w=["ST4b"], scale=1.0 / D, bias=EPS_AP[0])
        pg.op("dve", lambda e: e.reciprocal(out=rstd, in_=rms), r=["ST4b"], w=["ST4c"])
        pg.op("dve", lambda e: e.scalar_tensor_tensor(out=XB[:, :], in0=XB[:, :], scalar=rstd, in1=FGB[:, :],
                                                      op0=ALU.mult, op1=ALU.mult), r=["XB", "ST4c", "FGB"], w=["XB"])
        o = dma("pool", y_d[u, j * P:(j + 1) * P, :], XB[:, :], "OUT", r=["XB"], w=["YOUT"])
        pg.out_dmas.append(o)

    pg.op("dve", lambda e: e.memset(ST[:, 25:26], EPS), w=["EPS"])
    EPS_AP[0] = ST[:, 25:26]
    _orig_act = act

    pg.op("pool", lambda e: e.memset(QEO[:, :, :, :].rearrange("p a b c -> p (a b c)"), 0.0), w=["QA", "QB"])
    for i in range(NR):
        pg.op("pool", lambda e, i=i: e.memset(VAr[i][:, :, 64:65], 1.0), w=[f"VA{i}"])
        pg.op("pool", lambda e, i=i: e.memset(VBr[i][:, :, 64:65], 1.0), w=[f"VB{i}"])

    pg.mark("onescols")
    dma("sp", XA[0][:, :], xmeta_d[:, :], "XA0", w=["XA0"])
    a_pre(XA[0], "XA0", 0)
    pg.mark("meta_pre")
    a_tr(0)
    pg.mark("meta_tr")
    gm = nextG()
    for f in range(4):
        fm_proj(gm, f, 0, KA + f * P)
    for c in range(4):
        pg.op("dve", lambda e, c=c: e.tensor_tensor(out=BDK[:, c, :], in0=banks[gm][:, c * P:(c + 1) * P],
                                                    in1=MSK[:, c, :], op=ALU.mult), r=[f"B{gm}", "M1"], w=["BDK"])
    gm2 = nextG()
    for f in range(2):
        fm_proj(gm2, f, 0, KB + f * P)
    tm_proj(gm2, 256, 128, 0, VB)
    for c in range(4):
        pg.op("dve", lambda e, c=c: e.tensor_tensor(out=BDK[:, 4 + c, :], in0=banks[gm2][:, (c // 2) * P:(c // 2 + 1) * P],
                                                    in1=MSK[:, c, :], op=ALU.mult), r=[f"B{gm2}", "M1"], w=["BDK"])
    gm3 = nextG()
    tm_proj(gm3, 0, 512, 0, VA)
    for h in range(8):
        hg, hl = h // 4, h % 4
        pg.op("dve", lambda e, h=h, hg=hg, hl=hl: e.tensor_scalar(
            out=VMA[:, hg, hl * 65:hl * 65 + 64], in0=banks[gm3][:, h * 64:(h + 1) * 64], scalar1=CST[:, 17 + h:18 + h],
            scalar2=None, op0=ALU.mult), r=[f"B{gm3}", "MASKV"], w=["VM"])
        pg.op("dve", lambda e, h=h, hg=hg, hl=hl: e.tensor_scalar(
            out=VMB[:, hg, hl * 65:hl * 65 + 64], in0=banks[gm2][:, 256 + hg * 64:256 + (hg + 1) * 64],
            scalar1=CST[:, 17 + h:18 + h], scalar2=None, op0=ALU.mult), r=[f"B{gm2}", "MASKV"], w=["VM"])
        pg.op("dve", lambda e, h=h, hg=hg, hl=hl: e.tensor_copy(out=VMA[:, hg, hl * 65 + 64:hl * 65 + 65],
                                                                in_=CST[:, 17 + h:18 + h]), r=["MASKV"], w=["VM"])
        pg.op("dve", lambda e, h=h, hg=hg, hl=hl: e.tensor_copy(out=VMB[:, hg, hl * 65 + 64:hl * 65 + 65],
                                                                in_=CST[:, 17 + h:18 + h]), r=["MASKV"], w=["VM"])

    pg.mark("meta_done")
    gt = [0]
    for u in range(U):
        dma("sp", ESP[:, :], scr[u, 0, :, :], "ESP", r=[f"SCR{u}_0_{i}" for i in range(5)], w=["ESP"])
        base = gt[0]
        xa_of = lambda s: (base + s) % 2
        ring_of = lambda t: (base + t) % NR
        ut_of = lambda t: (base + t) % 3
        dma("sp", XA[xa_of(0)][:, :], xe[u, 0:P, :], f"XA{xa_of(0)}", w=[f"XA{xa_of(0)}"])
        pending_finish = None
        pending_drain = None
        for s in range(NT + 3):
            tq = s - 2
            tt = s - 3
            doB = 2 <= tq <= NT - 3
            doT = 2 <= tt <= NT - 3
            if pending_finish is not None:
                tail_finish(u, pending_finish)
                pending_finish = None
            if s + 1 < NT:
                k = xa_of(s + 1)
                dma("sp", XA[k][:, :], xe[u, (s + 1) * P:(s + 2) * P, :], f"XA{k}", w=[f"XA{k}"])
            if doT:
                dma("sp", XB[:, :], xe[u, tt * P:(tt + 1) * P, :], "XB", w=["XB"])
            if s < NT:
                a_pre(XA[xa_of(s)], f"XA{xa_of(s)}", 0)
            tp = tail_phases(u, tt - 2) if doT else []
            qp = qzg_phases(ut_of(tq)) if doB else []
            def run(lst, i):
                if i < len(lst):
                    lst[i]()
            run(qp, 0)
            run(qp, 1)
            if pending_drain is not None:
                pending_drain()
                pending_drain = None
            run(qp, 2)
            run(qp, 3)
            run(tp, 0)
            if s < NT:
                a_tr(ut_of(s))
            run(tp, 1)
            run(qp, 4)
            run(tp, 2)
            run(qp, 5)
            run(tp, 3)
            run(qp, 6)
            run(tp, 4)
            run(qp, 7)
            run(tp, 5)
            if doB:
                j = tq - 2
                pre, main, drain = attention(lambda dt, tq=tq: ring_of(tq + dt), j, nq)
                pre()
            if s < NT:
                a_kv(ut_of(s), ring_of(s))
            if doB:
                main()
                pending_drain = drain
                if j == 1:
                    dma("sp", ESP[:, 0:32 * P], scr[u, 1, :, 0:32 * P], "ESP", r=[f"SCR{u}_1_{i}" for i in range(4)],
                        w=["ESP"])
            if doT:
                pending_finish = tt - 2
        assert pending_drain is None
        if pending_finish is not None:
            tail_finish(u, pending_finish)
        gt[0] += NT
    pg.emit(nc, es)
    es.close()
    return nc


def make_core_inputs(x_chunks, shared):
    return None


def prep_shared(meta_tokens, norm_g, w_in, w_proj_a, w_proj_b, w_out, sink_logit, t5_bias, final_g):
    perm = col_perm()
    w = w_in[0][:, perm]
    wp = np.ascontiguousarray(w.reshape(8, P, WCOLS).transpose(1, 0, 2))
    wa = np.ascontiguousarray(w_proj_a[0].reshape(4, P, D).transpose(1, 0, 2))
    wb = np.ascontiguousarray(w_proj_b[0].reshape(4, P, D).transpose(1, 0, 2))
    wo = np.ascontiguousarray(w_out[0].reshape(8, P, D).transpose(1, 0, 2))
    gcol = np.ascontiguousarray(norm_g[0].reshape(8, P).T)
    fgb = np.ascontiguousarray(np.broadcast_to(final_g[None, :], (P, D)))
    sinkb = np.ascontiguousarray(np.broadcast_to(sink_logit[0][None, :], (P, 8)))
    b15 = np.ascontiguousarray(np.repeat(t5_bias[15], N_META)[:, None])
    ident = np.eye(P, dtype=np.float32).astype(ml_dtypes.bfloat16)
    xmeta = np.ascontiguousarray(np.tile(meta_tokens, (8, 1)))
    maskk = np.zeros((P, 4, P), np.float32)
    for c in range(4):
        for hh in range(2):
            maskk[hh * 64:(hh + 1) * 64, c, (2 * c + hh) * 16:(2 * c + hh + 1) * 16] = 1.0
    maskv = np.zeros((P, 8), np.float32)
    for h in range(8):
        maskv[h * 16:(h + 1) * 16, h] = 1.0
    return dict(wp=wp.astype(np.float32), wa=wa, wb=wb, wo=wo, gcol=gcol, fgb=fgb, sinkb=sinkb, b15=b15,
                ident=ident, xmeta=xmeta, maskk=maskk, maskv=maskv)


def ext_tokens(xseq, start, nq):
    n = xseq.shape[0]
    _, actt = unit_slots(start, nq, n)
    return np.ascontiguousarray(xseq[actt])


def kernel(x_prompt, x_sample, meta_tokens, norm_g, w_in, na_rpb, sink_logit, w_proj_a, w_proj_b, w_out,
           t5_bias, final_g):
    f = lambda a: np.asarray(a, dtype=np.float32)
    x_prompt, x_sample = f(x_prompt), f(x_sample)
    NQ = 32
    UNITS = 3
    shared = prep_shared(f(meta_tokens), f(norm_g), f(w_in), f(w_proj_a), f(w_proj_b), f(w_out), f(sink_logit),
                         f(t5_bias), f(final_g))
    nc = build_program(UNITS, NQ)
    in_maps = []
    for c in range(NCORES):
        ulist = [(x_prompt[c // 4], (c % 4) * 4096), (x_sample[2 * c], 0), (x_sample[2 * c + 1], 0)]
        xe = np.stack([ext_tokens(xs, st, NQ) for xs, st in ulist])
        gna, gwa, sna, swa, smeta = build_bias_inputs(f(na_rpb)[0], f(t5_bias), [(st, xs.shape[0]) for xs, st in ulist], NQ)
        m = dict(shared)
        m.update(xe=xe, gna=gna, gwa=gwa, sna=sna, swa=swa, smeta=smeta)
        in_maps.append(m)
    res = run_bass_kernel_spmd(nc, in_maps, core_ids=list(range(NCORES)))
    yp = np.empty_like(x_prompt)
    ys = np.empty_like(x_sample)
    for c in range(NCORES):
        y = np.asarray(res.results[c]["y"])
        yp[c // 4, (c % 4) * 4096:(c % 4 + 1) * 4096] = y[0]
        ys[2 * c] = y[1]
        ys[2 * c + 1] = y[2]
    return (yp, ys)
```

```python
import math
from contextlib import ExitStack

import numpy as np
import ml_dtypes

import concourse.bass as bass
import concourse.mybir as mybir
from concourse.bass_utils import run_bass_kernel_spmd

F32 = mybir.dt.float32
BF16 = mybir.dt.bfloat16
AF = mybir.ActivationFunctionType
ALU = mybir.AluOpType

D = 1024
P = 128
N_META = 16
GRID_W = 64
NCORES = 8
EPS = 1e-6
NEG = -30000.0

QA, QB, KA, KB, VA, VB, ZA, ZB, GA, GB, WCOLS = 0, 512, 1024, 1536, 1792, 2304, 2432, 2944, 3456, 4480, 5504
WBLOCKS = [(0, 1024), (1024, 1792), (1792, 2432), (2432, 3456), (3456, 4480), (4480, 5504)]


def wblk(col):
    for i, (a, b) in enumerate(WBLOCKS):
        if a <= col < b:
            return i
    raise ValueError(col)


def t5_bucket_np(rel):
    rel = np.asarray(rel, dtype=np.int64)
    half, exact = 16, 8
    ret = np.where(rel > 0, half, 0)
    n = np.abs(rel)
    nf = np.maximum(n, 1).astype(np.float32)
    large = exact + (np.log(nf / np.float32(exact)) / np.float32(math.log(128 / exact))
                     * np.float32(half - exact)).astype(np.int32)
    for nn, b in ((8, 8), (16, 10), (32, 12), (64, 14)):
        large = np.where(n == nn, b, large)
    large = np.minimum(large, half - 1)
    return ret + np.where(n < exact, n, large)


def col_perm():
    offs = np.cumsum([0, 512, 512, 512, 512, 512, 128, 128, 512, 1024, 1024])
    qa, ka, va, za, qb, kb, vb, zb, ga, gb = [np.arange(offs[i], offs[i + 1]) for i in range(10)]
    kbd = np.concatenate([kb[0:64], kb[0:64], kb[64:128], kb[64:128]])
    perm = np.concatenate([qa, qb, ka, kbd, va, vb, za, zb, ga, gb])
    assert perm.shape[0] == WCOLS
    return perm


def unit_slots(start, nq, n):
    nt = nq + 4
    pos = start - 256 + np.arange(nt * P)
    act = np.where(pos < 0, pos + 512, np.where(pos >= n, pos - 512, pos))
    return pos, act


def na_tile(rpb_ext, pos, act, n, start, j, dt):
    R = n // GRID_W
    tq = start + j * P + np.arange(P)
    e = (j + 2 + dt) * P + np.arange(P)
    kpos, kact = pos[e], act[e]
    qr, qc = tq // GRID_W, tq % GRID_W
    kprow = np.floor_divide(kpos, GRID_W)
    kr, kc = kact // GRID_W, kact % GRID_W
    cs = np.clip(qc - 8, 0, GRID_W - 16)
    rs = np.clip(qr - 4, 0, R - 8)
    vis = ((kprow[:, None] >= qr[None, :] - 4) & (kprow[:, None] <= qr[None, :] + 3)
           & (kc[:, None] >= cs[None, :]) & (kc[:, None] < cs[None, :] + 16))
    inwin = (kr[:, None] >= rs[None, :]) & (kr[:, None] < rs[None, :] + 8)
    assert not np.any(vis & ~inwin), "halo wrap produced an out-of-window key"
    rr = np.clip(kr[:, None] - qr[None, :] + 7, 0, 14)
    cc = np.clip(kc[:, None] - qc[None, :] + 15, 0, 30)
    idx = rr * 31 + cc
    nmask = 15 * 31
    idx = np.where(vis, idx, nmask)
    return idx


def wa_tile(pos, n, start, j, b):
    tq = start + j * P + np.arange(P)
    e = (j + 2 + b) * P + np.arange(P)
    kpos = pos[e]
    rel = kpos[:, None] - tq[None, :]
    vis = (np.abs(rel) <= 128) & (kpos[:, None] >= 0) & (kpos[:, None] < n)
    bk = t5_bucket_np(np.clip(rel, -200, 200))
    return np.where(vis, bk, 32)


def meta_tile(start, j):
    tq = start + j * P + np.arange(P)
    m = np.arange(N_META)
    rel = m[:, None] - (N_META + tq[None, :])
    return t5_bucket_np(np.clip(rel, -200, 200))


def build_bias_inputs(na_rpb, t5_bias, units, nq):
    rpb_ext = np.concatenate([na_rpb.reshape(8, 15 * 31), np.full((8, 1), NEG, np.float32)], axis=1)
    t5_ext = np.concatenate([t5_bias, np.full((1, 8), NEG, np.float32)], axis=0)

    def na_vals(idx):
        return np.ascontiguousarray(rpb_ext[:, idx].transpose(1, 0, 2))

    def wa_vals(bk):
        return np.ascontiguousarray(t5_ext[bk].transpose(0, 2, 1))

    big_n = 1 << 20
    gstart = 1 << 16
    gpos, gact = unit_slots(gstart, 8, big_n)
    gna = np.stack([na_vals(na_tile(rpb_ext, gpos, gact, big_n, gstart, 4, dt)) for dt in (-2, -1, 0, 1, 2)], axis=1)
    gwa = np.stack([wa_vals(wa_tile(gpos, big_n, gstart, 4, b)) for b in (-1, 0, 1)], axis=1)
    sna, swa, smeta = [], [], []
    for (start, n) in units:
        pos, act = unit_slots(start, nq, n)
        spec = [(0, -2), (0, -1), (1, -2), (nq - 2, 2), (nq - 1, 1), (nq - 1, 2)]
        sna.append(np.stack([na_vals(na_tile(rpb_ext, pos, act, n, start, j, dt)) for (j, dt) in spec], axis=1))
        swa.append(np.stack([wa_vals(wa_tile(pos, n, start, 0, -1)), wa_vals(wa_tile(pos, n, start, nq - 1, 1))], axis=1))
        mb = meta_tile(start, 0)
        mt = t5_bias[mb]
        smeta.append(np.ascontiguousarray(mt.transpose(2, 0, 1)).reshape(P, P))
    return (gna.astype(np.float32), gwa.astype(np.float32), np.stack(sna).astype(np.float32),
            np.stack(swa).astype(np.float32), np.stack(smeta).astype(np.float32))


class Op:
    __slots__ = ("eng", "fn", "deps", "sig", "sigval", "dma_key", "dma_val")

    def __init__(self, eng, fn):
        self.eng, self.fn = eng, fn
        self.deps = ()
        self.sig = False
        self.sigval = 0
        self.dma_key = None
        self.dma_val = 0


class Prog:
    ENG = ("pe", "act", "dve", "pool", "sp")

    LIMIT = None
    MARKS = []

    def mark(self, name):
        Prog.MARKS.append((name, self.nops))

    def __init__(self):
        self.nops = 0
        self.q = {e: [] for e in self.ENG}
        self.lastw = {}
        self.readers = {}
        self.dma_cnt = {}
        self.out_dmas = []

    def op(self, eng, fn, r=(), w=(), dma=None):
        o = Op(eng, fn)
        self.nops += 1
        if Prog.LIMIT is not None and self.nops > Prog.LIMIT:
            return o
        deps = set()
        for x in r:
            lw = self.lastw.get(x)
            if lw is not None:
                deps.add(lw)
        for x in w:
            lw = self.lastw.get(x)
            if lw is not None:
                deps.add(lw)
            rd = self.readers.get(x)
            if rd:
                deps.update(rd.values())
        o.deps = tuple(deps)
        for x in r:
            rk = eng if dma is None else ("dma", self.nops)
            self.readers.setdefault(x, {})[rk] = o
        for x in w:
            self.lastw[x] = o
            self.readers[x] = {}
        if dma is not None:
            c = self.dma_cnt.get(dma, 0) + 1
            self.dma_cnt[dma] = c
            o.dma_key, o.dma_val = dma, 16 * c
        self.q[eng].append(o)
        return o

    def emit(self, nc, es):
        for e in self.ENG:
            for o in self.q[e]:
                for d in o.deps:
                    if d.dma_key is None and not (d.eng == "pe" and o.eng == "pe" and o.dma_key is None):
                        d.sig = True
        for e in self.ENG:
            c = 0
            for o in self.q[e]:
                if o.sig:
                    c += 1
                    o.sigval = c
        esem = {e: es.enter_context(nc.semaphore("s_" + e)) for e in ("pe", "act", "dve", "pool")}
        dsem = {k: es.enter_context(nc.semaphore("d_" + k)) for k in self.dma_cnt}
        blk = es.enter_context(nc.Block())
        final = [(o.dma_key, o.dma_val) for o in self.out_dmas if o.dma_key is not None]

        def run(e, engobj):
            waited = {}
            for o in self.q[e]:
                for d in o.deps:
                    if d.dma_key is not None:
                        key, val, sem = ("d", d.dma_key), d.dma_val, dsem[d.dma_key]
                    else:
                        if d.eng == "pe" and e == "pe" and o.dma_key is None:
                            continue
                        key, val, sem = ("e", d.eng), d.sigval, esem[d.eng]
                    if waited.get(key, 0) >= val:
                        continue
                    waited[key] = val
                    engobj.wait_ge(sem, val)
                ins = o.fn(engobj)
                if o.dma_key is not None:
                    ins.then_inc(dsem[o.dma_key], 16)
                elif o.sig:
                    ins.then_inc(esem[e], 1)
            if e == "pool":
                fin = {}
                for k, v in final:
                    fin[k] = max(fin.get(k, 0), v)
                for k, v in fin.items():
                    engobj.wait_ge(dsem[k], v)

        blk.tensor(lambda t: run("pe", t))
        blk.scalar(lambda t: run("act", t))
        blk.vector(lambda t: run("dve", t))
        blk.gpsimd(lambda t: run("pool", t))
        blk.sync(lambda t: run("sp", t))


def build_program(units, nq):
    U = units
    NT = nq + 4
    NR = 5
    nc = bass.Bass("TRN2", target_bir_lowering=False)
    dram = lambda n, s, dt=F32, kind="ExternalInput": nc.dram_tensor(n, s, dt, kind=kind)
    xe = dram("xe", [U, NT * P, D])
    wp = dram("wp", [P, 8, WCOLS])
    wa_d = dram("wa", [P, 4, D])
    wb_d = dram("wb", [P, 4, D])
    wo_d = dram("wo", [P, 8, D])
    gcol_d = dram("gcol", [P, 8])
    fgb_d = dram("fgb", [P, D])
    sink_d = dram("sinkb", [P, 8])
    b15_d = dram("b15", [P, 1])
    ident_d = dram("ident", [P, P], BF16)
    xmeta_d = dram("xmeta", [P, D])
    maskk_d = dram("maskk", [P, 4, P])
    maskv_d = dram("maskv", [P, 8])
    gna_d = dram("gna", [P, 5, 8, P])
    gwa_d = dram("gwa", [P, 3, 8, P])
    sna_d = dram("sna", [U, P, 6, 8, P])
    swa_d = dram("swa", [U, P, 2, 8, P])
    smeta_d = dram("smeta", [U, P, P])
    y_d = dram("y", [U, nq * P, D], F32, kind="ExternalOutput")
    scr = nc.dram_tensor("espscr", [U, 2, P, 33 * P], BF16)

    pg = Prog()
    es = ExitStack()
    sb = lambda n, s, dt: es.enter_context(nc.sbuf_tensor(n, s, dt))
    W = sb("W", [P, 8, WCOLS], BF16)
    Wa = sb("Wa", [P, 4, D], BF16)
    Wb = sb("Wb", [P, 4, D], BF16)
    Wo = sb("Wo", [P, 8, D], BF16)
    EBNA = sb("EBNA", [P, 5, 8 * P], BF16)
    EBWA = sb("EBWA", [P, 3, 8 * P], BF16)
    ESP = sb("ESP", [P, 33 * P], BF16)
    BDK = sb("BDK", [P, 8, P], BF16)
    VMA = sb("VMA", [P, 2, 260], BF16)
    VMB = sb("VMB", [P, 2, 260], BF16)
    FGB = sb("FGB", [P, D], F32)
    IDT = sb("IDT", [P, P], BF16)
    CST = sb("CST", [P, 32], F32)
    ST = sb("ST", [P, 32], F32)
    XA = [sb(f"XA{i}", [P, D], F32) for i in range(2)]
    XB = sb("XB", [P, D], F32)
    XN = sb("XN", [P, D], BF16)
    UT = [sb(f"UT{i}", [P, 8, P], BF16) for i in range(3)]
    KAr = [sb(f"KAr{i}", [P, 4, P], BF16) for i in range(NR)]
    KBr = [sb(f"KBr{i}", [P, 2, P], BF16) for i in range(NR)]
    VAr = [sb(f"VAr{i}", [P, 8, 72], BF16) for i in range(NR)]
    VBr = [sb(f"VBr{i}", [P, 2, 72], BF16) for i in range(NR)]
    QEO = sb("QEO", [P, 8, 2, P], BF16)
    SZ = sb("SZ", [P, D], BF16)
    TG = sb("TG", [P, 2 * D], BF16)
    PT = [sb(f"PT{i}", [P, 4 * P], BF16) for i in range(5)]
    PM = sb("PM", [P, 2 * P], BF16)
    G = sb("G", [P, D], BF16)
    GT = sb("GT", [P, 8, P], BF16)
    M1 = sb("M1", [P, 512], F32)
    MSK = M1[:, :].rearrange("p (c k) -> p c k", k=P)
    banks = [es.enter_context(nc.psum_tensor(f"bank{i}", [P, 512], F32)) for i in range(8)]
    bview = [b.bitcast(BF16) for b in banks]

    gctr = [0]
    sctr = [0]

    def nextG():
        i = gctr[0] % 3
        gctr[0] += 1
        return i

    wide = [False]

    def nextS():
        if wide[0]:
            i = (3 + sctr[0]) % 6
        else:
            i = 3 + sctr[0] % 3
        sctr[0] += 1
        return i

    BO = [6, 7]
    flat = lambda ap3: ap3.rearrange("p a b -> p (a b)")

    def dma(eng, out, in_, key, r=(), w=()):
        return pg.op(eng, lambda e: e.dma_start(out=out, in_=in_), r=r, w=w, dma=key)

    def act(out, in_, func, r, w, **kw):
        return pg.op("act", lambda e: e.activation(out=out, in_=in_, func=func, **kw), r=r, w=w)

    dma("sp", IDT[:, :], ident_d[:, :], "IDT", w=["IDT"])
    dma("sp", CST[:, 0:8], gcol_d[:, :], "C0", w=["GCOL"])
    dma("sp", CST[:, 8:16], sink_d[:, :], "C1", w=["SINK"])
    dma("sp", CST[:, 16:17], b15_d[:, :], "C2", w=["B15"])
    dma("sp", CST[:, 17:25], maskv_d[:, :], "C3", w=["MASKV"])
    dma("sp", FGB[:, :], fgb_d[:, :], "FGB", w=["FGB"])
    dma("sp", MSK, maskk_d[:, :, :], "MSK", w=["M1"])
    act(CST[:, 8:16], CST[:, 8:16], AF.Exp, r=["SINK"], w=["SINK"])
    pg.op("dve", lambda e: e.memset(ST[:, 24:25], 1.0), w=["ONE"])

    pg.mark("consts")
    TGf = TG.bitcast(F32)
    SZf = SZ.bitcast(F32)
    f32flat = lambda t: t.bitcast(F32)[:, :, :].rearrange("p a b -> p (a b)")
    big = [dict(ap=XA[0][:, :], res=["XA0"], key="XA0"), dict(ap=XA[1][:, :], res=["XA1"], key="XA1"),
           dict(ap=XB[:, :], res=["XB"], key="XB"),
           dict(ap=TGf[:, :], res=["TG0", "TG1", "TG2", "TG3"], key="STG_TG")]
    small = big + [dict(ap=SZf[:, :], res=["SZ0", "SZ1"], key="STG_SZ"),
                   dict(ap=f32flat(GT), res=["GT"], key="STG_GT"),
                   dict(ap=f32flat(UT[1]), res=["UT1"], key="STG_UT1"),
                   dict(ap=f32flat(UT[2]), res=["UT2"], key="STG_UT2")]
    bigc = [0]
    smallc = [0]

    def next_big():
        i = bigc[0] % len(big)
        bigc[0] += 1
        return big[i]

    def next_small():
        i = smallc[0] % len(small)
        smallc[0] += 1
        return small[i]

    castctr = [0]
    dq = [0]

    def cast_block(dst, src_d, res, scale_ap=None, scale_res=None):
        n = dst.shape[-1]
        eng = "act" if castctr[0] % 2 == 0 else "dve"
        castctr[0] += 1
        off = 0
        while off < n:
            u_ = next_small()
            m = min(u_["ap"].shape[-1], n - off)
            st = u_["ap"][:, 0:m]
            dpiece, spiece = dst[:, off:off + m], src_d[:, off:off + m]
            dma("sp", st, spiece, u_["key"], w=u_["res"])
            rr = list(u_["res"]) + ([scale_res] if scale_res else [])
            if eng == "act":
                if scale_ap is None:
                    act(dpiece, st, AF.Copy, r=rr, w=[res])
                else:
                    act(dpiece, st, AF.Copy, r=rr, w=[res], scale=scale_ap)
            else:
                if scale_ap is None:
                    pg.op("dve", lambda e, dpiece=dpiece, st=st: e.tensor_copy(out=dpiece, in_=st), r=rr, w=[res])
                else:
                    pg.op("dve", lambda e, dpiece=dpiece, st=st: e.tensor_scalar(
                        out=dpiece, in0=st, scalar1=scale_ap, scalar2=None, op0=ALU.mult), r=rr, w=[res])
            off += m

    for c in range(8):
        for bi, (a, b) in enumerate(WBLOCKS):
            cast_block(W[:, c, a:b], wp[:, c, a:b], f"W{c}_{bi}", scale_ap=CST[:, c:c + 1], scale_res="GCOL")
    for c in range(4):
        cast_block(Wa[:, c, :], wa_d[:, c, :], f"Wa{c}")
        cast_block(Wb[:, c, :], wb_d[:, c, :], f"Wb{c}")
    for c in range(8):
        cast_block(Wo[:, c, :], wo_d[:, c, :], f"Wo{c}")

    pg.mark("wcast")

    def exp_block(dst, src_d, res):
        n = dst.shape[-1]
        u_ = next_big()
        dma("sp", u_["ap"][:, 0:n], src_d, u_["key"], w=u_["res"])
        act(dst, u_["ap"][:, 0:n], AF.Exp, r=u_["res"], w=[res])

    for i in range(5):
        exp_block(EBNA[:, i, :], gna_d[:, i, :, :].rearrange("p h q -> p (h q)"), f"EBNA{i}")
    for i in range(3):
        exp_block(EBWA[:, i, :], gwa_d[:, i, :, :].rearrange("p h q -> p (h q)"), f"EBWA{i}")

    pg.mark("ebgen")
    tb = [(XN, "XN"), (G, "G")]
    tbc = [0]
    for u in range(U):
        pieces = [(0, 0, sna_d[u, :, 0, :, :]), (0, 1, sna_d[u, :, 1, :, :]), (0, 2, sna_d[u, :, 2, :, :]),
                  (0, 3, swa_d[u, :, 0, :, :]),
                  (1, 0, sna_d[u, :, 3, :, :]), (1, 1, sna_d[u, :, 4, :, :]), (1, 2, sna_d[u, :, 5, :, :]),
                  (1, 3, swa_d[u, :, 1, :, :])]
        for (side, pi, src) in pieces:
            tbuf, tres = tb[tbc[0] % 2]
            tbc[0] += 1
            exp_block(tbuf[:, :], src.rearrange("p h q -> p (h q)"), tres)
            dma("sp", scr[u, side, :, pi * 1024:(pi + 1) * 1024], tbuf[:, :], "SCRW" + tres, r=[tres],
                w=[f"SCR{u}_{side}_{pi}"])
        tbuf, tres = tb[tbc[0] % 2]
        tbc[0] += 1
        exp_block(tbuf[:, 0:P], smeta_d[u, :, :], tres)
        dma("sp", scr[u, 0, :, 4096:4096 + P], tbuf[:, 0:P], "SCRW" + tres, r=[tres], w=[f"SCR{u}_0_4"])

    pg.mark("scratch")

    def a_pre(xbuf, xres, ss_col):
        ss, rms, rstd = ST[:, ss_col:ss_col + 1], ST[:, ss_col + 1:ss_col + 2], ST[:, ss_col + 2:ss_col + 3]
        nm = f"ST{ss_col}"
        act(XN[:, :], xbuf[:, :], AF.Square, r=[xres], w=["XN", nm + "a"], accum_out=ss)
        act(rms, ss, AF.Sqrt, r=[nm + "a", "EPS"], w=[nm + "b"], scale=1.0 / D, bias=EPS_AP[0])
        pg.op("dve", lambda e: e.reciprocal(out=rstd, in_=rms), r=[nm + "b"], w=[nm + "c"])
        act(XN[:, :], xbuf[:, :], AF.Copy, r=[xres, nm + "c"], w=["XN"], scale=rstd)

    EPS_AP = [None]

    def a_tr(uslot):
        g = nextG()
        for c in range(8):
            pg.op("pe", lambda e, c=c: e.transpose(out=bview[g][:, c * P:(c + 1) * P], in_=XN[:, c * P:(c + 1) * P],
                                                    identity=IDT[:, :]), r=["XN", "IDT"], w=[f"B{g}"])
        pg.op("dve", lambda e: e.tensor_copy(out=flat(UT[uslot][:, :, :]), in_=bview[g][:, 0:D]),
              r=[f"B{g}"], w=[f"UT{uslot}"])

    def fm_proj(g, off, uslot, col):
        for c in range(8):
            pg.op("pe", lambda e, c=c: e.matmul(banks[g][:, off * P:(off + 1) * P], lhsT=W[:, c, col:col + P],
                                                 rhs=UT[uslot][:, c, :], start=(c == 0), stop=(c == 7)),
                  r=[f"W{c}_{wblk(col)}", f"UT{uslot}"], w=[f"B{g}"])

    def tm_proj(g, boff, n, uslot, col):
        for c in range(8):
            pg.op("pe", lambda e, c=c: e.matmul(banks[g][:, boff:boff + n], lhsT=UT[uslot][:, c, :],
                                                 rhs=W[:, c, col:col + n], start=(c == 0), stop=(c == 7)),
                  r=[f"W{c}_{wblk(col)}", f"UT{uslot}"], w=[f"B{g}"])

    def a_kv(uslot, rs):
        g1 = nextG()
        for f in range(4):
            fm_proj(g1, f, uslot, KA + f * P)
        act(flat(KAr[rs][:, :, :]), banks[g1][:, :], AF.Copy, r=[f"B{g1}"], w=[f"KA{rs}"])
        g2 = nextG()
        for f in range(2):
            fm_proj(g2, f, uslot, KB + f * P)
        tm_proj(g2, 256, 128, uslot, VB)
        act(flat(KBr[rs][:, :, :]), banks[g2][:, 0:256], AF.Copy, r=[f"B{g2}"], w=[f"KB{rs}"])
        act(VBr[rs][:, :, 0:64], banks[g2][:, 256:384].rearrange("p (h e) -> p h e", e=64), AF.Copy,
            r=[f"B{g2}"], w=[f"VB{rs}"])
        g3 = nextG()
        tm_proj(g3, 0, 512, uslot, VA)
        pg.op("dve", lambda e: e.tensor_copy(out=VAr[rs][:, :, 0:64],
                                             in_=banks[g3][:, :].rearrange("p (h e) -> p h e", e=64)),
              r=[f"B{g3}"], w=[f"VA{rs}"])

    def qzg_phases(uslot):
        ph = []

        def q1():
            g = nextG()
            for f in range(4):
                fm_proj(g, f, uslot, QA + f * P)
            act(QEO[0:64, 0:4, 0, :], banks[g][0:64, :].rearrange("p (c q) -> p c q", q=P), AF.Copy,
                r=[f"B{g}"], w=["QA"])
            act(QEO[64:128, 0:4, 1, :], banks[g][64:128, :].rearrange("p (c q) -> p c q", q=P), AF.Copy,
                r=[f"B{g}"], w=["QA"])

        def q2():
            g = nextG()
            for f in range(4):
                fm_proj(g, f, uslot, QB + f * P)
            pg.op("dve", lambda e, g=g: e.tensor_copy(
                out=QEO[0:64, 4:8, 0, :], in_=banks[g][0:64, :].rearrange("p (c q) -> p c q", q=P)),
                r=[f"B{g}"], w=["QB"])
            pg.op("dve", lambda e, g=g: e.tensor_copy(
                out=QEO[64:128, 4:8, 1, :], in_=banks[g][64:128, :].rearrange("p (c q) -> p c q", q=P)),
                r=[f"B{g}"], w=["QB"])

        def zk(k, col):
            g = nextG()
            tm_proj(g, 0, 512, uslot, col)
            act(SZ[:, k * 512:(k + 1) * 512], banks[g][:, :], AF.Tanh, r=[f"B{g}"], w=[f"SZ{k}"], scale=0.5)
            pg.op("dve", lambda e, g=g, k=k: e.scalar_tensor_tensor(
                out=SZ[:, k * 512:(k + 1) * 512], in0=SZ[:, k * 512:(k + 1) * 512], scalar=1.0, in1=banks[g][:, :],
                op0=ALU.add, op1=ALU.mult), r=[f"B{g}", f"SZ{k}"], w=[f"SZ{k}"])

        def gk(k):
            g = nextG()
            tm_proj(g, 0, 512, uslot, GA + k * 512)
            act(TG[:, k * 512:(k + 1) * 512], banks[g][:, :], AF.Tanh, r=[f"B{g}"], w=[f"TG{k}"], scale=0.5)

        ph.append(q1)
        ph.append(q2)
        ph.append(lambda: zk(0, ZA))
        ph.append(lambda: zk(1, ZB))
        for k in range(4):
            ph.append(lambda k=k: gk(k))
        return ph

    ptc = [0]
    LOOK = 4

    def attention(rs_of, j, nqv):
        def meta():
            s0 = nextS()
            for br in range(2):
                for c in range(4):
                    for qi in range(2):
                        pg.op("pe", lambda e, br=br, c=c, qi=qi: e.matmul(
                            banks[s0][:, br * P:(br + 1) * P], lhsT=BDK[:, br * 4 + c, :],
                            rhs=QEO[:, br * 4 + c, qi, :],
                            start=(c == 0 and qi == 0), stop=(c == 3 and qi == 1)),
                            r=["BDK", "QA" if br == 0 else "QB"], w=[f"B{s0}"])
            act(PM[:, 0:P], banks[s0][:, 0:P], AF.Exp, r=[f"B{s0}"], w=["PMA"], scale=0.125)
            if j == 0:
                act(PM[:, P:2 * P], banks[s0][:, P:2 * P], AF.Exp, r=[f"B{s0}"], w=["PMB"], scale=0.125)
                pg.op("dve", lambda e: e.tensor_tensor(out=PM[:, P:2 * P], in0=PM[:, P:2 * P],
                                                       in1=ESP[:, 32 * P:33 * P], op=ALU.mult),
                      r=["PMB", "ESP"], w=["PMB"])
            else:
                act(PM[:, P:2 * P], banks[s0][:, P:2 * P], AF.Exp, r=[f"B{s0}", "B15"], w=["PMB"], scale=0.125,
                    bias=CST[:, 16:17])

        jobs = []
        for br in range(2):
            dts = (-2, -1, 0, 1, 2) if br == 0 else (-1, 0, 1)
            for hg in range(2):
                for di, dt in enumerate(dts):
                    jobs.append(dict(br=br, hg=hg, dt=dt, first=(di == 0), last=(di == len(dts) - 1)))

        def emit_qk(jb):
            br, hg, dt = jb["br"], jb["hg"], jb["dt"]
            rs = rs_of(dt)
            sbk = nextS()
            jb["rs"], jb["sbk"] = rs, sbk
            if br == 0:
                for pp in range(2):
                    c = 2 * hg + pp
                    pg.op("pe", lambda e, sbk=sbk, pp=pp, c=c, rs=rs: e.matmul(
                        banks[sbk][:, pp * 256:(pp + 1) * 256], lhsT=KAr[rs][:, c, :],
                        rhs=QEO[:, c, :, :].rearrange("p a q -> p (a q)"), start=True, stop=True),
                        r=[f"KA{rs}", "QA"], w=[f"B{sbk}"])
            else:
                pg.op("pe", lambda e, sbk=sbk, hg=hg, rs=rs: e.matmul(
                    banks[sbk][:, :], lhsT=KBr[rs][:, hg, :],
                    rhs=QEO[:, 4 + 2 * hg:6 + 2 * hg, :, :].rearrange("p c a q -> p (c a q)"), start=True, stop=True),
                    r=[f"KB{rs}", "QB"], w=[f"B{sbk}"])

        def emit_sm(jb):
            br, hg, dt, sbk = jb["br"], jb["hg"], jb["dt"], jb["sbk"]
            pb = ptc[0] % 5
            ptc[0] += 1
            jb["pb"] = pb
            act(PT[pb][:, :], banks[sbk][:, :], AF.Exp, r=[f"B{sbk}"], w=[f"PT{pb}"], scale=0.125)
            if br == 0:
                base = None
                if j == 0 and dt == -2:
                    base = 0
                elif j == 0 and dt == -1:
                    base = 8
                elif j == 1 and dt == -2:
                    base = 16
                elif j == nqv - 2 and dt == 2:
                    base = 0
                elif j == nqv - 1 and dt == 1:
                    base = 8
                elif j == nqv - 1 and dt == 2:
                    base = 16
                if base is None:
                    eb, ebres = EBNA[:, dt + 2, hg * 512:(hg + 1) * 512], f"EBNA{dt + 2}"
                else:
                    eb, ebres = ESP[:, (base + 4 * hg) * P:(base + 4 * hg + 4) * P], "ESP"
            else:
                if (j == 0 and dt == -1) or (j == nqv - 1 and dt == 1):
                    eb, ebres = ESP[:, (24 + 4 * hg) * P:(24 + 4 * hg + 4) * P], "ESP"
                else:
                    eb, ebres = EBWA[:, dt + 1, hg * 512:(hg + 1) * 512], f"EBWA{dt + 1}"
            pg.op("dve", lambda e, pb=pb, eb=eb: e.tensor_tensor(out=PT[pb][:, :], in0=PT[pb][:, :], in1=eb,
                                                                  op=ALU.mult),
                  r=[f"PT{pb}", ebres], w=[f"PT{pb}"])

        def emit_pv(jb):
            br, hg, rs, pb = jb["br"], jb["hg"], jb["rs"], jb["pb"]
            ob = BO[hg]
            if jb["first"]:
                VM = VMA if br == 0 else VMB
                pg.op("pe", lambda e, ob=ob, VM=VM, hg=hg, br=br: e.matmul(
                    banks[ob][:, 0:260], lhsT=PM[:, br * P:(br + 1) * P], rhs=VM[:, hg, :], start=True, stop=False,
                    skip_group_check=True),
                    r=["PMA" if br == 0 else "PMB", "VM"], w=[f"B{ob}"])
            last = jb["last"]
            for hl in range(4):
                h = 4 * hg + hl
                rhs = VAr[rs][:, h, 0:65] if br == 0 else VBr[rs][:, h // 4, 0:65]
                pg.op("pe", lambda e, ob=ob, hl=hl, pb=pb, rhs=rhs, last=last: e.matmul(
                    banks[ob][:, hl * 65:(hl + 1) * 65], lhsT=PT[pb][:, hl * P:(hl + 1) * P], rhs=rhs,
                    start=False, stop=last, skip_group_check=True),
                    r=[f"PT{pb}", (f"VA{rs}" if br == 0 else f"VB{rs}")], w=[f"B{ob}"])
            if not last:
                return
            ov = banks[ob][:, 0:260].rearrange("p (h e) -> p h e", e=65)
            dcol = 8 + br * 8 + hg * 4
            den = ST[:, dcol:dcol + 4]
            dres = f"DEN{br}{hg}"
            if br == 0:
                pg.op("dve", lambda e, ov=ov, den=den: e.reciprocal(out=den, in_=ov[:, :, 64]),
                      r=[f"B{ob}"], w=[dres])
            else:
                pg.op("dve", lambda e, ov=ov, den=den, hg=hg: e.tensor_tensor(
                    out=den, in0=ov[:, :, 64], in1=CST[:, 8 + 4 * hg:12 + 4 * hg], op=ALU.add),
                    r=[f"B{ob}", "SINK"], w=[dres])
                pg.op("dve", lambda e, den=den: e.reciprocal(out=den, in_=den), r=[dres], w=[dres])
            o0 = br * 512 + hg * 256
            szv = SZ[:, o0:o0 + 256].rearrange("p (h e) -> p h e", e=64)
            pg.op("pool", lambda e, szv=szv, den=den: e.tensor_tensor(
                out=szv, in0=szv, in1=den.unsqueeze(2).broadcast_to([P, 4, 64]), op=ALU.mult),
                r=[dres, f"SZ{br}"], w=[f"SZ{br}"])
            gv = G[:, o0:o0 + 256].rearrange("p (h e) -> p h e", e=64)
            pg.op("dve", lambda e, gv=gv, ov=ov, szv=szv: e.tensor_tensor(out=gv, in0=ov[:, :, 0:64], in1=szv,
                                                                          op=ALU.mult),
                  r=[f"B{ob}", f"SZ{br}"], w=["G"])

        n = len(jobs)

        def pre():
            meta()
            for i in range(LOOK):
                emit_qk(jobs[i])
                emit_sm(jobs[i])

        def main():
            wide[0] = True
            for i in range(n - LOOK):
                emit_qk(jobs[i + LOOK])
                emit_sm(jobs[i + LOOK])
                emit_pv(jobs[i])
            wide[0] = False

        def drain():
            for i in range(n - LOOK, n):
                emit_pv(jobs[i])

        return pre, main, drain

    def tail_phases(u, j):
        ph = []

        def p1():
            g = nextG()
            for c in range(8):
                pg.op("pe", lambda e, c=c, g=g: e.transpose(out=bview[g][:, c * P:(c + 1) * P],
                                                             in_=G[:, c * P:(c + 1) * P], identity=IDT[:, :]),
                      r=["G", "IDT"], w=[f"B{g}"])
            act(flat(GT[:, :, :]), bview[g][:, 0:D], AF.Copy, r=[f"B{g}"], w=["GT"])

        def ph_y(half):
            ga = nextG()
            for c in range(4):
                pg.op("pe", lambda e, c=c, ga=ga, half=half: e.matmul(
                    banks[ga][:, :], lhsT=GT[:, c, :], rhs=Wa[:, c, half * 512:(half + 1) * 512],
                    start=(c == 0), stop=(c == 3)), r=["GT", f"Wa{c}"], w=[f"B{ga}"])
            pg.op("dve", lambda e, ga=ga, half=half: e.scalar_tensor_tensor(
                out=M1[:, :], in0=TG[:, half * 512:(half + 1) * 512], scalar=1.0, in1=banks[ga][:, :],
                op0=ALU.add, op1=ALU.mult), r=[f"B{ga}", f"TG{half}"], w=["M1"])
            gb = nextG()
            for c in range(4):
                pg.op("pe", lambda e, c=c, gb=gb, half=half: e.matmul(
                    banks[gb][:, :], lhsT=GT[:, 4 + c, :], rhs=Wb[:, c, half * 512:(half + 1) * 512],
                    start=(c == 0), stop=(c == 3)), r=["GT", f"Wb{c}"], w=[f"B{gb}"])
            gh = G[:, half * 512:(half + 1) * 512]
            pg.op("dve", lambda e, gb=gb, half=half, gh=gh: e.scalar_tensor_tensor(
                out=gh, in0=TG[:, D + half * 512:D + (half + 1) * 512], scalar=1.0, in1=banks[gb][:, :],
                op0=ALU.add, op1=ALU.mult), r=[f"B{gb}", f"TG{2 + half}"], w=["G"])
            pg.op("dve", lambda e, gh=gh: e.tensor_tensor(out=gh, in0=gh, in1=M1[:, :], op=ALU.add),
                  r=["M1", "G"], w=["G"])

        def p4():
            g = nextG()
            for c in range(8):
                pg.op("pe", lambda e, c=c, g=g: e.transpose(out=bview[g][:, c * P:(c + 1) * P],
                                                             in_=G[:, c * P:(c + 1) * P], identity=IDT[:, :]),
                      r=["G", "IDT"], w=[f"B{g}"])
            act(flat(GT[:, :, :]), bview[g][:, 0:D], AF.Copy, r=[f"B{g}"], w=["GT"])

        def ph_o(half):
            go = nextG()
            for c in range(8):
                pg.op("pe", lambda e, c=c, go=go, half=half: e.matmul(
                    banks[go][:, :], lhsT=GT[:, c, :], rhs=Wo[:, c, half * 512:(half + 1) * 512],
                    start=(c == 0), stop=(c == 7)), r=["GT", f"Wo{c}"], w=[f"B{go}"])
            pg.op("dve", lambda e, go=go, half=half: e.scalar_tensor_tensor(
                out=XB[:, half * 512:(half + 1) * 512], in0=banks[go][:, :], scalar=0.25,
                in1=XB[:, half * 512:(half + 1) * 512], op0=ALU.mult, op1=ALU.add), r=[f"B{go}", "XB"], w=["XB"])

        ph.append(p1)
        ph.append(lambda: ph_y(0))
        ph.append(lambda: ph_y(1))
        ph.append(p4)
        ph.append(lambda: ph_o(0))
        ph.append(lambda: ph_o(1))
        return ph

    def tail_finish(u, j):
        ss, rms, rstd = ST[:, 4:5], ST[:, 5:6], ST[:, 6:7]
        act(flat(GT[:, :, :]), XB[:, :], AF.Square, r=["XB"], w=["GT", "ST4a"], accum_out=ss)
        act(rms, ss, AF.Sqrt, r=["ST4a", "EPS"], w=["ST4b"], scale=1.0 / D, bias=EPS_AP[0])
        pg.op("dve", lambda e: e.reciprocal(out=rstd, in_=rms), r=["ST4b"], w=["ST4c"])
        pg.op("dve", lambda e: e.scalar_tensor_tensor(out=XB[:, :], in0=XB[:, :], scalar=rstd, in1=FGB[:, :],
                                                      op0=ALU.mult, op1=ALU.mult), r=["XB", "ST4c", "FGB"], w=["XB"])
        o = dma("pool", y_d[u, j * P:(j + 1) * P, :], XB[:, :], "OUT", r=["XB"], w=["YOUT"])
        pg.out_dmas.append(o)

    pg.op("dve", lambda e: e.memset(ST[:, 25:26], EPS), w=["EPS"])
    EPS_AP[0] = ST[:, 25:26]
    _orig_act = act

    pg.op("pool", lambda e: e.memset(QEO[:, :, :, :].rearrange("p a b c -> p (a b c)"), 0.0), w=["QA", "QB"])
    for i in range(NR):
        pg.op("pool", lambda e, i=i: e.memset(VAr[i][:, :, 64:65], 1.0), w=[f"VA{i}"])
        pg.op("pool", lambda e, i=i: e.memset(VBr[i][:, :, 64:65], 1.0), w=[f"VB{i}"])

    pg.mark("onescols")
    dma("sp", XA[0][:, :], xmeta_d[:, :], "XA0", w=["XA0"])
    a_pre(XA[0], "XA0", 0)
    pg.mark("meta_pre")
    a_tr(0)
    pg.mark("meta_tr")
    gm = nextG()
    for f in range(4):
        fm_proj(gm, f, 0, KA + f * P)
    for c in range(4):
        pg.op("dve", lambda e, c=c: e.tensor_tensor(out=BDK[:, c, :], in0=banks[gm][:, c * P:(c + 1) * P],
                                                    in1=MSK[:, c, :], op=ALU.mult), r=[f"B{gm}", "M1"], w=["BDK"])
    gm2 = nextG()
    for f in range(2):
        fm_proj(gm2, f, 0, KB + f * P)
    tm_proj(gm2, 256, 128, 0, VB)
    for c in range(4):
        pg.op("dve", lambda e, c=c: e.tensor_tensor(out=BDK[:, 4 + c, :], in0=banks[gm2][:, (c // 2) * P:(c // 2 + 1) * P],
                                                    in1=MSK[:, c, :], op=ALU.mult), r=[f"B{gm2}", "M1"], w=["BDK"])
    gm3 = nextG()
    tm_proj(gm3, 0, 512, 0, VA)
    for h in range(8):
        hg, hl = h // 4, h % 4
        pg.op("dve", lambda e, h=h, hg=hg, hl=hl: e.tensor_scalar(
            out=VMA[:, hg, hl * 65:hl * 65 + 64], in0=banks[gm3][:, h * 64:(h + 1) * 64], scalar1=CST[:, 17 + h:18 + h],
            scalar2=None, op0=ALU.mult), r=[f"B{gm3}", "MASKV"], w=["VM"])
        pg.op("dve", lambda e, h=h, hg=hg, hl=hl: e.tensor_scalar(
            out=VMB[:, hg, hl * 65:hl * 65 + 64], in0=banks[gm2][:, 256 + hg * 64:256 + (hg + 1) * 64],
            scalar1=CST[:, 17 + h:18 + h], scalar2=None, op0=ALU.mult), r=[f"B{gm2}", "MASKV"], w=["VM"])
        pg.op("dve", lambda e, h=h, hg=hg, hl=hl: e.tensor_copy(out=VMA[:, hg, hl * 65 + 64:hl * 65 + 65],
                                                                in_=CST[:, 17 + h:18 + h]), r=["MASKV"], w=["VM"])
        pg.op("dve", lambda e, h=h, hg=hg, hl=hl: e.tensor_copy(out=VMB[:, hg, hl * 65 + 64:hl * 65 + 65],
                                                                in_=CST[:, 17 + h:18 + h]), r=["MASKV"], w=["VM"])

    pg.mark("meta_done")
    gt = [0]
    for u in range(U):
        dma("sp", ESP[:, :], scr[u, 0, :, :], "ESP", r=[f"SCR{u}_0_{i}" for i in range(5)], w=["ESP"])
        base = gt[0]
        xa_of = lambda s: (base + s) % 2
        ring_of = lambda t: (base + t) % NR
        ut_of = lambda t: (base + t) % 3
        dma("sp", XA[xa_of(0)][:, :], xe[u, 0:P, :], f"XA{xa_of(0)}", w=[f"XA{xa_of(0)}"])
        pending_finish = None
        pending_drain = None
        for s in range(NT + 3):
            tq = s - 2
            tt = s - 3
            doB = 2 <= tq <= NT - 3
            doT = 2 <= tt <= NT - 3
            if pending_finish is not None:
                tail_finish(u, pending_finish)
                pending_finish = None
            if s + 1 < NT:
                k = xa_of(s + 1)
                dma("sp", XA[k][:, :], xe[u, (s + 1) * P:(s + 2) * P, :], f"XA{k}", w=[f"XA{k}"])
            if doT:
                dma("sp", XB[:, :], xe[u, tt * P:(tt + 1) * P, :], "XB", w=["XB"])
            if s < NT:
                a_pre(XA[xa_of(s)], f"XA{xa_of(s)}", 0)
            tp = tail_phases(u, tt - 2) if doT else []
            qp = qzg_phases(ut_of(tq)) if doB else []
            def run(lst, i):
                if i < len(lst):
                    lst[i]()
            run(qp, 0)
            run(qp, 1)
            if pending_drain is not None:
                pending_drain()
                pending_drain = None
            run(qp, 2)
            run(qp, 3)
            run(tp, 0)
            if s < NT:
                a_tr(ut_of(s))
            run(tp, 1)
            run(qp, 4)
            run(tp, 2)
            run(qp, 5)
            run(tp, 3)
            run(qp, 6)
            run(tp, 4)
            run(qp, 7)
            run(tp, 5)
            if doB:
                j = tq - 2
                pre, main, drain = attention(lambda dt, tq=tq: ring_of(tq + dt), j, nq)
                pre()
            if s < NT:
                a_kv(ut_of(s), ring_of(s))
            if doB:
                main()
                pending_drain = drain
                if j == 1:
                    dma("sp", ESP[:, 0:32 * P], scr[u, 1, :, 0:32 * P], "ESP", r=[f"SCR{u}_1_{i}" for i in range(4)],
                        w=["ESP"])
            if doT:
                pending_finish = tt - 2
        assert pending_drain is None
        if pending_finish is not None:
            tail_finish(u, pending_finish)
        gt[0] += NT
    pg.emit(nc, es)
    es.close()
    return nc


def make_core_inputs(x_chunks, shared):
    return None


def prep_shared(meta_tokens, norm_g, w_in, w_proj_a, w_proj_b, w_out, sink_logit, t5_bias, final_g):
    perm = col_perm()
    w = w_in[0][:, perm]
    wp = np.ascontiguousarray(w.reshape(8, P, WCOLS).transpose(1, 0, 2))
    wa = np.ascontiguousarray(w_proj_a[0].reshape(4, P, D).transpose(1, 0, 2))
    wb = np.ascontiguousarray(w_proj_b[0].reshape(4, P, D).transpose(1, 0, 2))
    wo = np.ascontiguousarray(w_out[0].reshape(8, P, D).transpose(1, 0, 2))
    gcol = np.ascontiguousarray(norm_g[0].reshape(8, P).T)
    fgb = np.ascontiguousarray(np.broadcast_to(final_g[None, :], (P, D)))
    sinkb = np.ascontiguousarray(np.broadcast_to(sink_logit[0][None, :], (P, 8)))
    b15 = np.ascontiguousarray(np.repeat(t5_bias[15], N_META)[:, None])
    ident = np.eye(P, dtype=np.float32).astype(ml_dtypes.bfloat16)
    xmeta = np.ascontiguousarray(np.tile(meta_tokens, (8, 1)))
    maskk = np.zeros((P, 4, P), np.float32)
    for c in range(4):
        for hh in range(2):
            maskk[hh * 64:(hh + 1) * 64, c, (2 * c + hh) * 16:(2 * c + hh + 1) * 16] = 1.0
    maskv = np.zeros((P, 8), np.float32)
    for h in range(8):
        maskv[h * 16:(h + 1) * 16, h] = 1.0
    return dict(wp=wp.astype(np.float32), wa=wa, wb=wb, wo=wo, gcol=gcol, fgb=fgb, sinkb=sinkb, b15=b15,
                ident=ident, xmeta=xmeta, maskk=maskk, maskv=maskv)


def ext_tokens(xseq, start, nq):
    n = xseq.shape[0]
    _, actt = unit_slots(start, nq, n)
    return np.ascontiguousarray(xseq[actt])


def kernel(x_prompt, x_sample, meta_tokens, norm_g, w_in, na_rpb, sink_logit, w_proj_a, w_proj_b, w_out,
           t5_bias, final_g):
    f = lambda a: np.asarray(a, dtype=np.float32)
    x_prompt, x_sample = f(x_prompt), f(x_sample)
    NQ = 32
    UNITS = 3
    shared = prep_shared(f(meta_tokens), f(norm_g), f(w_in), f(w_proj_a), f(w_proj_b), f(w_out), f(sink_logit),
                         f(t5_bias), f(final_g))
    nc = build_program(UNITS, NQ)
    in_maps = []
    for c in range(NCORES):
        ulist = [(x_prompt[c // 4], (c % 4) * 4096), (x_sample[2 * c], 0), (x_sample[2 * c + 1], 0)]
        xe = np.stack([ext_tokens(xs, st, NQ) for xs, st in ulist])
        gna, gwa, sna, swa, smeta = build_bias_inputs(f(na_rpb)[0], f(t5_bias), [(st, xs.shape[0]) for xs, st in ulist], NQ)
        m = dict(shared)
        m.update(xe=xe, gna=gna, gwa=gwa, sna=sna, swa=swa, smeta=smeta)
        in_maps.append(m)
    res = run_bass_kernel_spmd(nc, in_maps, core_ids=list(range(NCORES)))
    yp = np.empty_like(x_prompt)
    ys = np.empty_like(x_sample)
    for c in range(NCORES):
        y = np.asarray(res.results[c]["y"])
        yp[c // 4, (c % 4) * 4096:(c % 4 + 1) * 4096] = y[0]
        ys[2 * c] = y[1]
        ys[2 * c + 1] = y[2]
    return (yp, ys)
```
